# Optimizing a Trainium2 kernel written in Bass

```python
import math, functools
import jax, jax.numpy as jnp
from jax import lax
import numpy as np

D_MODEL = 2048
BATCH = 32
SEQ = 256
DEPTH = 2
DEC_BATCH = 2
DEC_SEQ = 2048
PAST_LEN = 512

GRID_W = 64
N_HEADS = D_MODEL // 128
HEAD_DIM = 64
ATTN_W = N_HEADS * HEAD_DIM
SSM_W = D_MODEL // 4
SSM_GROUP = 16
SSM_GROUPS = SSM_W // SSM_GROUP
SSM_STATE = 64
POOL_W = D_MODEL // 4
POOL_WINDOWS = (2, 4, 8, 16)
POOL_GROUP = POOL_W // len(POOL_WINDOWS)
IN_W = 3 * ATTN_W + SSM_W + POOL_W
SPLITS = (ATTN_W, 2 * ATTN_W, 3 * ATTN_W, 3 * ATTN_W + SSM_W)
WIN_ROWS_MAX = 8
WIN_COLS = 16
Q_BLOCK = 128
FFN_HIDDEN = ((8 * D_MODEL + 3 * 256 - 1) // (3 * 256)) * 256
RMS_EPS = 1e-6
NEG_INF = -1e30

kernel_name = 'hybrid_s5_natten_pool_diffusion_step'


def rms_norm(x, g):
    xf = x.astype(jnp.float32)
    y = xf * lax.rsqrt(jnp.mean(xf * xf, axis=-1, keepdims=True) + RMS_EPS)
    return (y * g.astype(jnp.float32)).astype(x.dtype)


def modulation(cond, w_mod, b_mod):
    m = jax.nn.silu(cond) @ w_mod + b_mod
    return [t[:, None, :] for t in jnp.split(m, 6, axis=-1)]


def to_heads(t):
    b, n, _ = t.shape
    return t.reshape(b, n, N_HEADS, HEAD_DIM).transpose(0, 2, 1, 3)


def from_heads(t):
    b, h, n, d = t.shape
    return t.transpose(0, 2, 1, 3).reshape(b, n, h * d)


def context_attention(q, k, v):
    b, h, n, d = q.shape
    qb = q.reshape(b, h, n // Q_BLOCK, Q_BLOCK, d).transpose(2, 0, 1, 3, 4)

    def block(qi):
        s = jnp.einsum('bhqd,bhkd->bhqk', qi, k).astype(jnp.float32) * HEAD_DIM ** -0.5
        p = jax.nn.softmax(s, axis=-1).astype(v.dtype)
        return jnp.einsum('bhqk,bhkd->bhqd', p, v)

    o = lax.map(block, qb)
    return o.transpose(1, 2, 0, 3, 4).reshape(b, h, n, d)


def latent_attention(q, k, v, ctx_k, ctx_v, rpb):
    f32 = jnp.float32
    b, h, n, d = q.shape
    rows = n // GRID_W
    wr = min(WIN_ROWS_MAX, rows)
    scale = HEAD_DIM ** -0.5
    qg = q.reshape(b, h, rows, GRID_W, d).transpose(2, 0, 1, 3, 4)
    kg = k.reshape(b, h, rows, GRID_W, d)
    vg = v.reshape(b, h, rows, GRID_W, d)
    cols = jnp.arange(GRID_W)
    col_start = jnp.clip(cols - WIN_COLS // 2, 0, GRID_W - WIN_COLS)
    col_mask = (cols[None, :] >= col_start[:, None]) & (cols[None, :] < col_start[:, None] + WIN_COLS)
    dc_idx = jnp.clip(cols[None, :] - cols[:, None], 1 - WIN_COLS, WIN_COLS - 1) + WIN_COLS - 1

    def row(args):
        r, qr = args
        rs = jnp.clip(r - wr // 2, 0, rows - wr)
        kb = lax.dynamic_slice_in_dim(kg, rs, wr, axis=2)
        vb = lax.dynamic_slice_in_dim(vg, rs, wr, axis=2)
        dr_idx = rs + jnp.arange(wr) - r + WIN_ROWS_MAX - 1
        bias = rpb[:, dr_idx[None, :, None], dc_idx[:, None, :]].astype(f32)
        s_loc = jnp.einsum('bhqd,bhrkd->bhqrk', qr, kb).astype(f32) * scale + bias
        s_loc = jnp.where(col_mask[:, None, :], s_loc, NEG_INF).reshape(b, h, GRID_W, wr * GRID_W)
        s_ctx = jnp.einsum('bhqd,bhld->bhql', qr, ctx_k).astype(f32) * scale
        p = jax.nn.softmax(jnp.concatenate([s_loc, s_ctx], axis=-1), axis=-1).astype(v.dtype)
        p_loc = p[..., :wr * GRID_W].reshape(b, h, GRID_W, wr, GRID_W)
        return (jnp.einsum('bhqrk,bhrkd->bhqd', p_loc, vb)
                + jnp.einsum('bhql,bhld->bhqd', p[..., wr * GRID_W:], ctx_v))

    o = lax.map(row, (jnp.arange(rows), qg))
    return o.transpose(1, 2, 0, 3, 4).reshape(b, h, n, d)


def _scan_combine(e1, e2):
    a1, b1 = e1
    a2, b2 = e2
    return a1 * a2, a2 * b1 + b2


def diag_scan(a_bar, bu, h0, reverse):
    a = jnp.broadcast_to(a_bar, bu.shape)
    a_cum, h = lax.associative_scan(_scan_combine, (a, bu), reverse=reverse, axis=1)
    return h + a_cum * h0[:, None]


def ssm_mixer(u, h0, a_re, a_im, log_dt, b_re, b_im, c_re, c_im, d_skip, w_glu, b_glu):
    f32 = jnp.float32
    bsz, n, _ = u.shape
    lam = lax.complex(a_re.astype(f32), a_im.astype(f32))
    dt = jnp.exp(log_dt.astype(f32))[..., None]
    a_bar = jnp.exp(lam * dt)
    b_bar = ((a_bar - 1.0) / lam)[..., None] * lax.complex(b_re.astype(f32), b_im.astype(f32))
    c_mat = lax.complex(c_re.astype(f32), c_im.astype(f32))
    uf = u.astype(f32)
    ug = uf.reshape(bsz, n, SSM_GROUPS, SSM_GROUP)
    y = d_skip.astype(f32) * uf
    finals = []
    for direction, rev in ((0, False), (1, True)):
        bu = jnp.einsum('btgm,gpm->btgp', ug, b_bar[direction])
        h = diag_scan(a_bar[direction], bu, h0[:, direction], rev)
        y = y + jnp.real(jnp.einsum('btgp,gmp->btgm', h, c_mat[direction])).reshape(bsz, n, SSM_W)
        finals.append(h[:, 0] if rev else h[:, -1])
    z = jax.nn.gelu(y).astype(u.dtype) @ w_glu + b_glu
    z_val, z_gate = jnp.split(z, 2, axis=-1)
    return z_val * jax.nn.sigmoid(z_gate), jnp.stack(finals, axis=1)


def pool_mixer(p, w_pool, pool_scale):
    f32 = jnp.float32
    bsz, n, _ = p.shape
    ng = len(POOL_WINDOWS)
    pf = p.astype(f32).reshape(bsz, n, ng, POOL_GROUP)
    csum = jnp.concatenate([jnp.zeros((bsz, 1, ng, POOL_GROUP), f32), jnp.cumsum(pf, axis=1)], axis=1)
    t = jnp.arange(n)[:, None]
    win = jnp.array(POOL_WINDOWS, dtype=jnp.int32)[None, :]
    lo = jnp.clip(t - win // 2, 0, n)
    hi = jnp.clip(t - win // 2 + win, 0, n)
    g_idx = jnp.arange(ng)[None, :]
    sums = csum[:, hi, g_idx] - csum[:, lo, g_idx]
    mixed = sums / (hi - lo).astype(f32)[None, :, :, None] - pf
    out = jnp.einsum('btgc,gcd->btgd', mixed, w_pool.astype(f32)).reshape(bsz, n, POOL_W)
    return (out * pool_scale.astype(f32)).astype(p.dtype)


def swiglu(h, w_in, w_out):
    g, u = jnp.split(h @ w_in, 2, axis=-1)
    return (jax.nn.silu(g) * u) @ w_out


def trunk_layer(x, cond, lp, attend, h0):
    sh1, sc1, g1, sh2, sc2, g2 = modulation(cond, lp['w_mod'], lp['b_mod'])
    h = rms_norm(x, lp['norm1_g']) * (1 + sc1) + sh1
    q, k, v, u, p = jnp.split(h @ lp['w_in'], SPLITS, axis=-1)
    q, k, v = to_heads(q), to_heads(k), to_heads(v)
    a_out = from_heads(attend(q, k, v))
    s_out, s_final = ssm_mixer(u, h0, lp['ssm_a_re'], lp['ssm_a_im'], lp['ssm_log_dt'],
                               lp['ssm_b_re'], lp['ssm_b_im'], lp['ssm_c_re'], lp['ssm_c_im'],
                               lp['ssm_d'], lp['ssm_w_glu'], lp['ssm_b_glu'])
    p_out = pool_mixer(p, lp['pool_w'], lp['pool_scale'])
    mixed = jnp.concatenate([a_out, s_out, p_out], axis=-1) @ lp['w_out']
    x = x + g1 * mixed
    h2 = rms_norm(x, lp['norm2_g']) * (1 + sc2) + sh2
    x = x + g2 * swiglu(h2, lp['ffn_w_in'], lp['ffn_w_out'])
    return x, k, v, s_final


def setup_inputs(seed: int = 0) -> dict:
    key = jax.random.key(seed)
    ks = jax.random.split(key, 32)
    f32 = jnp.float32
    L, D, G, P, M = DEPTH, D_MODEL, SSM_GROUPS, SSM_STATE, SSM_GROUP

    def nrm(k, shape, scale):
        return jax.random.normal(k, shape, f32) * scale

    n_idx = jnp.arange(P, dtype=f32)
    return {
        'x_prompt': nrm(ks[0], (BATCH, SEQ, D), 1.0),
        'x_sample': nrm(ks[1], (DEC_BATCH, DEC_SEQ, D), 1.0),
        'c': nrm(ks[2], (DEC_BATCH, D), 1.0),
        'cache_k': nrm(ks[3], (DEC_BATCH, L, N_HEADS, PAST_LEN, HEAD_DIM), 1.0),
        'cache_v': nrm(ks[4], (DEC_BATCH, L, N_HEADS, PAST_LEN, HEAD_DIM), 1.0),
        'state_ssm': nrm(ks[5], (DEC_BATCH, L, 2, G, P, 2), 0.5),
        'c_ctx': nrm(ks[6], (D,), 1.0),
        'w_mod': nrm(ks[7], (L, D, 6 * D), 0.5 * D ** -0.5),
        'b_mod': nrm(ks[8], (L, 6 * D), 0.01),
        'norm1_g': 1.0 + nrm(ks[9], (L, D), 0.1),
        'norm2_g': 1.0 + nrm(ks[10], (L, D), 0.1),
        'w_in': nrm(ks[11], (L, D, IN_W), D ** -0.5),
        'attn_rpb': nrm(ks[12], (L, N_HEADS, 2 * WIN_ROWS_MAX - 1, 2 * WIN_COLS - 1), 0.1),
        'ssm_a_re': -0.5 + nrm(ks[13], (L, 2, G, P), 0.01),
        'ssm_a_im': math.pi * n_idx + nrm(ks[14], (L, 2, G, P), 0.01),
        'ssm_log_dt': jax.random.uniform(ks[15], (L, 2, G), f32, math.log(1e-3), math.log(1e-1)),
        'ssm_b_re': nrm(ks[16], (L, 2, G, P, M), (2 * M) ** -0.5),
        'ssm_b_im': nrm(ks[17], (L, 2, G, P, M), (2 * M) ** -0.5),
        'ssm_c_re': nrm(ks[18], (L, 2, G, M, P), (2 * P) ** -0.5),
        'ssm_c_im': nrm(ks[19], (L, 2, G, M, P), (2 * P) ** -0.5),
        'ssm_d': nrm(ks[20], (L, SSM_W), 1.0),
        'ssm_w_glu': nrm(ks[21], (L, SSM_W, 2 * SSM_W), SSM_W ** -0.5),
        'ssm_b_glu': nrm(ks[22], (L, 2 * SSM_W), 0.01),
        'pool_w': nrm(ks[23], (L, len(POOL_WINDOWS), POOL_GROUP, POOL_GROUP), POOL_GROUP ** -0.5),
        'pool_scale': 1.0 + nrm(ks[24], (L, POOL_W), 0.1),
        'w_out': nrm(ks[25], (L, D, D), D ** -0.5),
        'ffn_w_in': nrm(ks[26], (L, D, 2 * FFN_HIDDEN), D ** -0.5),
        'ffn_w_out': nrm(ks[27], (L, FFN_HIDDEN, D), FFN_HIDDEN ** -0.5),
        'final_norm_g': 1.0 + nrm(ks[28], (D,), 0.1),
    }


def reference(x_prompt, x_sample, c, cache_k, cache_v, state_ssm, c_ctx, w_mod, b_mod,
              norm1_g, norm2_g, w_in, attn_rpb, ssm_a_re, ssm_a_im, ssm_log_dt,
              ssm_b_re, ssm_b_im, ssm_c_re, ssm_c_im, ssm_d, ssm_w_glu, ssm_b_glu,
              pool_w, pool_scale, w_out, ffn_w_in, ffn_w_out, final_norm_g):
    f32 = jnp.float32

    def layer_params(l):
        return {
            'w_mod': w_mod[l], 'b_mod': b_mod[l], 'norm1_g': norm1_g[l], 'norm2_g': norm2_g[l],
            'w_in': w_in[l], 'ssm_a_re': ssm_a_re[l], 'ssm_a_im': ssm_a_im[l],
            'ssm_log_dt': ssm_log_dt[l], 'ssm_b_re': ssm_b_re[l], 'ssm_b_im': ssm_b_im[l],
            'ssm_c_re': ssm_c_re[l], 'ssm_c_im': ssm_c_im[l], 'ssm_d': ssm_d[l],
            'ssm_w_glu': ssm_w_glu[l], 'ssm_b_glu': ssm_b_glu[l], 'pool_w': pool_w[l],
            'pool_scale': pool_scale[l], 'w_out': w_out[l], 'ffn_w_in': ffn_w_in[l],
            'ffn_w_out': ffn_w_out[l],
        }

    xp = x_prompt
    cond_ctx = c_ctx[None, :]
    h0_ctx = jnp.zeros((x_prompt.shape[0], 2, SSM_GROUPS, SSM_STATE), jnp.complex64)
    ks_out, vs_out, st_out = [], [], []
    for l in range(DEPTH):
        xp, k_l, v_l, fin = trunk_layer(xp, cond_ctx, layer_params(l), context_attention, h0_ctx)
        ks_out.append(k_l)
        vs_out.append(v_l)
        st_out.append(jnp.stack([jnp.real(fin), jnp.imag(fin)], axis=-1))
    y_prompt = rms_norm(xp, final_norm_g)
    new_cache_k = jnp.stack(ks_out, axis=1)
    new_cache_v = jnp.stack(vs_out, axis=1)
    new_state_ssm = jnp.stack(st_out, axis=1)

    xs = x_sample
    for l in range(DEPTH):
        st = state_ssm[:, l]
        h0 = lax.complex(st[..., 0].astype(f32), st[..., 1].astype(f32))
        attend = functools.partial(latent_attention, ctx_k=cache_k[:, l], ctx_v=cache_v[:, l], rpb=attn_rpb[l])
        xs, _, _, _ = trunk_layer(xs, c, layer_params(l), attend, h0)
    y_sample = rms_norm(xs, final_norm_g)

    return (y_prompt, y_sample, new_cache_k, new_cache_v, new_state_ssm)
```

```python
import math
from contextlib import ExitStack

import numpy as np
import concourse.bass as bass
import concourse.mybir as mybir
from concourse.bass_utils import run_bass_kernel_spmd

F32 = mybir.dt.float32
BF16 = mybir.dt.bfloat16
I32 = mybir.dt.int32
AF = mybir.ActivationFunctionType
ALU = mybir.AluOpType
AX = mybir.AxisListType

D = 2048
NCH = 16
DEPTH = 2
NH = 16
HD = 64
TP = 1024
TS = 2048
T = TP + TS
TT = 1024
NT = T // TT
FF = 5632
FCH = 44
NSPLIT = 4
SCH = FCH // NSPLIT
PAST = 512
GW = 64
EPS = 1e-6
NEG = -1e30
LC = 512
TWO_PI = 2.0 * math.pi

MDT = BF16
DEBUG = False
STOP_AFTER = None
BISECT = None


class KB:
    NDMA = 24

    def __init__(self, nc, stack):
        self.nc = nc
        self.engs = {"pe": nc.tensor, "act": nc.scalar, "dve": nc.vector,
                     "pool": nc.gpsimd, "sp": nc.sync}
        self.sem = {}
        self.cnt = {}
        for n in self.engs:
            self.sem[n] = stack.enter_context(nc.semaphore("s_" + n))
            self.cnt[n] = 0
        self.dsem = [stack.enter_context(nc.semaphore("d%d" % i)) for i in range(self.NDMA)]
        self.dcnt = [0] * self.NDMA
        self.dnext = 0
        self.seen = {n: {} for n in self.engs}
        self.bufs = {}
        self.n_ops = 0

    def _semh(self, sk):
        return self.sem[sk] if isinstance(sk, str) else self.dsem[sk[1]]

    def _wait(self, eng, tok):
        sk, v = tok
        if sk == eng and eng in ("pe", "sp"):
            return
        if self.seen[eng].get(sk, 0) >= v:
            return
        self.engs[eng].wait_ge(self._semh(sk), v)
        self.seen[eng][sk] = v

    def _deps(self, eng, reads, writes):
        toks = {}

        def add(t):
            if t is not None and toks.get(t[0], 0) < t[1]:
                toks[t[0]] = t[1]
        for r in reads:
            st = self.bufs.get(r)
            if st:
                add(st[0])
        for w in writes:
            st = self.bufs.get(w)
            if st:
                add(st[0])
                for t in st[1]:
                    add(t)
        for sk, v in toks.items():
            self._wait(eng, (sk, v))

    def _record(self, tok, reads, writes):
        for r in reads:
            st = self.bufs.setdefault(r, [None, []])
            st[1] = [t for t in st[1] if t[0] != tok[0]] + [tok]
        for w in writes:
            self.bufs[w] = [tok, []]

    def op(self, eng, fn, reads=(), writes=()):
        reads, writes = list(reads), list(writes)
        for r in reads:
            if isinstance(r, tuple) and r[0] == "ps" and r not in writes:
                writes.append(r)
        self._deps(eng, reads, writes)
        ins = fn()
        self.cnt[eng] += 1
        ins.then_inc(self.sem[eng], 1)
        tok = (eng, self.cnt[eng])
        self._record(tok, reads, writes)
        self.n_ops += 1
        return tok

    def dma(self, q, out, in_, reads=(), writes=(), **kw):
        i = self.dnext
        self.dnext = (self.dnext + 1) % self.NDMA
        if self.dcnt[i]:
            self._wait(q, (("d", i), 16 * self.dcnt[i]))
        self._deps(q, reads, writes)
        ins = self.engs[q].dma_start(out=out, in_=in_, **kw)
        self.dcnt[i] += 1
        ins.then_inc(self.dsem[i], 16)
        tok = (("d", i), 16 * self.dcnt[i])
        self._record(tok, reads, writes)
        self.n_ops += 1
        return tok

    def barrier(self):
        for e in self.engs:
            for o in self.engs:
                if self.cnt[o]:
                    self._wait(e, (o, self.cnt[o]))
            for i in range(self.NDMA):
                if self.dcnt[i]:
                    self._wait(e, (("d", i), 16 * self.dcnt[i]))
        self.bufs = {}


class WStream:
    def __init__(self, k, bufs, srcs):
        self.k, self.bufs, self.srcs = k, bufs, srcs
        self.nb = len(bufs)
        self.issued = 0

    def get(self, i):
        while self.issued < len(self.srcs) and self.issued <= i + self.nb - 1:
            j = self.issued
            src = self.srcs[j]
            n = src.shape[1]
            dst = self.bufs[j % self.nb][:, 0:n]
            if n > 1024:
                dst = dst.rearrange("p (a b) -> p a b", a=2)
                src = src.rearrange("p (a b) -> p a b", a=2)
            self.k.dma("pool", dst, src, writes=[("wb", j % self.nb)])
            self.issued += 1
        return self.bufs[i % self.nb], ("wb", i % self.nb)


def build_program():
    nc = bass.Bass("TRN2", target_bir_lowering=False)
    dt_in = lambda name, shape, dt=F32: nc.dram_tensor(name, list(shape), dt, kind="ExternalInput").ap()
    dt_out = lambda name, shape, dt=F32: nc.dram_tensor(name, list(shape), dt, kind="ExternalOutput").ap()
    dt_scr = lambda name, shape, dt=F32: (nc.dram_tensor(name, list(shape), dt, kind="ExternalOutput").ap()
                                          if DEBUG else nc.dram_tensor(name, list(shape), dt).ap())

    x_in = dt_in("x_in", [T, D])
    condT = dt_in("condT", [128, NCH, 2])
    ck_d = dt_in("ck", [DEPTH, NH, PAST, HD])
    cv_d = dt_in("cv", [DEPTH, NH, PAST, HD])
    st0_d = dt_in("st0", [DEPTH, 128, 2, 16, 2])
    wmod_t = dt_in("wmod_t", [DEPTH, 24, 128, NCH * 512])
    bmodT = dt_in("bmodT", [DEPTH, 128, 96])
    gT = dt_in("gT", [128, 5, NCH])
    win_t = dt_in("win_t", [DEPTH, 32, 128, NCH * 128])
    wout_t = dt_in("wout_t", [DEPTH, 16, 128, NCH * 128])
    ffin_t = dt_in("ffin_t", [DEPTH, 88, 128, NCH * 128])
    ffout_t = dt_in("ffout_t", [DEPTH, NSPLIT, 16, 128, SCH * 128])
    rpbtab = dt_in("rpbtab", [DEPTH, NH, 64, 15 * 64])
    colmask = dt_in("colmask", [64, 64])
    ssm_a = dt_in("ssm_a", [DEPTH, 128, 3, 32])
    ssm_B = dt_in("ssm_B", [DEPTH, 128, 2, 32, 16])
    ssm_C = dt_in("ssm_C", [DEPTH, 2, 128, 32 * 128])
    vecs = dt_in("vecs", [DEPTH, 128, 16])
    wglu_t = dt_in("wglu_t", [DEPTH, 128, 4 * 1024])
    poolw_t = dt_in("poolw_t", [DEPTH, 128, 4 * 128])
    invcnt = dt_in("invcnt", [128, 4, 256 + TS])
    ident_d = dt_in("ident", [128, 128])

    y_out = dt_out("y", [T, D])
    nk_out = dt_out("nk", [4, DEPTH, NH, 256, HD])
    nv_out = dt_out("nv", [4, DEPTH, NH, 256, HD])
    nst_out = dt_out("nst", [4, DEPTH, 2, 32, 64, 2])

    XT = dt_scr("XT", [D, T])
    QT = dt_scr("QT", [1024, T], MDT)
    KT = dt_scr("KT", [1024, T], MDT)
    VTOK = dt_scr("VTOK", [T, 1024], MDT)
    UT = dt_scr("UT", [512, T])
    PT = dt_scr("PT", [512, T])
    MIX = dt_scr("MIX", [D, T], MDT)

    with ExitStack() as top:
        k = KB(nc, top)
        _uid = [0]

        def sb(name, shape, dt=F32, st=top):
            _uid[0] += 1
            return st.enter_context(nc.sbuf_tensor("sb%d_%s" % (_uid[0], name), list(shape), dt))
        PSB = [top.enter_context(nc.psum_tensor("psb%d" % i, [128, 1024], F32)) for i in range(4)]
        PS = [PSB[i // 2][:, (i % 2) * 512:(i % 2 + 1) * 512] for i in range(8)]
        psk = lambda i: ("ps", i)

        ident = sb("ident", [128, 128])
        ones = sb("ones", [128, 128])
        modv = sb("modv", [128, DEPTH, 6, NCH, 2])
        gsb = sb("gsb", [128, 5, NCH])
        k.dma("sp", ident[:], ident_d, writes=["ident"])
        k.dma("sp", gsb[:], gT, writes=["gsb"])
        k.op("dve", lambda: nc.vector.memset(ones[:], 1.0), writes=["ones"])

        def mm(out, pairs, reads, writes):
            def f():
                n = len(pairs)
                ins = None
                for i, (l, r) in enumerate(pairs):
                    ins = nc.tensor.matmul(out, l, r, start=(i == 0), stop=(i == n - 1))
                return ins
            return k.op("pe", f, reads, writes)

        def act(out, in_, func, reads, writes, **kw):
            return k.op("act", lambda: nc.scalar.activation(out, in_, func, **kw), reads, writes)

        def tt(eng, out, a, b, op, reads, writes):
            e = nc.vector if eng == "dve" else nc.gpsimd
            return k.op(eng, lambda: e.tensor_tensor(out, a, b, op), reads, writes)

        def ts(eng, out, a, s1, s2, op0, op1, reads, writes):
            e = nc.vector if eng == "dve" else nc.gpsimd
            if s2 is None:
                return k.op(eng, lambda: e.tensor_scalar(out, a, s1, None, op0), reads, writes)
            return k.op(eng, lambda: e.tensor_scalar(out, a, s1, s2, op0, op1), reads, writes)

        def stt(out, a, s, b, op0, op1, reads, writes):
            return k.op("dve", lambda: nc.vector.scalar_tensor_tensor(out, a, s, b, op0, op1), reads, writes)

        def cp(eng, out, in_, reads, writes):
            if eng == "act":
                return k.op("act", lambda: nc.scalar.copy(out, in_), reads, writes)
            e = nc.vector if eng == "dve" else nc.gpsimd
            return k.op(eng, lambda: e.tensor_copy(out, in_), reads, writes)

        def stop(name):
            return STOP_AFTER == name

        class Ctx:
            pass
        c = Ctx()
        c.nc, c.k, c.PS, c.PSB, c.ident, c.ones, c.sb = nc, k, PS, PSB, ident, ones, sb
        c.mm, c.act, c.tt, c.ts, c.stt, c.cp, c.psk = mm, act, tt, ts, stt, cp, psk

        with ExitStack() as ph:
            xt4 = [sb("p0_x%d" % i, [128, 4, D], F32, ph) for i in range(2)]
            stg = [sb("p0_s%d" % i, [128, 512], F32, ph) for i in range(4)]
            si = 0
            for g in range(T // 512):
                xb = xt4[g % 2]
                kx = ("p0x", g % 2)
                k.dma("sp", xb[:], x_in[g * 512:(g + 1) * 512, :].rearrange("(b p) f -> p b f", p=128),
                      writes=[kx])
                for fc in range(NCH):
                    bank = fc % 4
                    def f(bank=bank, fc=fc, xb=xb):
                        ins = None
                        for b in range(4):
                            ins = nc.tensor.transpose(PS[bank][:, b * 128:(b + 1) * 128],
                                                      xb[:, b, fc * 128:(fc + 1) * 128], ident[:])
                        return ins
                    k.op("pe", f, reads=[kx, "ident"], writes=[psk(bank)])
                    s = stg[si % 4]
                    ks = ("p0s", si % 4)
                    si += 1
                    cp("act" if fc % 2 else "dve", s[:], PS[bank][:], [psk(bank)], [ks])
                    k.dma("sp", XT[fc * 128:(fc + 1) * 128, g * 512:(g + 1) * 512], s[:],
                          reads=[ks], writes=[("XT", g // 2)])
        k.barrier()

        with ExitStack() as ph:
            cs = sb("m_cs", [128, NCH, 2], F32, ph)
            wb = [sb("m_w%d" % i, [128, NCH * 512], F32, ph) for i in range(2)]
            row = [sb("m_r%d" % i, [2, 512], F32, ph) for i in range(2)]
            bm = sb("m_b", [128, DEPTH, 96], F32, ph)
            modT = sb("m_modT", [128, DEPTH, 96, 2], F32, ph)
            k.dma("sp", cs[:], condT, writes=["cs"])
            k.dma("sp", bm[:], bmodT.rearrange("l p c -> p l c"), writes=["bm"])
            act(cs[:], cs[:], AF.Silu, ["cs"], ["cs"])
            it = 0
            for l in range(DEPTH):
                for n in range(24):
                    w = wb[it % 2]
                    kw = ("mw", it % 2)
                    k.dma("sp" if it % 2 else "act", w[:], wmod_t[l, n], writes=[kw])
                    wv = w[:].rearrange("p (k c) -> p k c", k=NCH)
                    bank = it % 2
                    mm(PS[bank][0:2, :], [(cs[:, kc, :], wv[:, kc, :]) for kc in range(NCH)],
                       [kw, "cs"], [psk(bank)])
                    r = row[it % 2]
                    kr = ("mr", it % 2)
                    cp("act", r[:], PS[bank][0:2, :], [psk(bank)], [kr])
                    tb = 2 + (it % 2)
                    def f(r=r, tb=tb):
                        ins = None
                        for j in range(4):
                            ins = nc.tensor.transpose(PS[tb][:, j * 2:(j + 1) * 2],
                                                      r[0:2, j * 128:(j + 1) * 128], ident[0:2, 0:2])
                        return ins
                    k.op("pe", f, reads=[kr, "ident"], writes=[psk(tb)])
                    pv = PS[tb][:, 0:8].rearrange("p (j c) -> p j c", c=2)
                    for cc in range(2):
                        tt("dve", modT[:, l, n * 4:(n + 1) * 4, cc], pv[:, :, cc],
                           bm[:, l, n * 4:(n + 1) * 4], ALU.add, [psk(tb), "bm"], ["modT"])
                    it += 1
            for l in range(DEPTH):
                for half in range(2):
                    o = half * 48
                    gi = l if half == 0 else 2 + l
                    for cc in range(2):
                        stt(modv[:, l, 3 * half + 0, :, cc], modT[:, l, o + 16:o + 32, cc], 1.0,
                            gsb[:, gi, :], ALU.add, ALU.mult, ["modT", "gsb"], ["modv"])
                        cp("dve", modv[:, l, 3 * half + 1, :, cc], modT[:, l, o:o + 16, cc], ["modT"], ["modv"])
                        cp("dve", modv[:, l, 3 * half + 2, :, cc], modT[:, l, o + 32:o + 48, cc], ["modT"], ["modv"])
        if DEBUG:
            dbg_modv = dt_out("dbg_modv", [128, DEPTH * 6 * NCH * 2])
            k.dma("sp", dbg_modv, modv[:].rearrange("p l s f c -> p (l s f c)"), reads=["modv"], writes=["dbg"])
        k.barrier()
        if stop("M"):
            return finish(nc, k)

        def rms_rstd(xT, kx, rstd, sq, ssb):
            for fc in range(NCH):
                s = sq[fc % 2]
                ksq = ("sq", fc % 2)
                if fc % 2 == 0:
                    act(s[:], xT[:, fc, :], AF.Square, [kx], [ksq])
                else:
                    tt("dve", s[:], xT[:, fc, :], xT[:, fc, :], ALU.mult, [kx], [ksq])
                for hb in range(TT // 512):
                    def f(hb=hb, s=s, fc=fc):
                        return nc.tensor.matmul(PS[ssb + hb][:], ones[:], s[:, hb * 512:(hb + 1) * 512],
                                                start=(fc == 0), stop=(fc == NCH - 1))
                    k.op("pe", f, reads=[ksq, "ones"], writes=[psk(ssb + hb)])
            for hb in range(TT // 512):
                sl = slice(hb * 512, (hb + 1) * 512)
                ts("dve", rstd[:, sl], PS[ssb + hb][:], 1.0 / D, EPS, ALU.mult, ALU.add,
                   [psk(ssb + hb)], ["rstd"])
                k.op("act", lambda sl=sl: nc.scalar.sqrt(rstd[:, sl], rstd[:, sl]), ["rstd"], ["rstd"])
                k.op("dve", lambda sl=sl: nc.vector.reciprocal(rstd[:, sl], rstd[:, sl]), ["rstd"], ["rstd"])

        for l in range(DEPTH):
            last = (l == DEPTH - 1)
            with ExitStack() as ph:
                xT = sb("a_xT", [128, NCH, TT], F32, ph)
                hT = sb("a_hT", [128, NCH, TT], MDT, ph)
                rstd = sb("a_rstd", [128, TT], F32, ph)
                sq = [sb("a_sq%d" % i, [128, TT], F32, ph) for i in range(2)]
                wbufs = [sb("a_w%d" % i, [128, NCH * 128], MDT, ph) for i in range(8)]
                sgm = [sb("a_sm%d" % i, [128, TT], MDT, ph) for i in range(2)]
                sgf = [sb("a_sf%d" % i, [128, TT], F32, ph) for i in range(2)]
                tmf = [sb("a_tf%d" % i, [128, 4, 128], F32, ph) for i in range(2)]
                tmm = [sb("a_tm%d" % i, [128, 4, 128], MDT, ph) for i in range(2)]
                ws = WStream(k, wbufs, [win_t[l, oc] for oc in range(32)] * NT)
                wi = 0
                n_sm = n_sf = n_tm = 0
                for t in range(NT):
                    cond = 0 if t == 0 else 1
                    tok0 = t * TT
                    for fc in range(NCH):
                        k.dma("sp", xT[:, fc, :], XT[fc * 128:(fc + 1) * 128, tok0:tok0 + TT],
                              reads=[("XT", t)], writes=["a_xT"])
                    rms_rstd(xT, "a_xT", rstd, sq, 4)
                    for fc in range(NCH):
                        s = sq[fc % 2]
                        ksq = ("sq", fc % 2)
                        tt("dve", s[:], xT[:, fc, :], rstd[:], ALU.mult, ["a_xT", "rstd"], [ksq])
                        act(hT[:, fc, :], s[:], AF.Identity, [ksq, "modv"], ["a_hT"],
                            scale=modv[:, l, 0, fc, cond:cond + 1], bias=modv[:, l, 1, fc, cond:cond + 1])
                    if BISECT == -1:
                        return finish(nc, k)
                    for oc in range(32):
                        if BISECT is not None and oc == BISECT % 100:
                            return finish(nc, k)
                        w, kw = ws.get(wi)
                        wi += 1
                        wv = w[:].rearrange("p (k c) -> p k c", k=NCH)
                        acc = 2 * (oc % 2)
                        for hb in range(TT // 512):
                            mm(PS[acc + hb][:],
                               [(wv[:, kc, :], hT[:, kc, hb * 512:(hb + 1) * 512]) for kc in range(NCH)],
                               [kw, "a_hT"], [psk(acc + hb)])
                        kind = ("q" if oc < 8 else "k" if oc < 16 else "v" if oc < 24
                                else "u" if oc < 28 else "p")
                        need_f32 = kind in ("u", "p", "v") or (kind == "k" and t == 0)
                        need_m = kind in ("q", "k")
                        pr = [psk(acc), psk(acc + 1)]
                        if need_m:
                            s = sgm[n_sm % 2]
                            ks = ("a_sm", n_sm % 2)
                            n_sm += 1
                            for hb in range(TT // 512):
                                sl = slice(hb * 512, (hb + 1) * 512)
                                if kind == "q":
                                    k.op("act", lambda s=s, sl=sl, hb=hb: nc.scalar.mul(s[:, sl], PS[acc + hb][:], 0.125),
                                         [psk(acc + hb)], [ks])
                                else:
                                    cp("act", s[:, sl], PS[acc + hb][:], [psk(acc + hb)], [ks])
                            dst = QT if kind == "q" else KT
                            r0 = (oc % 8) * 128
                            k.dma("sp", dst[r0:r0 + 128, tok0:tok0 + TT], s[:], reads=[ks],
                                  writes=[("QK", kind, t)])
                        if need_f32:
                            s = sgf[n_sf % 2]
                            ks = ("a_sf", n_sf % 2)
                            n_sf += 1
                            for hb in range(TT // 512):
                                sl = slice(hb * 512, (hb + 1) * 512)
                                cp("dve", s[:, sl], PS[acc + hb][:], [psk(acc + hb)], [ks])
                            if kind in ("u", "p"):
                                dst = UT if kind == "u" else PT
                                r0 = (oc % 4) * 128
                                k.dma("sp", dst[r0:r0 + 128, tok0:tok0 + TT], s[:], reads=[ks],
                                      writes=[("UP", kind, t)])
                            else:
                                ocl = oc % 8
                                for q4 in range(TT // 512):
                                    if BISECT is not None and BISECT >= 200:
                                        continue
                                    tb = 6 + (n_tm % 2)
                                    def f(s=s, q4=q4, tb=tb):
                                        ins = None
                                        for b in range(4):
                                            c0 = q4 * 512 + b * 128
                                            ins = nc.tensor.transpose(PS[tb][:, b * 128:(b + 1) * 128],
                                                                      s[:, c0:c0 + 128], ident[:])
                                        return ins
                                    k.op("pe", f, reads=[ks, "ident"], writes=[psk(tb)])
                                    pv = PS[tb][:].rearrange("p (b f) -> p b f", b=4)
                                    if kind == "v":
                                        m = tmm[n_tm % 2]
                                        km = ("a_tmm", n_tm % 2)
                                        cp("act", m[:], pv, [psk(tb)], [km])
                                        r0 = tok0 + q4 * 512
                                        k.dma("sp", VTOK[r0:r0 + 512, ocl * 128:(ocl + 1) * 128]
                                              .rearrange("(b p) f -> p b f", p=128), m[:],
                                              reads=[km], writes=[("VTOK", t)])
                                    if t == 0:
                                        m = tmf[n_tm % 2]
                                        km = ("a_tmf", n_tm % 2)
                                        cp("dve", m[:], pv, [psk(tb)], [km])
                                        dst = nk_out if kind == "k" else nv_out
                                        for sq_ in range(2):
                                            seq = q4 * 2 + sq_
                                            for hh in range(2):
                                                if BISECT is not None and BISECT >= 100:
                                                    continue
                                                k.dma("sp",
                                                      dst[seq, l, 2 * ocl + hh, :, :].rearrange("(b p) d -> p b d", p=128),
                                                      m[:, 2 * sq_:2 * sq_ + 2, hh * 64:(hh + 1) * 64],
                                                      reads=[km], writes=[("nkv", kind)])
                                    n_tm += 1
            k.barrier()
            if stop("A%d" % l):
                return finish(nc, k)

            phase_b_pool(c, l, PT, MIX, vecs, poolw_t, invcnt)
            k.barrier()
            if stop("Bp%d" % l):
                return finish(nc, k)
            phase_b_attn_prompt(c, l, QT, KT, VTOK, MIX)
            k.barrier()
            if stop("Ba%d" % l):
                return finish(nc, k)
            phase_b_attn_sample(c, l, QT, KT, VTOK, MIX, ck_d, cv_d, rpbtab, colmask)
            k.barrier()
            if stop("Bs%d" % l):
                return finish(nc, k)
            phase_b_ssm(c, l, UT, MIX, ssm_a, ssm_B, ssm_C, st0_d, vecs, wglu_t, nst_out)
            k.barrier()
            if stop("B%d" % l):
                return finish(nc, k)

            with ExitStack() as ph:
                xT = sb("c_xT", [128, NCH, TT], F32, ph)
                mT = sb("c_mT", [128, NCH, TT], MDT, ph)
                hid = sb("c_hid", [128, SCH, TT], MDT, ph)
                rstd = sb("c_rstd", [128, TT], F32, ph)
                sq = [sb("c_sq%d" % i, [128, TT], F32, ph) for i in range(2)]
                wbufs = [sb("c_w%d" % i, [128, NCH * 128], MDT, ph) for i in range(8)]
                srcs = []
                for t in range(NT):
                    srcs += [wout_t[l, oc] for oc in range(16)]
                    for sp_ in range(NSPLIT):
                        for j in range(SCH):
                            srcs += [ffin_t[l, sp_ * SCH + j], ffin_t[l, 44 + sp_ * SCH + j]]
                        srcs += [ffout_t[l, sp_, oc] for oc in range(16)]
                ws = WStream(k, wbufs, srcs)
                wi = 0
                for t in range(NT):
                    cond = 0 if t == 0 else 1
                    tok0 = t * TT
                    for fc in range(NCH):
                        k.dma("sp", xT[:, fc, :], XT[fc * 128:(fc + 1) * 128, tok0:tok0 + TT],
                              reads=[("XT", t)], writes=[("c_xT", fc)])
                        k.dma("sp", mT[:, fc, :], MIX[fc * 128:(fc + 1) * 128, tok0:tok0 + TT],
                              reads=[("MIX", t)], writes=["c_mT"])
                    for oc in range(16):
                        w, kw = ws.get(wi)
                        wi += 1
                        wv = w[:].rearrange("p (k c) -> p k c", k=NCH)
                        acc = 2 * (oc % 4)
                        for hb in range(TT // 512):
                            sl = slice(hb * 512, (hb + 1) * 512)
                            mm(PS[acc + hb][:], [(wv[:, kc, :], mT[:, kc, sl]) for kc in range(NCH)],
                               [kw, "c_mT"], [psk(acc + hb)])
                            stt(xT[:, oc, sl], PS[acc + hb][:], modv[:, l, 2, oc, cond:cond + 1], xT[:, oc, sl],
                                ALU.mult, ALU.add, [psk(acc + hb), ("c_xT", oc), "modv"], [("c_xT", oc)])
                    allx = [("c_xT", fc) for fc in range(NCH)]
                    for fc in range(NCH):
                        s = sq[fc % 2]
                        ksq = ("sq", fc % 2)
                        if fc % 2 == 0:
                            act(s[:], xT[:, fc, :], AF.Square, [("c_xT", fc)], [ksq])
                        else:
                            tt("dve", s[:], xT[:, fc, :], xT[:, fc, :], ALU.mult, [("c_xT", fc)], [ksq])
                        for hb in range(TT // 512):
                            def f(hb=hb, s=s, fc=fc):
                                return nc.tensor.matmul(PS[6 + hb][:], ones[:], s[:, hb * 512:(hb + 1) * 512],
                                                        start=(fc == 0), stop=(fc == NCH - 1))
                            k.op("pe", f, reads=[ksq, "ones"], writes=[psk(6 + hb)])
                    for hb in range(TT // 512):
                        sl = slice(hb * 512, (hb + 1) * 512)
                        ts("dve", rstd[:, sl], PS[6 + hb][:], 1.0 / D, EPS, ALU.mult, ALU.add,
                           [psk(6 + hb)], ["rstd"])
                        k.op("act", lambda sl=sl: nc.scalar.sqrt(rstd[:, sl], rstd[:, sl]), ["rstd"], ["rstd"])
                        k.op("dve", lambda sl=sl: nc.vector.reciprocal(rstd[:, sl], rstd[:, sl]), ["rstd"], ["rstd"])
                    for fc in range(NCH):
                        s = sq[fc % 2]
                        ksq = ("sq", fc % 2)
                        tt("dve", s[:], xT[:, fc, :], rstd[:], ALU.mult, [("c_xT", fc), "rstd"], [ksq])
                        act(mT[:, fc, :], s[:], AF.Identity, [ksq, "modv"], ["c_mT"],
                            scale=modv[:, l, 3, fc, cond:cond + 1], bias=modv[:, l, 4, fc, cond:cond + 1])
                    for sp_ in range(NSPLIT):
                        for j in range(SCH):
                            wg, kwg = ws.get(wi)
                            wgv = wg[:].rearrange("p (k c) -> p k c", k=NCH)
                            pa = 4 * (j % 2)
                            for hb in range(TT // 512):
                                sl = slice(hb * 512, (hb + 1) * 512)
                                mm(PS[pa + hb][:], [(wgv[:, kc, :], mT[:, kc, sl]) for kc in range(NCH)],
                                   [kwg, "c_mT"], [psk(pa + hb)])
                            wu, kwu = ws.get(wi + 1)
                            wi += 2
                            wuv = wu[:].rearrange("p (k c) -> p k c", k=NCH)
                            for hb in range(TT // 512):
                                sl = slice(hb * 512, (hb + 1) * 512)
                                mm(PS[pa + 2 + hb][:], [(wuv[:, kc, :], mT[:, kc, sl]) for kc in range(NCH)],
                                   [kwu, "c_mT"], [psk(pa + 2 + hb)])
                            for hb in range(TT // 512):
                                sl = slice(hb * 512, (hb + 1) * 512)
                                s = sq[hb]
                                ksq = ("sq", hb)
                                act(s[:, 0:512], PS[pa + hb][:], AF.Silu, [psk(pa + hb)], [ksq])
                                tt("dve", hid[:, j, sl], s[:, 0:512], PS[pa + 2 + hb][:], ALU.mult,
                                   [ksq, psk(pa + 2 + hb)], [("hid", j)])
                        hk = [("hid", j) for j in range(SCH)]
                        for oc in range(16):
                            w, kw = ws.get(wi)
                            wi += 1
                            wv = w[:, 0:SCH * 128].rearrange("p (k c) -> p k c", k=SCH)
                            acc = 2 * (oc % 4)
                            for hb in range(TT // 512):
                                sl = slice(hb * 512, (hb + 1) * 512)
                                mm(PS[acc + hb][:], [(wv[:, kc, :], hid[:, kc, sl]) for kc in range(SCH)],
                                   [kw] + hk, [psk(acc + hb)])
                                stt(xT[:, oc, sl], PS[acc + hb][:], modv[:, l, 5, oc, cond:cond + 1], xT[:, oc, sl],
                                    ALU.mult, ALU.add, [psk(acc + hb), ("c_xT", oc), "modv"], [("c_xT", oc)])
                    if not last:
                        for fc in range(NCH):
                            k.dma("sp", XT[fc * 128:(fc + 1) * 128, tok0:tok0 + TT], xT[:, fc, :],
                                  reads=[("c_xT", fc)], writes=[("XT", t)])
                    else:
                        for fc in range(NCH):
                            s = sq[fc % 2]
                            ksq = ("sq", fc % 2)
                            if fc % 2 == 0:
                                act(s[:], xT[:, fc, :], AF.Square, [("c_xT", fc)], [ksq])
                            else:
                                tt("dve", s[:], xT[:, fc, :], xT[:, fc, :], ALU.mult, [("c_xT", fc)], [ksq])
                            for hb in range(TT // 512):
                                def f(hb=hb, s=s, fc=fc):
                                    return nc.tensor.matmul(PS[6 + hb][:], ones[:], s[:, hb * 512:(hb + 1) * 512],
                                                            start=(fc == 0), stop=(fc == NCH - 1))
                                k.op("pe", f, reads=[ksq, "ones"], writes=[psk(6 + hb)])
                        for hb in range(TT // 512):
                            sl = slice(hb * 512, (hb + 1) * 512)
                            ts("dve", rstd[:, sl], PS[6 + hb][:], 1.0 / D, EPS, ALU.mult, ALU.add,
                               [psk(6 + hb)], ["rstd"])
                            k.op("act", lambda sl=sl: nc.scalar.sqrt(rstd[:, sl], rstd[:, sl]), ["rstd"], ["rstd"])
                            k.op("dve", lambda sl=sl: nc.vector.reciprocal(rstd[:, sl], rstd[:, sl]), ["rstd"], ["rstd"])
                        for fc in range(NCH):
                            stt(xT[:, fc, :], xT[:, fc, :], gsb[:, 4, fc:fc + 1], rstd[:], ALU.mult, ALU.mult,
                                [("c_xT", fc), "rstd", "gsb"], [("c_xT", fc)])
                        ytm = [sq[0], sq[1]]
                        it = 0
                        for b in range(TT // 128):
                            for q4 in range(4):
                                tb = (it % 2)
                                def f(b=b, q4=q4, tb=tb):
                                    ins = None
                                    for j in range(4):
                                        fc = q4 * 4 + j
                                        ins = nc.tensor.transpose(PS[tb][:, j * 128:(j + 1) * 128],
                                                                  xT[:, fc, b * 128:(b + 1) * 128], ident[:])
                                    return ins
                                k.op("pe", f, reads=[("c_xT", q4 * 4 + j) for j in range(4)] + ["ident"],
                                     writes=[psk(tb)])
                                s = ytm[it % 2]
                                ksq = ("sq", it % 2)
                                cp("act" if it % 2 else "dve", s[:, 0:512], PS[tb][:], [psk(tb)], [ksq])
                                r0 = tok0 + b * 128
                                k.dma("sp", y_out[r0:r0 + 128, q4 * 512:(q4 + 1) * 512], s[:, 0:512],
                                      reads=[ksq], writes=["y"])
                                it += 1
            k.barrier()
            if stop("C%d" % l):
                return finish(nc, k)
        return finish(nc, k)


def finish(nc, k, **kw):
    k.barrier()
    return nc


ITEMS = [(s * 256, 256, s) for s in range(4)] + [(TP, TS, None)]


def phase_b_pool(c, l, PT, MIX, vecs, poolw_t, invcnt):
    nc, k = c.nc, c.k
    PS, psk = c.PS, c.psk
    with ExitStack() as ph:
        sb = lambda n, s, d=F32: c.sb(n, s, d, ph)
        L = TS + 16
        pp = sb("bp_pp", [128, L])
        sa = sb("bp_sa", [128, L])
        sbb = sb("bp_sb", [128, L])
        mix = sb("bp_mix", [128, TS])
        icn = sb("bp_icn", [128, 4, 256 + TS])
        pw = sb("bp_pw", [128, 4, 128])
        vec = sb("bp_vec", [128, 16])
        stg = [sb("bp_st%d" % i, [128, 512], MDT) for i in range(2)]
        k.dma("sp", icn[:], invcnt, writes=["icn"])
        k.dma("sp", pw[:], poolw_t[l].rearrange("p (g d) -> p g d", g=4), writes=["pw"])
        k.dma("sp", vec[:], vecs[l], writes=["vec"])
        k.op("dve", lambda: nc.vector.memset(pp[:, 0:8], 0.0), writes=["pp"])
        n_st = 0
        for g in range(4):
            w = 2 << g
            for (off, n, pidx) in ITEMS:
                Ln = n + 16
                ioff = 0 if pidx is not None else 256
                k.op("dve", lambda: nc.vector.memset(pp[:, 8 + n:16 + n], 0.0), writes=["pp"])
                k.dma("sp", pp[:, 8:8 + n], PT[g * 128:(g + 1) * 128, off:off + n], writes=["pp"])
                c.tt("dve", sa[:, 0:Ln - 1], pp[:, 0:Ln - 1], pp[:, 1:Ln], ALU.add, ["pp"], ["sa"])
                src, ks, start = sa, "sa", 7
                if w >= 4:
                    c.tt("dve", sbb[:, 0:Ln - 3], sa[:, 0:Ln - 3], sa[:, 2:Ln - 1], ALU.add, ["sa"], ["sb"])
                    src, ks, start = sbb, "sb", 6
                if w >= 8:
                    c.tt("dve", sa[:, 0:Ln - 7], sbb[:, 0:Ln - 7], sbb[:, 4:Ln - 3], ALU.add, ["sb"], ["sa"])
                    src, ks, start = sa, "sa", 4
                if w >= 16:
                    c.tt("dve", sbb[:, 0:Ln - 15], sa[:, 0:Ln - 15], sa[:, 8:Ln - 7], ALU.add, ["sa"], ["sb"])
                    src, ks, start = sbb, "sb", 0
                c.tt("dve", mix[:, 0:n], src[:, start:start + n], icn[:, g, ioff:ioff + n], ALU.mult,
                     [ks, "icn"], ["mix"])
                c.tt("dve", mix[:, 0:n], mix[:, 0:n], pp[:, 8:8 + n], ALU.subtract, ["mix", "pp"], ["mix"])
                for c0 in range(0, n, 512):
                    cw = min(512, n - c0)
                    bank = n_st % 2
                    c.mm(PS[bank][:, 0:cw], [(pw[:, g, :], mix[:, c0:c0 + cw])], ["pw", "mix"], [psk(bank)])
                    s = stg[n_st % 2]
                    kst = ("bp_st", n_st % 2)
                    n_st += 1
                    c.act(s[:, 0:cw], PS[bank][:, 0:cw], AF.Copy, [psk(bank), "vec"], [kst],
                          scale=vec[:, 12 + g:13 + g])
                    k.dma("sp", MIX[1536 + g * 128:1536 + (g + 1) * 128, off + c0:off + c0 + cw], s[:, 0:cw],
                          reads=[kst], writes=[("MIX", "p")])


def phase_b_attn_prompt(c, l, QT, KT, VTOK, MIX):
    nc, k = c.nc, c.k
    PS, psk, ident = c.PS, c.psk, c.ident
    with ExitStack() as ph:
        sb = lambda n, s, d=F32: c.sb(n, s, d, ph)
        qt = [sb("pa_q%d" % i, [128, 256], MDT) for i in range(2)]
        kt = [sb("pa_k%d" % i, [128, 256], MDT) for i in range(2)]
        vt = [sb("pa_v%d" % i, [128, 2, 128], MDT) for i in range(2)]
        pb = [sb("pa_p%d" % i, [128, 256]) for i in range(2)]
        ptb = [sb("pa_pt%d" % i, [128, 2, 128], MDT) for i in range(2)]
        st = [sb("pa_st%d" % i, [128, 4]) for i in range(2)]
        og = [sb("pa_o%d" % i, [128, 256], MDT) for i in range(2)]
        it = 0
        n_in = 0
        for s in range(4):
            for j in range(8):
                b = n_in % 2
                n_in += 1
                q, kk, v = qt[b], kt[b], vt[b]
                kq, kkk, kv = ("pa_q", b), ("pa_k", b), ("pa_v", b)
                k.dma("sp", q[:], QT[j * 128:(j + 1) * 128, s * 256:(s + 1) * 256], writes=[kq])
                k.dma("sp", kk[:], KT[j * 128:(j + 1) * 128, s * 256:(s + 1) * 256], writes=[kkk])
                k.dma("sp", v[:], VTOK[s * 256:(s + 1) * 256, j * 128:(j + 1) * 128]
                      .rearrange("(b p) f -> p b f", p=128), writes=[kv])
                ob = 4 + b
                for hh in range(2):
                    hs = slice(64 * hh, 64 * hh + 64)
                    for qb in range(2):
                        i2 = it % 2
                        it += 1
                        sbk = i2
                        tbk = 2 + i2
                        c.mm(PS[sbk][:, 0:256], [(q[hs, qb * 128:(qb + 1) * 128], kk[hs, :])], [kq, kkk], [psk(sbk)])
                        stt_ = st[i2]
                        kst = ("pa_st", i2)
                        k.op("dve", lambda: nc.vector.reduce_max(stt_[:, 0:1], PS[sbk][:, 0:256], AX.X),
                             [psk(sbk)], [kst])
                        c.ts("dve", stt_[:, 1:2], stt_[:, 0:1], -1.0, None, ALU.mult, None, [kst], [kst])
                        p = pb[i2]
                        kp = ("pa_p", i2)
                        c.act(p[:], PS[sbk][:, 0:256], AF.Exp, [psk(sbk), kst], [kp, kst],
                              bias=stt_[:, 1:2], accum_out=stt_[:, 2:3])
                        k.op("dve", lambda: nc.vector.reciprocal(stt_[:, 3:4], stt_[:, 2:3]), [kst], [kst])
                        c.ts("dve", p[:], p[:], stt_[:, 3:4], None, ALU.mult, None, [kp, kst], [kp])

                        def f():
                            ins = None
                            for kb in range(2):
                                ins = nc.tensor.transpose(PS[tbk][:, kb * 128:(kb + 1) * 128],
                                                          p[:, kb * 128:(kb + 1) * 128], ident[:])
                            return ins
                        k.op("pe", f, [kp, "ident"], [psk(tbk)])
                        pt = ptb[i2]
                        kpt = ("pa_pt", i2)
                        c.cp("act", pt[:], PS[tbk][:, 0:256].rearrange("p (b q) -> p b q", b=2), [psk(tbk)], [kpt])
                        c.mm(PS[ob][hs, qb * 128:(qb + 1) * 128],
                             [(v[:, kb, hs], pt[:, kb, :]) for kb in range(2)], [kv, kpt], [psk(ob)])
                o = og[b]
                ko = ("pa_o", b)
                c.cp("dve", o[:], PS[ob][:, 0:256], [psk(ob)], [ko])
                k.dma("sp", MIX[j * 128:(j + 1) * 128, s * 256:(s + 1) * 256], o[:], reads=[ko],
                      writes=[("MIX", "a")])


def phase_b_attn_sample(c, l, QT, KT, VTOK, MIX, ck_d, cv_d, rpbtab, colmask):
    nc, k = c.nc, c.k
    PS, psk, ident = c.PS, c.psk, c.ident
    NR = TS // GW
    with ExitStack() as ph:
        sb = lambda n, s, d=F32: c.sb(n, s, d, ph)
        qt = sb("sa_q", [128, TS], MDT)
        kt = sb("sa_k", [128, TS], MDT)
        vt = sb("sa_v", [128, 16, 128], MDT)
        vt2 = sb("sa_v2", [128, 15, 128], MDT)
        ckt = sb("sa_ck", [128, 4, 128])
        kcT = sb("sa_kcT", [128, PAST], MDT)
        vc = sb("sa_vc", [128, 4, 128], MDT)
        tab = sb("sa_tab", [128, 15, 64])
        thi = sb("sa_thi", [128, 15 * 64], MDT)
        tlo = sb("sa_tlo", [128, 15 * 64], MDT)
        cm = sb("sa_cm", [128, 64])
        idm = sb("sa_idm", [128, 128], MDT)
        pb = [sb("sa_p%d" % i, [64, 1024]) for i in range(2)]
        ptb = [sb("sa_pt%d" % i, [128, 8, 64], MDT) for i in range(2)]
        st = [sb("sa_st%d" % i, [64, 4]) for i in range(2)]
        osb = [sb("sa_os%d" % i, [64, 128]) for i in range(2)]
        og = [sb("sa_o%d" % i, [128, 512], MDT) for i in range(2)]
        k.dma("sp", cm[0:64, :], colmask, writes=["cm"])
        k.dma("sp", cm[64:128, :], colmask, writes=["cm"])
        c.cp("dve", idm[:], ident[:], ["ident"], ["idm"])
        it = 0
        n_og = 0
        for j in range(8):
            k.dma("sp", qt[:], QT[j * 128:(j + 1) * 128, TP:T], writes=["sa_q"])
            k.dma("sp", kt[:], KT[j * 128:(j + 1) * 128, TP:T], writes=["sa_k"])
            k.dma("sp", vt[:], VTOK[TP:T, j * 128:(j + 1) * 128].rearrange("(b p) f -> p b f", p=128),
                  writes=["sa_v"])
            k.dma("sp", vt2[:], VTOK[TP + 64:T - 64, j * 128:(j + 1) * 128].rearrange("(b p) f -> p b f", p=128),
                  writes=["sa_v"])
            for hh in range(2):
                k.dma("sp", ckt[:, :, 64 * hh:64 * hh + 64],
                      ck_d[l, 2 * j + hh].rearrange("(b p) d -> p b d", p=128), writes=["sa_ck"])
                k.dma("pool", vc[:, :, 64 * hh:64 * hh + 64],
                      cv_d[l, 2 * j + hh].rearrange("(b p) d -> p b d", p=128), writes=["sa_vc"])
                k.dma("sp", tab[64 * hh:64 * hh + 64, :, :],
                      rpbtab[l, 2 * j + hh].rearrange("q (r c) -> q r c", r=15), writes=["sa_tab"])

            def f():
                ins = None
                for b in range(4):
                    ins = nc.tensor.transpose(PS[7][:, b * 128:(b + 1) * 128], ckt[:, b, :], ident[:])
                return ins
            k.op("pe", f, ["sa_ck", "ident"], [psk(7)])
            c.cp("act", kcT[:], PS[7][:], [psk(7)], ["sa_kcT"])
            for dr in range(15):
                c.tt("dve", tab[:, dr, :], tab[:, dr, :], cm[:], ALU.add, ["sa_tab", "cm"], ["sa_tab"])
            tabf = tab[:].rearrange("p r c -> p (r c)")
            c.cp("dve", thi[:], tabf, ["sa_tab"], ["sa_thi"])
            c.tt("dve", tabf, tabf, thi[:], ALU.subtract, ["sa_tab", "sa_thi"], ["sa_tab"])
            c.cp("dve", tlo[:], tabf, ["sa_tab"], ["sa_tlo"])
            obk = 6
            items = [(r, hh) for r in range(NR) for hh in range(2)]
            ctxs = {}

            def p1(idx):
                r, hh = items[idx]
                rs = min(max(r - 4, 0), NR - 8)
                d0 = rs - r + 7
                hs = slice(64 * hh, 64 * hh + 64)
                i2 = idx % 2
                sA, sB = 2 * i2, 2 * i2 + 1
                qs = qt[hs, r * 64:(r + 1) * 64]
                c.mm(PS[sA][0:64, :], [(qs, kt[hs, rs * 64:(rs + 8) * 64]),
                                        (idm[hs, hs], thi[hs, d0 * 64:(d0 + 8) * 64]),
                                        (idm[hs, hs], tlo[hs, d0 * 64:(d0 + 8) * 64])],
                     ["sa_q", "sa_k", "sa_thi", "sa_tlo", "idm"], [psk(sA)])
                c.mm(PS[sB][0:64, :], [(qs, kcT[hs, :])], ["sa_q", "sa_kcT"], [psk(sB)])
                S = c.PSB[i2][0:64, :]
                s_ = st[i2]
                kst = ("sa_st", i2)
                k.op("dve", lambda: nc.vector.reduce_max(s_[:, 0:1], S, AX.X), [psk(sA), psk(sB)], [kst])
                c.ts("dve", s_[:, 1:2], s_[:, 0:1], -1.0, None, ALU.mult, None, [kst], [kst])
                p = pb[i2]
                kp = ("sa_p", i2)
                c.act(p[:], S, AF.Exp, [psk(sA), psk(sB), kst], [kp, kst],
                      bias=s_[:, 1:2], accum_out=s_[:, 2:3])
                k.op("dve", lambda: nc.vector.reciprocal(s_[:, 3:4], s_[:, 2:3]), [kst], [kst])

            def p2(idx):
                nonlocal n_og
                r, hh = items[idx]
                rs = min(max(r - 4, 0), NR - 8)
                hs = slice(64 * hh, 64 * hh + 64)
                i2 = idx % 2
                tbk = 4 + i2
                s_ = st[i2]
                kst = ("sa_st", i2)
                p = pb[i2]
                kp = ("sa_p", i2)

                def f():
                    ins = None
                    for kb in range(8):
                        ins = nc.tensor.transpose(PS[tbk][:, kb * 64:(kb + 1) * 64],
                                                  p[:, kb * 128:(kb + 1) * 128], ident[0:64, 0:64])
                    return ins
                k.op("pe", f, [kp, "ident"], [psk(tbk)])
                pt = ptb[i2]
                kpt = ("sa_pt", i2)
                c.cp("dve" if i2 else "act", pt[:], PS[tbk][:].rearrange("p (b q) -> p b q", b=8),
                     [psk(tbk)], [kpt])
                vloc = vt if rs % 2 == 0 else vt2
                pairs = [(pt[:, kb, :], vloc[:, rs // 2 + kb, hs]) for kb in range(4)]
                pairs += [(pt[:, 4 + kb, :], vc[:, kb, hs]) for kb in range(4)]
                c.mm(PS[7][0:64, hh * 64:(hh + 1) * 64], pairs, [kpt, "sa_v", "sa_vc"], [psk(7)])
                o = osb[r % 2]
                ko = ("sa_os", r % 2)
                c.act(o[:, hh * 64:(hh + 1) * 64], PS[7][0:64, hh * 64:(hh + 1) * 64], AF.Copy,
                      [psk(7), kst], [ko], scale=s_[:, 3:4])
                if hh == 1:
                    cc = (r % 8) * 64
                    k.op("pe", lambda: nc.tensor.transpose(PS[obk][:, cc:cc + 64], o[:], ident[0:64, 0:64]),
                         [ko, "ident"], [psk(obk)])
                    if r % 8 == 7:
                        r0 = r - 7
                        ogt = og[n_og % 2]
                        kog = ("sa_o", n_og % 2)
                        n_og += 1
                        c.cp("act", ogt[:], PS[obk][:], [psk(obk)], [kog])
                        k.dma("sp", MIX[j * 128:(j + 1) * 128, TP + r0 * 64:TP + (r0 + 8) * 64], ogt[:],
                              reads=[kog], writes=[("MIX", "a")])

            p1(0)
            for idx in range(1, len(items)):
                p1(idx)
                p2(idx - 1)
            p2(len(items) - 1)


def phase_b_ssm(c, l, UT, MIX, ssm_a, ssm_B, ssm_C, st0_d, vecs, wglu_t, nst_out):
    nc, k = c.nc, c.k
    PS, psk, ident = c.PS, c.psk, c.ident
    PI = math.pi
    C1 = 6.28125
    C2 = TWO_PI - C1
    with ExitStack() as ph:
        sb = lambda n, s, d=F32: c.sb(n, s, d, ph)
        utc = sb("ss_ut", [128, T])
        yTc = sb("ss_yT", [128, T])
        Bl = sb("ss_Bl", [128, 32, 2, 128])
        Cl = sb("ss_Cl", [128, 2, 32, 128])
        pa = sb("ss_pa", [128, 3, 32])
        Bst = sb("ss_Bst", [128, 2, 32, 16])
        Bb = sb("ss_Bb", [128, 2, 32, 16])
        h0 = sb("ss_h0", [128, 2, 16, 2])
        sm = sb("ss_sm", [128, 24, 32])
        zp = [sb("ss_zp%d" % i, [128, 128]) for i in range(8)]
        iot = sb("ss_iota", [128, LC])
        iot_i = sb("ss_iotai", [128, LC], I32)
        onesr = sb("ss_ones", [128, LC])
        ki = sb("ss_ki", [128, LC], I32)
        wk = [sb("ss_wk%d" % i, [128, LC]) for i in range(4)]
        cosT = sb("ss_cos", [128, LC])
        sinT = sb("ss_sin", [128, LC])
        magT = sb("ss_mag", [128, LC])
        fin = sb("ss_fin", [128, 4, 2, 16, 2])
        vec = sb("ss_vec", [128, 16])
        tmp = {}
        for nm in ("t1", "t2", "t3", "t4", "gr", "gi", "sr", "si", "hr", "hi"):
            tmp[nm] = [sb("ss_%s%d" % (nm, i), [128, LC]) for i in range(2)]
        ini = [sb("ss_ini%d" % i, [128, 4]) for i in range(2)]

        DT_, AR, TH, MAG, COS, SIN, ABR, ABI, NR_, NI_, DEN, CR, CI, RR, RI, I0R, I0I, W0, W1, W2, W3 = range(21)

        k.dma("sp", pa[:], ssm_a[l], writes=["pa"])
        k.dma("sp", Bst[:], ssm_B[l], writes=["Bst"])
        for ri in range(2):
            k.dma("sp", Cl[:, ri], ssm_C[l, ri].rearrange("p (b c) -> p b c", b=32), writes=["Cl"])
        k.dma("sp", h0[:], st0_d[l], writes=["h0"])
        k.dma("sp", vec[:], vecs[l], writes=["vec"])
        for z in zp:
            k.op("pool", lambda: nc.gpsimd.memset(z[:], 0.0), writes=["zp"])
        k.op("pool", lambda: nc.gpsimd.memset(onesr[:], 1.0), writes=["onesr"])
        k.op("pool", lambda: nc.gpsimd.iota(iot_i[:], pattern=[[1, LC]], base=0, channel_multiplier=0),
             writes=["iot_i"])
        c.cp("dve", iot[:], iot_i[:], ["iot_i"], ["iot"])
        c.ts("dve", Cl[:, 1], Cl[:, 1], -1.0, None, ALU.mult, None, ["Cl"], ["Cl"])

        def sincos(phi, kphi, n, s_out, c_out, kout):
            for which, out in ((0, s_out), (1, c_out)):
                a, b = wk[0], wk[1]
                if which == 0:
                    c.cp("dve", a[:, 0:n], phi, [kphi], ["wk0"])
                else:
                    c.ts("dve", a[:, 0:n], phi, PI / 2, None, ALU.add, None, [kphi], ["wk0"])
                c.ts("dve", b[:, 0:n], a[:, 0:n], 1.0 / TWO_PI, None, ALU.mult, None, ["wk0"], ["wk1"])
                c.cp("dve", ki[:, 0:n], b[:, 0:n], ["wk1"], ["ki"])
                c.cp("dve", b[:, 0:n], ki[:, 0:n], ["ki"], ["wk1"])
                c.stt(a[:, 0:n], b[:, 0:n], -C1, a[:, 0:n], ALU.mult, ALU.add, ["wk0", "wk1"], ["wk0"])
                c.stt(a[:, 0:n], b[:, 0:n], -C2, a[:, 0:n], ALU.mult, ALU.add, ["wk0", "wk1"], ["wk0"])
                c.ts("dve", b[:, 0:n], a[:, 0:n], PI, -TWO_PI, ALU.is_gt, ALU.mult, ["wk0"], ["wk1"])
                c.tt("dve", a[:, 0:n], a[:, 0:n], b[:, 0:n], ALU.add, ["wk0", "wk1"], ["wk0"])
                c.ts("dve", b[:, 0:n], a[:, 0:n], -PI, TWO_PI, ALU.is_lt, ALU.mult, ["wk0"], ["wk1"])
                c.tt("dve", a[:, 0:n], a[:, 0:n], b[:, 0:n], ALU.add, ["wk0", "wk1"], ["wk0"])
                c.ts("dve", a[:, 0:n], a[:, 0:n], PI, -PI, ALU.min, ALU.max, ["wk0"], ["wk0"])
                c.act(out, a[:, 0:n], AF.Sin, ["wk0"], [kout])

        S = lambda i: sm[:, i, :]
        c.act(S(DT_), pa[:, 2, :], AF.Exp, ["pa"], ["sm"])
        c.tt("dve", S(AR), pa[:, 0, :], S(DT_), ALU.mult, ["pa", "sm"], ["sm"])
        c.tt("dve", S(TH), pa[:, 1, :], S(DT_), ALU.mult, ["pa", "sm"], ["sm"])
        c.act(S(MAG), S(AR), AF.Exp, ["sm"], ["sm"])
        sincos(S(TH), "sm", 32, S(SIN), S(COS), "sm")
        c.tt("dve", S(ABR), S(MAG), S(COS), ALU.mult, ["sm"], ["sm"])
        c.tt("dve", S(ABI), S(MAG), S(SIN), ALU.mult, ["sm"], ["sm"])
        c.ts("dve", S(W0), S(ABR), -1.0, None, ALU.add, None, ["sm"], ["sm"])
        c.tt("dve", S(NR_), S(W0), pa[:, 0, :], ALU.mult, ["sm", "pa"], ["sm"])
        c.tt("dve", S(W1), S(ABI), pa[:, 1, :], ALU.mult, ["sm", "pa"], ["sm"])
        c.tt("dve", S(NR_), S(NR_), S(W1), ALU.add, ["sm"], ["sm"])
        c.tt("dve", S(NI_), S(ABI), pa[:, 0, :], ALU.mult, ["sm", "pa"], ["sm"])
        c.tt("dve", S(W1), S(W0), pa[:, 1, :], ALU.mult, ["sm", "pa"], ["sm"])
        c.tt("dve", S(NI_), S(NI_), S(W1), ALU.subtract, ["sm"], ["sm"])
        c.tt("dve", S(DEN), pa[:, 0, :], pa[:, 0, :], ALU.mult, ["pa"], ["sm"])
        c.tt("dve", S(W1), pa[:, 1, :], pa[:, 1, :], ALU.mult, ["pa"], ["sm"])
        c.tt("dve", S(DEN), S(DEN), S(W1), ALU.add, ["sm"], ["sm"])
        k.op("dve", lambda: nc.vector.reciprocal(S(DEN), S(DEN)), ["sm"], ["sm"])
        c.tt("dve", S(CR), S(NR_), S(DEN), ALU.mult, ["sm"], ["sm"])
        c.tt("dve", S(CI), S(NI_), S(DEN), ALU.mult, ["sm"], ["sm"])
        c.ts("dve", S(W2), S(TH), float(LC), None, ALU.mult, None, ["sm"], ["sm"])
        sincos(S(W2), "sm", 32, S(RI), S(RR), "sm")
        h0r = h0[:, :, :, 0]
        h0i = h0[:, :, :, 1]
        v3 = lambda i: sm[:, i, :].rearrange("p (d b) -> p d b", d=2)
        c.tt("dve", v3(I0R), v3(COS), h0r, ALU.mult, ["sm", "h0"], ["sm"])
        c.tt("dve", v3(W3), v3(SIN), h0i, ALU.mult, ["sm", "h0"], ["sm"])
        c.tt("dve", S(I0R), S(I0R), S(W3), ALU.subtract, ["sm"], ["sm"])
        c.tt("dve", v3(I0I), v3(COS), h0i, ALU.mult, ["sm", "h0"], ["sm"])
        c.tt("dve", v3(W3), v3(SIN), h0r, ALU.mult, ["sm", "h0"], ["sm"])
        c.tt("dve", S(I0I), S(I0I), S(W3), ALU.add, ["sm"], ["sm"])
        for db in range(32):
            cr, ci = sm[:, CR, db:db + 1], sm[:, CI, db:db + 1]
            c.ts("dve", Bb[:, 0, db, :], Bst[:, 1, db, :], ci, None, ALU.mult, None, ["Bst", "sm"], ["Bb"])
            c.stt(Bb[:, 0, db, :], Bst[:, 0, db, :], cr, Bb[:, 0, db, :], ALU.mult, ALU.subtract,
                  ["Bst", "sm", "Bb"], ["Bb"])
            c.ts("dve", Bb[:, 1, db, :], Bst[:, 1, db, :], cr, None, ALU.mult, None, ["Bst", "sm"], ["Bb"])
            c.stt(Bb[:, 1, db, :], Bst[:, 0, db, :], ci, Bb[:, 1, db, :], ALU.mult, ALU.add,
                  ["Bst", "sm", "Bb"], ["Bb"])
        n_t = 0
        for db in range(32):
            blk = db % 16
            for ri in range(2):
                z = zp[(blk % 4) * 2 + ri]
                kz = ("zp", (blk % 4) * 2 + ri)
                for gl in range(2):
                    c0 = (blk % 4) * 32 + gl * 16
                    c.cp("dve", z[64 * gl:64 * gl + 64, c0:c0 + 16], Bb[64 * gl:64 * gl + 64, ri, db, :],
                         ["Bb"], [kz])
                bank = n_t % 2
                n_t += 1
                k.op("pe", lambda: nc.tensor.transpose(PS[bank][:, 0:128], z[:], ident[:]), [kz, "ident"], [psk(bank)])
                c.cp("act", Bl[:, db, ri, :], PS[bank][:, 0:128], [psk(bank)], ["Bl"])

        it = 0
        GC = 2.0 * math.sqrt(2.0 / math.pi)
        for db_i in range(32):
            cch, d, b4 = db_i // 8, (db_i // 4) % 2, db_i % 4
            blk = 4 * cch + b4
            db = d * 16 + blk
            rev = (d == 1)
            if db_i % 8 == 0:
                k.dma("sp", utc[:], UT[cch * 128:(cch + 1) * 128, :], reads=[("UTd", cch)], writes=["ut"])
                k.op("pool", lambda: nc.gpsimd.memset(yTc[:], 0.0), writes=["yT"])
            c.ts("dve", wk[2][:], iot[:], sm[:, TH, db:db + 1], None, ALU.mult, None, ["iot", "sm"], ["wk2"])
            sincos(wk[2][:], "wk2", LC, sinT[:], cosT[:], "tab")
            c.ts("dve", magT[:], onesr[:], sm[:, MAG, db:db + 1], None, ALU.mult, None, ["onesr", "sm"], ["magT"])
            chunks = [(s * 256, 256, s, True) for s in range(4)]
            sc = [(TP + cix * LC, LC, None, cix == 0) for cix in range(TS // LC)]
            if rev:
                sc = [(TP + cix * LC, LC, None, cix == TS // LC - 1) for cix in reversed(range(TS // LC))]
            chunks += sc
            prev = None
            for (off, n, pidx, first) in chunks:
                i2 = it % 2
                it += 1
                T_ = {nm: tmp[nm][i2] for nm in tmp}
                K_ = {nm: ("ss_" + nm, i2) for nm in tmp}
                rv = (lambda ap: ap[:, ::-1]) if rev else (lambda ap: ap)
                bR, bI, bY = 2 * i2, 2 * i2 + 1, 4 + i2
                ucols = utc[:, off:off + n]
                c.mm(PS[bR][:, 0:n], [(Bl[:, db, 0, :], ucols)], ["Bl", "ut"], [psk(bR)])
                c.mm(PS[bI][:, 0:n], [(Bl[:, db, 1, :], ucols)], ["Bl", "ut"], [psk(bI)])
                bre, bim = rv(PS[bR][:, 0:n]), rv(PS[bI][:, 0:n])
                cs_, sn_ = cosT[:, 0:n], sinT[:, 0:n]
                c.tt("dve", T_["t1"][:, 0:n], bre, cs_, ALU.mult, [psk(bR), "tab"], [K_["t1"]])
                c.tt("dve", T_["t2"][:, 0:n], bim, sn_, ALU.mult, [psk(bI), "tab"], [K_["t2"]])
                c.tt("dve", T_["t3"][:, 0:n], bim, cs_, ALU.mult, [psk(bI), "tab"], [K_["t3"]])
                c.tt("dve", T_["t4"][:, 0:n], bre, sn_, ALU.mult, [psk(bR), "tab"], [K_["t4"]])
                c.tt("pool", T_["gr"][:, 0:n], T_["t1"][:, 0:n], T_["t2"][:, 0:n], ALU.add,
                     [K_["t1"], K_["t2"]], [K_["gr"]])
                c.tt("pool", T_["gi"][:, 0:n], T_["t3"][:, 0:n], T_["t4"][:, 0:n], ALU.subtract,
                     [K_["t3"], K_["t4"]], [K_["gi"]])
                if pidx is not None:
                    ir, ii_ = 0.0, 0.0
                    kin = []
                elif first:
                    ir, ii_ = sm[:, I0R, db:db + 1], sm[:, I0I, db:db + 1]
                    kin = ["sm"]
                else:
                    pr, pi_, pn = prev
                    iv = ini[i2]
                    kin = [("ss_ini", i2)]
                    rr, ri_ = sm[:, RR, db:db + 1], sm[:, RI, db:db + 1]
                    c.ts("dve", iv[:, 0:1], pi_[:, pn - 1:pn], ri_, None, ALU.mult, None, [prev_k[1], "sm"], kin)
                    c.stt(iv[:, 1:2], pr[:, pn - 1:pn], rr, iv[:, 0:1], ALU.mult, ALU.subtract,
                          [prev_k[0], "sm"] + kin, kin)
                    c.ts("dve", iv[:, 2:3], pi_[:, pn - 1:pn], rr, None, ALU.mult, None, [prev_k[1], "sm"], kin)
                    c.stt(iv[:, 3:4], pr[:, pn - 1:pn], ri_, iv[:, 2:3], ALU.mult, ALU.add,
                          [prev_k[0], "sm"] + kin, kin)
                    ir, ii_ = iv[:, 1:2], iv[:, 3:4]
                sr, si = T_["sr"], T_["si"]
                k.op("dve", lambda: nc.vector.tensor_tensor_scan(sr[:, 0:n], magT[:, 0:n], T_["gr"][:, 0:n], ir,
                                                                 ALU.mult, ALU.add),
                     ["magT", K_["gr"]] + kin, [K_["sr"]])
                k.op("dve", lambda: nc.vector.tensor_tensor_scan(si[:, 0:n], magT[:, 0:n], T_["gi"][:, 0:n], ii_,
                                                                 ALU.mult, ALU.add),
                     ["magT", K_["gi"]] + kin, [K_["si"]])
                prev = (sr, si, n)
                prev_k = (K_["sr"], K_["si"])
                c.tt("pool", T_["t1"][:, 0:n], sr[:, 0:n], cs_, ALU.mult, [K_["sr"], "tab"], [K_["t1"]])
                c.tt("pool", T_["t2"][:, 0:n], si[:, 0:n], sn_, ALU.mult, [K_["si"], "tab"], [K_["t2"]])
                c.tt("pool", T_["t3"][:, 0:n], sr[:, 0:n], sn_, ALU.mult, [K_["sr"], "tab"], [K_["t3"]])
                c.tt("pool", T_["t4"][:, 0:n], si[:, 0:n], cs_, ALU.mult, [K_["si"], "tab"], [K_["t4"]])
                c.tt("dve", T_["hr"][:, 0:n], T_["t1"][:, 0:n], T_["t2"][:, 0:n], ALU.subtract,
                     [K_["t1"], K_["t2"]], [K_["hr"]])
                c.tt("dve", T_["hi"][:, 0:n], T_["t3"][:, 0:n], T_["t4"][:, 0:n], ALU.add,
                     [K_["t3"], K_["t4"]], [K_["hi"]])
                c.mm(PS[bY][:, 0:n], [(Cl[:, 0, db, :], T_["hr"][:, 0:n]), (Cl[:, 1, db, :], T_["hi"][:, 0:n])],
                     ["Cl", K_["hr"], K_["hi"]], [psk(bY)])
                c.tt("dve", yTc[:, off:off + n], yTc[:, off:off + n], rv(PS[bY][:, 0:n]), ALU.add,
                     [psk(bY), "yT"], ["yT"])
                if pidx is not None:
                    c.cp("act", fin[:, pidx, d, blk, 0:1], T_["hr"][:, n - 1:n], [K_["hr"]], ["fin"])
                    c.cp("act", fin[:, pidx, d, blk, 1:2], T_["hi"][:, n - 1:n], [K_["hi"]], ["fin"])
            if db_i % 8 == 7:
                for t0 in range(0, T, 512):
                    cols = slice(t0, t0 + 512)
                    yv = yTc[:, cols]
                    c.stt(yv, utc[:, cols], vec[:, cch:cch + 1], yv, ALU.mult, ALU.add, ["ut", "vec", "yT"], ["yT"])
                    a, b = wk[0], wk[1]
                    c.tt("dve", a[:, 0:512], yv, yv, ALU.mult, ["yT"], ["wk0"])
                    c.ts("dve", a[:, 0:512], a[:, 0:512], 0.044715, 1.0, ALU.mult, ALU.add, ["wk0"], ["wk0"])
                    c.tt("dve", a[:, 0:512], a[:, 0:512], yv, ALU.mult, ["wk0", "yT"], ["wk0"])
                    c.act(b[:, 0:512], a[:, 0:512], AF.Sigmoid, ["wk0"], ["wk1"], scale=GC)
                    c.tt("dve", yv, yv, b[:, 0:512], ALU.mult, ["yT", "wk1"], ["yT"])
                k.dma("sp", UT[cch * 128:(cch + 1) * 128, :], yTc[:], reads=["yT", "ut"], writes=[("UTd", cch)])
        for s in range(4):
            k.dma("sp", nst_out[s, l].rearrange("d (b g) p r -> (g p) d b r", g=2), fin[:, s],
                  reads=["fin"], writes=["nst"])
    k.barrier()
    with ExitStack() as ph:
        sb = lambda n, s, d=F32: c.sb(n, s, d, ph)
        vec = sb("sg_vec", [128, 16])
        wg = sb("sg_wg", [128, 4, 1024])
        gy = [sb("sg_gy%d" % i, [128, 4, 512]) for i in range(2)]
        wk = [sb("sg_wk%d" % i, [128, 512]) for i in range(2)]
        stg = [sb("sg_st%d" % i, [128, 512], MDT) for i in range(2)]
        k.dma("sp", vec[:], vecs[l], writes=["vec"])
        k.dma("sp", wg[:], wglu_t[l].rearrange("p (k c) -> p k c", k=4), writes=["wg"])
        n_s = 0
        for ti, t0 in enumerate(range(0, T, 512)):
            cols = slice(t0, t0 + 512)
            g = gy[ti % 2]
            kg = ("sg_gy", ti % 2)
            k.dma("sp", g[:], UT[:, cols].rearrange("(k p) t -> p k t", p=128), writes=[kg])
            for cc in range(4):
                bv, bg = cc % 2, 2 + cc % 2
                c.mm(PS[bv][:], [(wg[:, kc, cc * 128:(cc + 1) * 128], g[:, kc, :]) for kc in range(4)],
                     ["wg", kg], [psk(bv)])
                c.mm(PS[bg][:], [(wg[:, kc, 512 + cc * 128:512 + (cc + 1) * 128], g[:, kc, :]) for kc in range(4)],
                     ["wg", kg], [psk(bg)])
                a, b = wk[0], wk[1]
                c.act(a[:], PS[bv][:], AF.Identity, [psk(bv), "vec"], ["wk0"], bias=vec[:, 4 + cc:5 + cc])
                c.act(b[:], PS[bg][:], AF.Sigmoid, [psk(bg), "vec"], ["wk1"], bias=vec[:, 8 + cc:9 + cc])
                s = stg[n_s % 2]
                ks = ("sg_st", n_s % 2)
                n_s += 1
                c.tt("dve", s[:], a[:], b[:], ALU.mult, ["wk0", "wk1"], [ks])
                k.dma("sp", MIX[1024 + cc * 128:1024 + (cc + 1) * 128, cols], s[:], reads=[ks], writes=[("MIX", "s")])


def _prep_shared(inp):
    f = lambda a: np.ascontiguousarray(np.asarray(a), dtype=np.float32)
    L = DEPTH
    sh = {}
    w_mod = f(inp["w_mod"])
    sh["wmod_t"] = np.ascontiguousarray(
        w_mod.reshape(L, 16, 128, 24, 512).transpose(0, 3, 2, 1, 4)).reshape(L, 24, 128, 8192)
    sh["bmodT"] = np.ascontiguousarray(f(inp["b_mod"]).reshape(L, 96, 128).transpose(0, 2, 1))
    gs = [inp["norm1_g"][0], inp["norm1_g"][1], inp["norm2_g"][0], inp["norm2_g"][1], inp["final_norm_g"]]
    sh["gT"] = np.ascontiguousarray(np.stack([f(g).reshape(16, 128).T for g in gs], axis=1))
    w_in = f(inp["w_in"])
    sh["win_t"] = np.ascontiguousarray(
        w_in.reshape(L, 16, 128, 32, 128).transpose(0, 3, 2, 1, 4)).reshape(L, 32, 128, 2048)
    w_out = f(inp["w_out"])
    sh["wout_t"] = np.ascontiguousarray(
        w_out.reshape(L, 16, 128, 16, 128).transpose(0, 3, 2, 1, 4)).reshape(L, 16, 128, 2048)
    fi = f(inp["ffn_w_in"])
    sh["ffin_t"] = np.ascontiguousarray(
        fi.reshape(L, 16, 128, 88, 128).transpose(0, 3, 2, 1, 4)).reshape(L, 88, 128, 2048)
    fo = f(inp["ffn_w_out"])
    sh["ffout_t"] = np.ascontiguousarray(
        fo.reshape(L, NSPLIT, SCH, 128, 16, 128).transpose(0, 1, 4, 3, 2, 5)).reshape(L, NSPLIT, 16, 128, SCH * 128)
    cq = np.arange(64)
    idx = np.clip(cq[None, :] - cq[:, None], -15, 15) + 15
    rpb = f(inp["attn_rpb"])
    tab = rpb[:, :, :, idx]
    sh["rpbtab"] = np.ascontiguousarray(tab.transpose(0, 1, 3, 2, 4)).reshape(L, NH, 64, 15 * 64)
    cs = np.clip(cq - 8, 0, 48)
    valid = (cq[None, :] >= cs[:, None]) & (cq[None, :] < cs[:, None] + 16)
    sh["colmask"] = np.where(valid, 0.0, NEG).astype(np.float32)
    def st_lay(x):
        tail = x.shape[4:]
        y = x.reshape((L, 2, 16, 2, 64) + tail)
        perm = (0, 3, 4, 1, 2) + tuple(range(5, 5 + len(tail)))
        return np.ascontiguousarray(y.transpose(perm)).reshape((L, 128, 32) + tail)
    a_re, a_im = f(inp["ssm_a_re"]), f(inp["ssm_a_im"])
    ldt = np.broadcast_to(f(inp["ssm_log_dt"])[:, :, :, None], (L, 2, 32, 64))
    sh["ssm_a"] = np.ascontiguousarray(np.stack([st_lay(a_re), st_lay(a_im), st_lay(np.ascontiguousarray(ldt))], axis=2))
    sh["ssm_B"] = np.ascontiguousarray(np.stack([st_lay(f(inp["ssm_b_re"])), st_lay(f(inp["ssm_b_im"]))], axis=2))
    C = np.zeros((L, 2, 128, 32, 128), np.float32)
    for ri, key in enumerate(("ssm_c_re", "ssm_c_im")):
        cc = f(inp[key])
        for d in range(2):
            for bk in range(16):
                for gl in range(2):
                    c0 = (bk % 4) * 32 + gl * 16
                    C[:, ri, gl * 64:(gl + 1) * 64, d * 16 + bk, c0:c0 + 16] = \
                        cc[:, d, 2 * bk + gl].transpose(0, 2, 1)
    sh["ssm_C"] = C.reshape(L, 2, 128, 32 * 128)
    vec = np.zeros((L, 128, 16), np.float32)
    vec[:, :, 0:4] = f(inp["ssm_d"]).reshape(L, 4, 128).transpose(0, 2, 1)
    vec[:, :, 4:12] = f(inp["ssm_b_glu"]).reshape(L, 8, 128).transpose(0, 2, 1)
    vec[:, :, 12:16] = f(inp["pool_scale"]).reshape(L, 4, 128).transpose(0, 2, 1)
    sh["vecs"] = vec
    sh["wglu_t"] = np.ascontiguousarray(f(inp["ssm_w_glu"]).reshape(L, 4, 128, 1024).transpose(0, 2, 1, 3)).reshape(L, 128, 4096)
    sh["poolw_t"] = np.ascontiguousarray(f(inp["pool_w"]).transpose(0, 2, 1, 3)).reshape(L, 128, 512)
    ic = np.zeros((4, 256 + TS), np.float32)
    for n, o in ((256, 0), (TS, 256)):
        t = np.arange(n)
        for g in range(4):
            w = 2 << g
            lo = np.clip(t - w // 2, 0, n)
            hi = np.clip(t - w // 2 + w, 0, n)
            ic[g, o:o + n] = 1.0 / (hi - lo)
    sh["invcnt"] = np.ascontiguousarray(np.broadcast_to(ic[None], (128, 4, 256 + TS)))
    sh["ident"] = np.eye(128, dtype=np.float32)
    return sh


def _prep_core(inp, c, sh):
    f = lambda a: np.ascontiguousarray(np.asarray(a), dtype=np.float32)
    b = c // 4
    m = dict(sh)
    m["x_in"] = np.concatenate([f(inp["x_prompt"][4 * c:4 * c + 4]).reshape(TP, D), f(inp["x_sample"][b])], axis=0)
    cond = np.stack([f(inp["c_ctx"]), f(inp["c"][b])], axis=0)
    m["condT"] = np.ascontiguousarray(cond.reshape(2, 16, 128).transpose(2, 1, 0))
    m["ck"] = f(inp["cache_k"][b])
    m["cv"] = f(inp["cache_v"][b])
    st = f(inp["state_ssm"][b])
    m["st0"] = np.ascontiguousarray(
        st.reshape(DEPTH, 2, 16, 2, 64, 2).transpose(0, 3, 4, 1, 2, 5)).reshape(DEPTH, 128, 2, 16, 2)
    return m


_NC_CACHE = {}


def kernel(**inp):
    n = 8
    key = (str(MDT), DEBUG, STOP_AFTER)
    if key not in _NC_CACHE:
        _NC_CACHE[key] = build_program()
    nc = _NC_CACHE[key]
    sh = _prep_shared(inp)
    in_maps = [_prep_core(inp, c, sh) for c in range(n)]
    res = run_bass_kernel_spmd(nc, in_maps, core_ids=list(range(n)))
    R = res.results
    y_prompt = np.concatenate([R[c]["y"][:TP].reshape(4, 256, D) for c in range(n)], axis=0)
    y_sample = np.stack([R[0]["y"][TP:], R[4]["y"][TP:]], axis=0)
    nk = np.concatenate([R[c]["nk"] for c in range(n)], axis=0)
    nv = np.concatenate([R[c]["nv"] for c in range(n)], axis=0)
    nst = np.concatenate([R[c]["nst"] for c in range(n)], axis=0)
    return (np.ascontiguousarray(y_prompt, dtype=np.float32), np.ascontiguousarray(y_sample, dtype=np.float32),
            np.ascontiguousarray(nk, dtype=np.float32), np.ascontiguousarray(nv, dtype=np.float32),
            np.ascontiguousarray(nst, dtype=np.float32))
```

```python
import math
from contextlib import ExitStack

import numpy as np
import concourse.bass as bass
import concourse.mybir as mybir
from concourse.bass_utils import run_bass_kernel_spmd

F32 = mybir.dt.float32
BF16 = mybir.dt.bfloat16
I32 = mybir.dt.int32
AF = mybir.ActivationFunctionType
ALU = mybir.AluOpType
AX = mybir.AxisListType

D = 2048
NCH = 16
DEPTH = 2
NH = 16
HD = 64
TP = 1024
TS = 2048
T = TP + TS
TT = 1024
NT = T // TT
FF = 5632
FCH = 44
NSPLIT = 4
SCH = FCH // NSPLIT
PAST = 512
GW = 64
EPS = 1e-6
NEG = -1e30
LC = 512
TWO_PI = 2.0 * math.pi

MDT = BF16
DEBUG = False
STOP_AFTER = None
BISECT = None


class KB:
    NDMA = 24

    def __init__(self, nc, stack):
        self.nc = nc
        self.engs = {"pe": nc.tensor, "act": nc.scalar, "dve": nc.vector,
                     "pool": nc.gpsimd, "sp": nc.sync}
        self.sem = {}
        self.cnt = {}
        for n in self.engs:
            self.sem[n] = stack.enter_context(nc.semaphore("s_" + n))
            self.cnt[n] = 0
        self.dsem = [stack.enter_context(nc.semaphore("d%d" % i)) for i in range(self.NDMA)]
        self.dcnt = [0] * self.NDMA
        self.dnext = 0
        self.seen = {n: {} for n in self.engs}
        self.bufs = {}
        self.n_ops = 0

    def _semh(self, sk):
        return self.sem[sk] if isinstance(sk, str) else self.dsem[sk[1]]

    def _wait(self, eng, tok):
        sk, v = tok
        if sk == eng and eng in ("pe", "sp"):
            return
        if self.seen[eng].get(sk, 0) >= v:
            return
        self.engs[eng].wait_ge(self._semh(sk), v)
        self.seen[eng][sk] = v

    def _deps(self, eng, reads, writes):
        toks = {}

        def add(t):
            if t is not None and toks.get(t[0], 0) < t[1]:
                toks[t[0]] = t[1]
        for r in reads:
            st = self.bufs.get(r)
            if st:
                add(st[0])
        for w in writes:
            st = self.bufs.get(w)
            if st:
                add(st[0])
                for t in st[1]:
                    add(t)
        for sk, v in toks.items():
            self._wait(eng, (sk, v))

    def _record(self, tok, reads, writes):
        for r in reads:
            st = self.bufs.setdefault(r, [None, []])
            st[1] = [t for t in st[1] if t[0] != tok[0]] + [tok]
        for w in writes:
            self.bufs[w] = [tok, []]

    def op(self, eng, fn, reads=(), writes=()):
        reads, writes = list(reads), list(writes)
        for r in reads:
            if isinstance(r, tuple) and r[0] == "ps" and r not in writes:
                writes.append(r)
        self._deps(eng, reads, writes)
        ins = fn()
        self.cnt[eng] += 1
        ins.then_inc(self.sem[eng], 1)
        tok = (eng, self.cnt[eng])
        self._record(tok, reads, writes)
        self.n_ops += 1
        return tok

    def dma(self, q, out, in_, reads=(), writes=(), **kw):
        i = self.dnext
        self.dnext = (self.dnext + 1) % self.NDMA
        if self.dcnt[i]:
            self._wait(q, (("d", i), 16 * self.dcnt[i]))
        self._deps(q, reads, writes)
        ins = self.engs[q].dma_start(out=out, in_=in_, **kw)
        self.dcnt[i] += 1
        ins.then_inc(self.dsem[i], 16)
        tok = (("d", i), 16 * self.dcnt[i])
        self._record(tok, reads, writes)
        self.n_ops += 1
        return tok

    def barrier(self):
        for e in self.engs:
            for o in self.engs:
                if self.cnt[o]:
                    self._wait(e, (o, self.cnt[o]))
            for i in range(self.NDMA):
                if self.dcnt[i]:
                    self._wait(e, (("d", i), 16 * self.dcnt[i]))
        self.bufs = {}


class WStream:
    def __init__(self, k, bufs, srcs):
        self.k, self.bufs, self.srcs = k, bufs, srcs
        self.nb = len(bufs)
        self.issued = 0

    def get(self, i):
        while self.issued < len(self.srcs) and self.issued <= i + self.nb - 1:
            j = self.issued
            src = self.srcs[j]
            n = src.shape[1]
            dst = self.bufs[j % self.nb][:, 0:n]
            if n > 1024:
                dst = dst.rearrange("p (a b) -> p a b", a=2)
                src = src.rearrange("p (a b) -> p a b", a=2)
            self.k.dma("pool", dst, src, writes=[("wb", j % self.nb)])
            self.issued += 1
        return self.bufs[i % self.nb], ("wb", i % self.nb)


def build_program():
    nc = bass.Bass("TRN2", target_bir_lowering=False)
    dt_in = lambda name, shape, dt=F32: nc.dram_tensor(name, list(shape), dt, kind="ExternalInput").ap()
    dt_out = lambda name, shape, dt=F32: nc.dram_tensor(name, list(shape), dt, kind="ExternalOutput").ap()
    dt_scr = lambda name, shape, dt=F32: (nc.dram_tensor(name, list(shape), dt, kind="ExternalOutput").ap()
                                          if DEBUG else nc.dram_tensor(name, list(shape), dt).ap())

    x_in = dt_in("x_in", [T, D])
    condT = dt_in("condT", [128, NCH, 2])
    ck_d = dt_in("ck", [DEPTH, NH, PAST, HD])
    cv_d = dt_in("cv", [DEPTH, NH, PAST, HD])
    st0_d = dt_in("st0", [DEPTH, 128, 2, 16, 2])
    wmod_t = dt_in("wmod_t", [DEPTH, 24, 128, NCH * 512])
    bmodT = dt_in("bmodT", [DEPTH, 128, 96])
    gT = dt_in("gT", [128, 5, NCH])
    win_t = dt_in("win_t", [DEPTH, 32, 128, NCH * 128])
    wout_t = dt_in("wout_t", [DEPTH, 16, 128, NCH * 128])
    ffin_t = dt_in("ffin_t", [DEPTH, 88, 128, NCH * 128])
    ffout_t = dt_in("ffout_t", [DEPTH, NSPLIT, 16, 128, SCH * 128])
    rpbtab = dt_in("rpbtab", [DEPTH, NH, 64, 15 * 64])
    colmask = dt_in("colmask", [64, 64])
    ssm_a = dt_in("ssm_a", [DEPTH, 128, 3, 32])
    ssm_B = dt_in("ssm_B", [DEPTH, 128, 2, 32, 16])
    ssm_C = dt_in("ssm_C", [DEPTH, 2, 128, 32 * 128])
    vecs = dt_in("vecs", [DEPTH, 128, 16])
    wglu_t = dt_in("wglu_t", [DEPTH, 128, 4 * 1024])
    poolw_t = dt_in("poolw_t", [DEPTH, 128, 4 * 128])
    invcnt = dt_in("invcnt", [128, 4, 256 + TS])
    ident_d = dt_in("ident", [128, 128])

    y_out = dt_out("y", [T, D])
    nk_out = dt_out("nk", [4, DEPTH, NH, 256, HD])
    nv_out = dt_out("nv", [4, DEPTH, NH, 256, HD])
    nst_out = dt_out("nst", [4, DEPTH, 2, 32, 64, 2])

    XT = dt_scr("XT", [D, T])
    QT = dt_scr("QT", [1024, T], MDT)
    KT = dt_scr("KT", [1024, T], MDT)
    VTOK = dt_scr("VTOK", [T, 1024], MDT)
    UT = dt_scr("UT", [512, T])
    PT = dt_scr("PT", [512, T])
    MIX = dt_scr("MIX", [D, T], MDT)

    with ExitStack() as top:
        k = KB(nc, top)
        _uid = [0]

        def sb(name, shape, dt=F32, st=top):
            _uid[0] += 1
            return st.enter_context(nc.sbuf_tensor("sb%d_%s" % (_uid[0], name), list(shape), dt))
        PSB = [top.enter_context(nc.psum_tensor("psb%d" % i, [128, 1024], F32)) for i in range(4)]
        PS = [PSB[i // 2][:, (i % 2) * 512:(i % 2 + 1) * 512] for i in range(8)]
        psk = lambda i: ("ps", i)

        ident = sb("ident", [128, 128])
        ones = sb("ones", [128, 128])
        modv = sb("modv", [128, DEPTH, 6, NCH, 2])
        gsb = sb("gsb", [128, 5, NCH])
        k.dma("sp", ident[:], ident_d, writes=["ident"])
        k.dma("sp", gsb[:], gT, writes=["gsb"])
        k.op("dve", lambda: nc.vector.memset(ones[:], 1.0), writes=["ones"])

        def mm(out, pairs, reads, writes):
            def f():
                n = len(pairs)
                ins = None
                for i, (l, r) in enumerate(pairs):
                    ins = nc.tensor.matmul(out, l, r, start=(i == 0), stop=(i == n - 1))
                return ins
            return k.op("pe", f, reads, writes)

        def act(out, in_, func, reads, writes, **kw):
            return k.op("act", lambda: nc.scalar.activation(out, in_, func, **kw), reads, writes)

        def tt(eng, out, a, b, op, reads, writes):
            e = nc.vector if eng == "dve" else nc.gpsimd
            return k.op(eng, lambda: e.tensor_tensor(out, a, b, op), reads, writes)

        def ts(eng, out, a, s1, s2, op0, op1, reads, writes):
            e = nc.vector if eng == "dve" else nc.gpsimd
            if s2 is None:
                return k.op(eng, lambda: e.tensor_scalar(out, a, s1, None, op0), reads, writes)
            return k.op(eng, lambda: e.tensor_scalar(out, a, s1, s2, op0, op1), reads, writes)

        def stt(out, a, s, b, op0, op1, reads, writes):
            return k.op("dve", lambda: nc.vector.scalar_tensor_tensor(out, a, s, b, op0, op1), reads, writes)

        def cp(eng, out, in_, reads, writes):
            if eng == "act":
                return k.op("act", lambda: nc.scalar.copy(out, in_), reads, writes)
            e = nc.vector if eng == "dve" else nc.gpsimd
            return k.op(eng, lambda: e.tensor_copy(out, in_), reads, writes)

        def stop(name):
            return STOP_AFTER == name

        class Ctx:
            pass
        c = Ctx()
        c.nc, c.k, c.PS, c.PSB, c.ident, c.ones, c.sb = nc, k, PS, PSB, ident, ones, sb
        c.mm, c.act, c.tt, c.ts, c.stt, c.cp, c.psk = mm, act, tt, ts, stt, cp, psk

        with ExitStack() as ph:
            xt4 = [sb("p0_x%d" % i, [128, 4, D], F32, ph) for i in range(2)]
            stg = [sb("p0_s%d" % i, [128, 512], F32, ph) for i in range(4)]
            si = 0
            for g in range(T // 512):
                xb = xt4[g % 2]
                kx = ("p0x", g % 2)
                k.dma("sp", xb[:], x_in[g * 512:(g + 1) * 512, :].rearrange("(b p) f -> p b f", p=128),
                      writes=[kx])
                for fc in range(NCH):
                    bank = fc % 4
                    def f(bank=bank, fc=fc, xb=xb):
                        ins = None
                        for b in range(4):
                            ins = nc.tensor.transpose(PS[bank][:, b * 128:(b + 1) * 128],
                                                      xb[:, b, fc * 128:(fc + 1) * 128], ident[:])
                        return ins
                    k.op("pe", f, reads=[kx, "ident"], writes=[psk(bank)])
                    s = stg[si % 4]
                    ks = ("p0s", si % 4)
                    si += 1
                    cp("act" if fc % 2 else "dve", s[:], PS[bank][:], [psk(bank)], [ks])
                    k.dma("sp", XT[fc * 128:(fc + 1) * 128, g * 512:(g + 1) * 512], s[:],
                          reads=[ks], writes=[("XT", g // 2)])
        k.barrier()

        with ExitStack() as ph:
            cs = sb("m_cs", [128, NCH, 2], F32, ph)
            wb = [sb("m_w%d" % i, [128, NCH * 512], F32, ph) for i in range(2)]
            row = [sb("m_r%d" % i, [2, 512], F32, ph) for i in range(2)]
            bm = sb("m_b", [128, DEPTH, 96], F32, ph)
            modT = sb("m_modT", [128, DEPTH, 96, 2], F32, ph)
            k.dma("sp", cs[:], condT, writes=["cs"])
            k.dma("sp", bm[:], bmodT.rearrange("l p c -> p l c"), writes=["bm"])
            act(cs[:], cs[:], AF.Silu, ["cs"], ["cs"])
            it = 0
            for l in range(DEPTH):
                for n in range(24):
                    w = wb[it % 2]
                    kw = ("mw", it % 2)
                    k.dma("sp" if it % 2 else "act", w[:], wmod_t[l, n], writes=[kw])
                    wv = w[:].rearrange("p (k c) -> p k c", k=NCH)
                    bank = it % 2
                    mm(PS[bank][0:2, :], [(cs[:, kc, :], wv[:, kc, :]) for kc in range(NCH)],
                       [kw, "cs"], [psk(bank)])
                    r = row[it % 2]
                    kr = ("mr", it % 2)
                    cp("act", r[:], PS[bank][0:2, :], [psk(bank)], [kr])
                    tb = 2 + (it % 2)
                    def f(r=r, tb=tb):
                        ins = None
                        for j in range(4):
                            ins = nc.tensor.transpose(PS[tb][:, j * 2:(j + 1) * 2],
                                                      r[0:2, j * 128:(j + 1) * 128], ident[0:2, 0:2])
                        return ins
                    k.op("pe", f, reads=[kr, "ident"], writes=[psk(tb)])
                    pv = PS[tb][:, 0:8].rearrange("p (j c) -> p j c", c=2)
                    for cc in range(2):
                        tt("dve", modT[:, l, n * 4:(n + 1) * 4, cc], pv[:, :, cc],
                           bm[:, l, n * 4:(n + 1) * 4], ALU.add, [psk(tb), "bm"], ["modT"])
                    it += 1
            for l in range(DEPTH):
                for half in range(2):
                    o = half * 48
                    gi = l if half == 0 else 2 + l
                    for cc in range(2):
                        stt(modv[:, l, 3 * half + 0, :, cc], modT[:, l, o + 16:o + 32, cc], 1.0,
                            gsb[:, gi, :], ALU.add, ALU.mult, ["modT", "gsb"], ["modv"])
                        cp("dve", modv[:, l, 3 * half + 1, :, cc], modT[:, l, o:o + 16, cc], ["modT"], ["modv"])
                        cp("dve", modv[:, l, 3 * half + 2, :, cc], modT[:, l, o + 32:o + 48, cc], ["modT"], ["modv"])
        if DEBUG:
            dbg_modv = dt_out("dbg_modv", [128, DEPTH * 6 * NCH * 2])
            k.dma("sp", dbg_modv, modv[:].rearrange("p l s f c -> p (l s f c)"), reads=["modv"], writes=["dbg"])
        k.barrier()
        if stop("M"):
            return finish(nc, k)

        def rms_rstd(xT, kx, rstd, sq, ssb):
            for fc in range(NCH):
                s = sq[fc % 2]
                ksq = ("sq", fc % 2)
                if fc % 2 == 0:
                    act(s[:], xT[:, fc, :], AF.Square, [kx], [ksq])
                else:
                    tt("dve", s[:], xT[:, fc, :], xT[:, fc, :], ALU.mult, [kx], [ksq])
                for hb in range(TT // 512):
                    def f(hb=hb, s=s, fc=fc):
                        return nc.tensor.matmul(PS[ssb + hb][:], ones[:], s[:, hb * 512:(hb + 1) * 512],
                                                start=(fc == 0), stop=(fc == NCH - 1))
                    k.op("pe", f, reads=[ksq, "ones"], writes=[psk(ssb + hb)])
            for hb in range(TT // 512):
                sl = slice(hb * 512, (hb + 1) * 512)
                ts("dve", rstd[:, sl], PS[ssb + hb][:], 1.0 / D, EPS, ALU.mult, ALU.add,
                   [psk(ssb + hb)], ["rstd"])
                k.op("act", lambda sl=sl: nc.scalar.sqrt(rstd[:, sl], rstd[:, sl]), ["rstd"], ["rstd"])
                k.op("dve", lambda sl=sl: nc.vector.reciprocal(rstd[:, sl], rstd[:, sl]), ["rstd"], ["rstd"])

        for l in range(DEPTH):
            last = (l == DEPTH - 1)
            with ExitStack() as ph:
                xT = sb("a_xT", [128, NCH, TT], F32, ph)
                hT = sb("a_hT", [128, NCH, TT], MDT, ph)
                rstd = sb("a_rstd", [128, TT], F32, ph)
                sq = [sb("a_sq%d" % i, [128, TT], F32, ph) for i in range(2)]
                wbufs = [sb("a_w%d" % i, [128, NCH * 128], MDT, ph) for i in range(8)]
                sgm = [sb("a_sm%d" % i, [128, TT], MDT, ph) for i in range(2)]
                sgf = [sb("a_sf%d" % i, [128, TT], F32, ph) for i in range(2)]
                tmf = [sb("a_tf%d" % i, [128, 4, 128], F32, ph) for i in range(2)]
                tmm = [sb("a_tm%d" % i, [128, 4, 128], MDT, ph) for i in range(2)]
                ws = WStream(k, wbufs, [win_t[l, oc] for oc in range(32)] * NT)
                wi = 0
                n_sm = n_sf = n_tm = 0
                for t in range(NT):
                    cond = 0 if t == 0 else 1
                    tok0 = t * TT
                    for fc in range(NCH):
                        k.dma("sp", xT[:, fc, :], XT[fc * 128:(fc + 1) * 128, tok0:tok0 + TT],
                              reads=[("XT", t)], writes=["a_xT"])
                    rms_rstd(xT, "a_xT", rstd, sq, 4)
                    for fc in range(NCH):
                        s = sq[fc % 2]
                        ksq = ("sq", fc % 2)
                        tt("dve", s[:], xT[:, fc, :], rstd[:], ALU.mult, ["a_xT", "rstd"], [ksq])
                        act(hT[:, fc, :], s[:], AF.Identity, [ksq, "modv"], ["a_hT"],
                            scale=modv[:, l, 0, fc, cond:cond + 1], bias=modv[:, l, 1, fc, cond:cond + 1])
                    if BISECT == -1:
                        return finish(nc, k)
                    for oc in range(32):
                        if BISECT is not None and oc == BISECT % 100:
                            return finish(nc, k)
                        w, kw = ws.get(wi)
                        wi += 1
                        wv = w[:].rearrange("p (k c) -> p k c", k=NCH)
                        acc = 2 * (oc % 2)
                        for hb in range(TT // 512):
                            mm(PS[acc + hb][:],
                               [(wv[:, kc, :], hT[:, kc, hb * 512:(hb + 1) * 512]) for kc in range(NCH)],
                               [kw, "a_hT"], [psk(acc + hb)])
                        kind = ("q" if oc < 8 else "k" if oc < 16 else "v" if oc < 24
                                else "u" if oc < 28 else "p")
                        need_f32 = kind in ("u", "p", "v") or (kind == "k" and t == 0)
                        need_m = kind in ("q", "k")
                        pr = [psk(acc), psk(acc + 1)]
                        if need_m:
                            s = sgm[n_sm % 2]
                            ks = ("a_sm", n_sm % 2)
                            n_sm += 1
                            for hb in range(TT // 512):
                                sl = slice(hb * 512, (hb + 1) * 512)
                                if kind == "q":
                                    k.op("act", lambda s=s, sl=sl, hb=hb: nc.scalar.mul(s[:, sl], PS[acc + hb][:], 0.125),
                                         [psk(acc + hb)], [ks])
                                else:
                                    cp("act", s[:, sl], PS[acc + hb][:], [psk(acc + hb)], [ks])
                            dst = QT if kind == "q" else KT
                            r0 = (oc % 8) * 128
                            k.dma("sp", dst[r0:r0 + 128, tok0:tok0 + TT], s[:], reads=[ks],
                                  writes=[("QK", kind, t)])
                        if need_f32:
                            s = sgf[n_sf % 2]
                            ks = ("a_sf", n_sf % 2)
                            n_sf += 1
                            for hb in range(TT // 512):
                                sl = slice(hb * 512, (hb + 1) * 512)
                                cp("dve", s[:, sl], PS[acc + hb][:], [psk(acc + hb)], [ks])
                            if kind in ("u", "p"):
                                dst = UT if kind == "u" else PT
                                r0 = (oc % 4) * 128
                                k.dma("sp", dst[r0:r0 + 128, tok0:tok0 + TT], s[:], reads=[ks],
                                      writes=[("UP", kind, t)])
                            else:
                                ocl = oc % 8
                                for q4 in range(TT // 512):
                                    if BISECT is not None and BISECT >= 200:
                                        continue
                                    tb = 6 + (n_tm % 2)
                                    def f(s=s, q4=q4, tb=tb):
                                        ins = None
                                        for b in range(4):
                                            c0 = q4 * 512 + b * 128
                                            ins = nc.tensor.transpose(PS[tb][:, b * 128:(b + 1) * 128],
                                                                      s[:, c0:c0 + 128], ident[:])
                                        return ins
                                    k.op("pe", f, reads=[ks, "ident"], writes=[psk(tb)])
                                    pv = PS[tb][:].rearrange("p (b f) -> p b f", b=4)
                                    if kind == "v":
                                        m = tmm[n_tm % 2]
                                        km = ("a_tmm", n_tm % 2)
                                        cp("act", m[:], pv, [psk(tb)], [km])
                                        r0 = tok0 + q4 * 512
                                        k.dma("sp", VTOK[r0:r0 + 512, ocl * 128:(ocl + 1) * 128]
                                              .rearrange("(b p) f -> p b f", p=128), m[:],
                                              reads=[km], writes=[("VTOK", t)])
                                    if t == 0:
                                        m = tmf[n_tm % 2]
                                        km = ("a_tmf", n_tm % 2)
                                        cp("dve", m[:], pv, [psk(tb)], [km])
                                        dst = nk_out if kind == "k" else nv_out
                                        for sq_ in range(2):
                                            seq = q4 * 2 + sq_
                                            for hh in range(2):
                                                if BISECT is not None and BISECT >= 100:
                                                    continue
                                                k.dma("sp",
                                                      dst[seq, l, 2 * ocl + hh, :, :].rearrange("(b p) d -> p b d", p=128),
                                                      m[:, 2 * sq_:2 * sq_ + 2, hh * 64:(hh + 1) * 64],
                                                      reads=[km], writes=[("nkv", kind)])
                                    n_tm += 1
            k.barrier()
            if stop("A%d" % l):
                return finish(nc, k)

            phase_b_pool(c, l, PT, MIX, vecs, poolw_t, invcnt)
            k.barrier()
            if stop("Bp%d" % l):
                return finish(nc, k)
            phase_b_attn_prompt(c, l, QT, KT, VTOK, MIX)
            k.barrier()
            if stop("Ba%d" % l):
                return finish(nc, k)
            phase_b_attn_sample(c, l, QT, KT, VTOK, MIX, ck_d, cv_d, rpbtab, colmask)
            k.barrier()
            if stop("Bs%d" % l):
                return finish(nc, k)
            phase_b_ssm(c, l, UT, MIX, ssm_a, ssm_B, ssm_C, st0_d, vecs, wglu_t, nst_out)
            k.barrier()
            if stop("B%d" % l):
                return finish(nc, k)

            with ExitStack() as ph:
                xT = sb("c_xT", [128, NCH, TT], F32, ph)
                mT = sb("c_mT", [128, NCH, TT], MDT, ph)
                hid = sb("c_hid", [128, SCH, TT], MDT, ph)
                rstd = sb("c_rstd", [128, TT], F32, ph)
                sq = [sb("c_sq%d" % i, [128, TT], F32, ph) for i in range(2)]
                wbufs = [sb("c_w%d" % i, [128, NCH * 128], MDT, ph) for i in range(8)]
                srcs = []
                for t in range(NT):
                    srcs += [wout_t[l, oc] for oc in range(16)]
                    for sp_ in range(NSPLIT):
                        for j in range(SCH):
                            srcs += [ffin_t[l, sp_ * SCH + j], ffin_t[l, 44 + sp_ * SCH + j]]
                        srcs += [ffout_t[l, sp_, oc] for oc in range(16)]
                ws = WStream(k, wbufs, srcs)
                wi = 0
                for t in range(NT):
                    cond = 0 if t == 0 else 1
                    tok0 = t * TT
                    for fc in range(NCH):
                        k.dma("sp", xT[:, fc, :], XT[fc * 128:(fc + 1) * 128, tok0:tok0 + TT],
                              reads=[("XT", t)], writes=[("c_xT", fc)])
                        k.dma("sp", mT[:, fc, :], MIX[fc * 128:(fc + 1) * 128, tok0:tok0 + TT],
                              reads=[("MIX", t)], writes=["c_mT"])
                    for oc in range(16):
                        w, kw = ws.get(wi)
                        wi += 1
                        wv = w[:].rearrange("p (k c) -> p k c", k=NCH)
                        acc = 2 * (oc % 4)
                        for hb in range(TT // 512):
                            sl = slice(hb * 512, (hb + 1) * 512)
                            mm(PS[acc + hb][:], [(wv[:, kc, :], mT[:, kc, sl]) for kc in range(NCH)],
                               [kw, "c_mT"], [psk(acc + hb)])
                            stt(xT[:, oc, sl], PS[acc + hb][:], modv[:, l, 2, oc, cond:cond + 1], xT[:, oc, sl],
                                ALU.mult, ALU.add, [psk(acc + hb), ("c_xT", oc), "modv"], [("c_xT", oc)])
                    allx = [("c_xT", fc) for fc in range(NCH)]
                    for fc in range(NCH):
                        s = sq[fc % 2]
                        ksq = ("sq", fc % 2)
                        if fc % 2 == 0:
                            act(s[:], xT[:, fc, :], AF.Square, [("c_xT", fc)], [ksq])
                        else:
                            tt("dve", s[:], xT[:, fc, :], xT[:, fc, :], ALU.mult, [("c_xT", fc)], [ksq])
                        for hb in range(TT // 512):
                            def f(hb=hb, s=s, fc=fc):
                                return nc.tensor.matmul(PS[6 + hb][:], ones[:], s[:, hb * 512:(hb + 1) * 512],
                                                        start=(fc == 0), stop=(fc == NCH - 1))
                            k.op("pe", f, reads=[ksq, "ones"], writes=[psk(6 + hb)])
                    for hb in range(TT // 512):
                        sl = slice(hb * 512, (hb + 1) * 512)
                        ts("dve", rstd[:, sl], PS[6 + hb][:], 1.0 / D, EPS, ALU.mult, ALU.add,
                           [psk(6 + hb)], ["rstd"])
                        k.op("act", lambda sl=sl: nc.scalar.sqrt(rstd[:, sl], rstd[:, sl]), ["rstd"], ["rstd"])
                        k.op("dve", lambda sl=sl: nc.vector.reciprocal(rstd[:, sl], rstd[:, sl]), ["rstd"], ["rstd"])
                    for fc in range(NCH):
                        s = sq[fc % 2]
                        ksq = ("sq", fc % 2)
                        tt("dve", s[:], xT[:, fc, :], rstd[:], ALU.mult, [("c_xT", fc), "rstd"], [ksq])
                        act(mT[:, fc, :], s[:], AF.Identity, [ksq, "modv"], ["c_mT"],
                            scale=modv[:, l, 3, fc, cond:cond + 1], bias=modv[:, l, 4, fc, cond:cond + 1])
                    for sp_ in range(NSPLIT):
                        for j in range(SCH):
                            wg, kwg = ws.get(wi)
                            wgv = wg[:].rearrange("p (k c) -> p k c", k=NCH)
                            pa = 4 * (j % 2)
                            for hb in range(TT // 512):
                                sl = slice(hb * 512, (hb + 1) * 512)
                                mm(PS[pa + hb][:], [(wgv[:, kc, :], mT[:, kc, sl]) for kc in range(NCH)],
                                   [kwg, "c_mT"], [psk(pa + hb)])
                            wu, kwu = ws.get(wi + 1)
                            wi += 2
                            wuv = wu[:].rearrange("p (k c) -> p k c", k=NCH)
                            for hb in range(TT // 512):
                                sl = slice(hb * 512, (hb + 1) * 512)
                                mm(PS[pa + 2 + hb][:], [(wuv[:, kc, :], mT[:, kc, sl]) for kc in range(NCH)],
                                   [kwu, "c_mT"], [psk(pa + 2 + hb)])
                            for hb in range(TT // 512):
                                sl = slice(hb * 512, (hb + 1) * 512)
                                s = sq[hb]
                                ksq = ("sq", hb)
                                act(s[:, 0:512], PS[pa + hb][:], AF.Silu, [psk(pa + hb)], [ksq])
                                tt("dve", hid[:, j, sl], s[:, 0:512], PS[pa + 2 + hb][:], ALU.mult,
                                   [ksq, psk(pa + 2 + hb)], [("hid", j)])
                        hk = [("hid", j) for j in range(SCH)]
                        for oc in range(16):
                            w, kw = ws.get(wi)
                            wi += 1
                            wv = w[:, 0:SCH * 128].rearrange("p (k c) -> p k c", k=SCH)
                            acc = 2 * (oc % 4)
                            for hb in range(TT // 512):
                                sl = slice(hb * 512, (hb + 1) * 512)
                                mm(PS[acc + hb][:], [(wv[:, kc, :], hid[:, kc, sl]) for kc in range(SCH)],
                                   [kw] + hk, [psk(acc + hb)])
                                stt(xT[:, oc, sl], PS[acc + hb][:], modv[:, l, 5, oc, cond:cond + 1], xT[:, oc, sl],
                                    ALU.mult, ALU.add, [psk(acc + hb), ("c_xT", oc), "modv"], [("c_xT", oc)])
                    if not last:
                        for fc in range(NCH):
                            k.dma("sp", XT[fc * 128:(fc + 1) * 128, tok0:tok0 + TT], xT[:, fc, :],
                                  reads=[("c_xT", fc)], writes=[("XT", t)])
                    else:
                        for fc in range(NCH):
                            s = sq[fc % 2]
                            ksq = ("sq", fc % 2)
                            if fc % 2 == 0:
                                act(s[:], xT[:, fc, :], AF.Square, [("c_xT", fc)], [ksq])
                            else:
                                tt("dve", s[:], xT[:, fc, :], xT[:, fc, :], ALU.mult, [("c_xT", fc)], [ksq])
                            for hb in range(TT // 512):
                                def f(hb=hb, s=s, fc=fc):
                                    return nc.tensor.matmul(PS[6 + hb][:], ones[:], s[:, hb * 512:(hb + 1) * 512],
                                                            start=(fc == 0), stop=(fc == NCH - 1))
                                k.op("pe", f, reads=[ksq, "ones"], writes=[psk(6 + hb)])
                        for hb in range(TT // 512):
                            sl = slice(hb * 512, (hb + 1) * 512)
                            ts("dve", rstd[:, sl], PS[6 + hb][:], 1.0 / D, EPS, ALU.mult, ALU.add,
                               [psk(6 + hb)], ["rstd"])
                            k.op("act", lambda sl=sl: nc.scalar.sqrt(rstd[:, sl], rstd[:, sl]), ["rstd"], ["rstd"])
                            k.op("dve", lambda sl=sl: nc.vector.reciprocal(rstd[:, sl], rstd[:, sl]), ["rstd"], ["rstd"])
                        for fc in range(NCH):
                            stt(xT[:, fc, :], xT[:, fc, :], gsb[:, 4, fc:fc + 1], rstd[:], ALU.mult, ALU.mult,
                                [("c_xT", fc), "rstd", "gsb"], [("c_xT", fc)])
                        ytm = [sq[0], sq[1]]
                        it = 0
                        for b in range(TT // 128):
                            for q4 in range(4):
                                tb = (it % 2)
                                def f(b=b, q4=q4, tb=tb):
                                    ins = None
                                    for j in range(4):
                                        fc = q4 * 4 + j
                                        ins = nc.tensor.transpose(PS[tb][:, j * 128:(j + 1) * 128],
                                                                  xT[:, fc, b * 128:(b + 1) * 128], ident[:])
                                    return ins
                                k.op("pe", f, reads=[("c_xT", q4 * 4 + j) for j in range(4)] + ["ident"],
                                     writes=[psk(tb)])
                                s = ytm[it % 2]
                                ksq = ("sq", it % 2)
                                cp("act" if it % 2 else "dve", s[:, 0:512], PS[tb][:], [psk(tb)], [ksq])
                                r0 = tok0 + b * 128
                                k.dma("sp", y_out[r0:r0 + 128, q4 * 512:(q4 + 1) * 512], s[:, 0:512],
                                      reads=[ksq], writes=["y"])
                                it += 1
            k.barrier()
            if stop("C%d" % l):
                return finish(nc, k)
        return finish(nc, k)


def finish(nc, k, **kw):
    k.barrier()
    return nc


ITEMS = [(s * 256, 256, s) for s in range(4)] + [(TP, TS, None)]


def phase_b_pool(c, l, PT, MIX, vecs, poolw_t, invcnt):
    nc, k = c.nc, c.k
    PS, psk = c.PS, c.psk
    with ExitStack() as ph:
        sb = lambda n, s, d=F32: c.sb(n, s, d, ph)
        L = TS + 16
        pp = sb("bp_pp", [128, L])
        sa = sb("bp_sa", [128, L])
        sbb = sb("bp_sb", [128, L])
        mix = sb("bp_mix", [128, TS])
        icn = sb("bp_icn", [128, 4, 256 + TS])
        pw = sb("bp_pw", [128, 4, 128])
        vec = sb("bp_vec", [128, 16])
        stg = [sb("bp_st%d" % i, [128, 512], MDT) for i in range(2)]
        k.dma("sp", icn[:], invcnt, writes=["icn"])
        k.dma("sp", pw[:], poolw_t[l].rearrange("p (g d) -> p g d", g=4), writes=["pw"])
        k.dma("sp", vec[:], vecs[l], writes=["vec"])
        k.op("dve", lambda: nc.vector.memset(pp[:, 0:8], 0.0), writes=["pp"])
        n_st = 0
        for g in range(4):
            w = 2 << g
            for (off, n, pidx) in ITEMS:
                Ln = n + 16
                ioff = 0 if pidx is not None else 256
                k.op("dve", lambda: nc.vector.memset(pp[:, 8 + n:16 + n], 0.0), writes=["pp"])
                k.dma("sp", pp[:, 8:8 + n], PT[g * 128:(g + 1) * 128, off:off + n], writes=["pp"])
                c.tt("dve", sa[:, 0:Ln - 1], pp[:, 0:Ln - 1], pp[:, 1:Ln], ALU.add, ["pp"], ["sa"])
                src, ks, start = sa, "sa", 7
                if w >= 4:
                    c.tt("dve", sbb[:, 0:Ln - 3], sa[:, 0:Ln - 3], sa[:, 2:Ln - 1], ALU.add, ["sa"], ["sb"])
                    src, ks, start = sbb, "sb", 6
                if w >= 8:
                    c.tt("dve", sa[:, 0:Ln - 7], sbb[:, 0:Ln - 7], sbb[:, 4:Ln - 3], ALU.add, ["sb"], ["sa"])
                    src, ks, start = sa, "sa", 4
                if w >= 16:
                    c.tt("dve", sbb[:, 0:Ln - 15], sa[:, 0:Ln - 15], sa[:, 8:Ln - 7], ALU.add, ["sa"], ["sb"])
                    src, ks, start = sbb, "sb", 0
                c.tt("dve", mix[:, 0:n], src[:, start:start + n], icn[:, g, ioff:ioff + n], ALU.mult,
                     [ks, "icn"], ["mix"])
                c.tt("dve", mix[:, 0:n], mix[:, 0:n], pp[:, 8:8 + n], ALU.subtract, ["mix", "pp"], ["mix"])
                for c0 in range(0, n, 512):
                    cw = min(512, n - c0)
                    bank = n_st % 2
                    c.mm(PS[bank][:, 0:cw], [(pw[:, g, :], mix[:, c0:c0 + cw])], ["pw", "mix"], [psk(bank)])
                    s = stg[n_st % 2]
                    kst = ("bp_st", n_st % 2)
                    n_st += 1
                    c.act(s[:, 0:cw], PS[bank][:, 0:cw], AF.Copy, [psk(bank), "vec"], [kst],
                          scale=vec[:, 12 + g:13 + g])
                    k.dma("sp", MIX[1536 + g * 128:1536 + (g + 1) * 128, off + c0:off + c0 + cw], s[:, 0:cw],
                          reads=[kst], writes=[("MIX", "p")])


def phase_b_attn_prompt(c, l, QT, KT, VTOK, MIX):
    nc, k = c.nc, c.k
    PS, psk, ident = c.PS, c.psk, c.ident
    with ExitStack() as ph:
        sb = lambda n, s, d=F32: c.sb(n, s, d, ph)
        qt = [sb("pa_q%d" % i, [128, 256], MDT) for i in range(2)]
        kt = [sb("pa_k%d" % i, [128, 256], MDT) for i in range(2)]
        vt = [sb("pa_v%d" % i, [128, 2, 128], MDT) for i in range(2)]
        pb = [sb("pa_p%d" % i, [128, 256]) for i in range(2)]
        ptb = [sb("pa_pt%d" % i, [128, 2, 128], MDT) for i in range(2)]
        st = [sb("pa_st%d" % i, [128, 4]) for i in range(2)]
        og = [sb("pa_o%d" % i, [128, 256], MDT) for i in range(2)]
        it = 0
        n_in = 0
        for s in range(4):
            for j in range(8):
                b = n_in % 2
                n_in += 1
                q, kk, v = qt[b], kt[b], vt[b]
                kq, kkk, kv = ("pa_q", b), ("pa_k", b), ("pa_v", b)
                k.dma("sp", q[:], QT[j * 128:(j + 1) * 128, s * 256:(s + 1) * 256], writes=[kq])
                k.dma("sp", kk[:], KT[j * 128:(j + 1) * 128, s * 256:(s + 1) * 256], writes=[kkk])
                k.dma("sp", v[:], VTOK[s * 256:(s + 1) * 256, j * 128:(j + 1) * 128]
                      .rearrange("(b p) f -> p b f", p=128), writes=[kv])
                ob = 4 + b
                for hh in range(2):
                    hs = slice(64 * hh, 64 * hh + 64)
                    for qb in range(2):
                        i2 = it % 2
                        it += 1
                        sbk = i2
                        tbk = 2 + i2
                        c.mm(PS[sbk][:, 0:256], [(q[hs, qb * 128:(qb + 1) * 128], kk[hs, :])], [kq, kkk], [psk(sbk)])
                        stt_ = st[i2]
                        kst = ("pa_st", i2)
                        k.op("dve", lambda: nc.vector.reduce_max(stt_[:, 0:1], PS[sbk][:, 0:256], AX.X),
                             [psk(sbk)], [kst])
                        c.ts("dve", stt_[:, 1:2], stt_[:, 0:1], -1.0, None, ALU.mult, None, [kst], [kst])
                        p = pb[i2]
                        kp = ("pa_p", i2)
                        c.act(p[:], PS[sbk][:, 0:256], AF.Exp, [psk(sbk), kst], [kp, kst],
                              bias=stt_[:, 1:2], accum_out=stt_[:, 2:3])
                        k.op("dve", lambda: nc.vector.reciprocal(stt_[:, 3:4], stt_[:, 2:3]), [kst], [kst])
                        c.ts("dve", p[:], p[:], stt_[:, 3:4], None, ALU.mult, None, [kp, kst], [kp])

                        def f():
                            ins = None
                            for kb in range(2):
                                ins = nc.tensor.transpose(PS[tbk][:, kb * 128:(kb + 1) * 128],
                                                          p[:, kb * 128:(kb + 1) * 128], ident[:])
                            return ins
                        k.op("pe", f, [kp, "ident"], [psk(tbk)])
                        pt = ptb[i2]
                        kpt = ("pa_pt", i2)
                        c.cp("act", pt[:], PS[tbk][:, 0:256].rearrange("p (b q) -> p b q", b=2), [psk(tbk)], [kpt])
                        c.mm(PS[ob][hs, qb * 128:(qb + 1) * 128],
                             [(v[:, kb, hs], pt[:, kb, :]) for kb in range(2)], [kv, kpt], [psk(ob)])
                o = og[b]
                ko = ("pa_o", b)
                c.cp("dve", o[:], PS[ob][:, 0:256], [psk(ob)], [ko])
                k.dma("sp", MIX[j * 128:(j + 1) * 128, s * 256:(s + 1) * 256], o[:], reads=[ko],
                      writes=[("MIX", "a")])


def phase_b_attn_sample(c, l, QT, KT, VTOK, MIX, ck_d, cv_d, rpbtab, colmask):
    nc, k = c.nc, c.k
    PS, psk, ident = c.PS, c.psk, c.ident
    NR = TS // GW
    with ExitStack() as ph:
        sb = lambda n, s, d=F32: c.sb(n, s, d, ph)
        qt = sb("sa_q", [128, TS], MDT)
        kt = sb("sa_k", [128, TS], MDT)
        vt = sb("sa_v", [128, 16, 128], MDT)
        vt2 = sb("sa_v2", [128, 15, 128], MDT)
        ckt = sb("sa_ck", [128, 4, 128])
        kcT = sb("sa_kcT", [128, PAST], MDT)
        vc = sb("sa_vc", [128, 4, 128], MDT)
        tab = sb("sa_tab", [128, 15, 64])
        thi = sb("sa_thi", [128, 15 * 64], MDT)
        tlo = sb("sa_tlo", [128, 15 * 64], MDT)
        cm = sb("sa_cm", [128, 64])
        idm = sb("sa_idm", [128, 128], MDT)
        pb = [sb("sa_p%d" % i, [64, 1024]) for i in range(2)]
        ptb = [sb("sa_pt%d" % i, [128, 8, 64], MDT) for i in range(2)]
        st = [sb("sa_st%d" % i, [64, 4]) for i in range(2)]
        osb = [sb("sa_os%d" % i, [64, 128]) for i in range(2)]
        og = [sb("sa_o%d" % i, [128, 512], MDT) for i in range(2)]
        k.dma("sp", cm[0:64, :], colmask, writes=["cm"])
        k.dma("sp", cm[64:128, :], colmask, writes=["cm"])
        c.cp("dve", idm[:], ident[:], ["ident"], ["idm"])
        it = 0
        n_og = 0
        for j in range(8):
            k.dma("sp", qt[:], QT[j * 128:(j + 1) * 128, TP:T], writes=["sa_q"])
            k.dma("sp", kt[:], KT[j * 128:(j + 1) * 128, TP:T], writes=["sa_k"])
            k.dma("sp", vt[:], VTOK[TP:T, j * 128:(j + 1) * 128].rearrange("(b p) f -> p b f", p=128),
                  writes=["sa_v"])
            k.dma("sp", vt2[:], VTOK[TP + 64:T - 64, j * 128:(j + 1) * 128].rearrange("(b p) f -> p b f", p=128),
                  writes=["sa_v"])
            for hh in range(2):
                k.dma("sp", ckt[:, :, 64 * hh:64 * hh + 64],
                      ck_d[l, 2 * j + hh].rearrange("(b p) d -> p b d", p=128), writes=["sa_ck"])
                k.dma("pool", vc[:, :, 64 * hh:64 * hh + 64],
                      cv_d[l, 2 * j + hh].rearrange("(b p) d -> p b d", p=128), writes=["sa_vc"])
                k.dma("sp", tab[64 * hh:64 * hh + 64, :, :],
                      rpbtab[l, 2 * j + hh].rearrange("q (r c) -> q r c", r=15), writes=["sa_tab"])

            def f():
                ins = None
                for b in range(4):
                    ins = nc.tensor.transpose(PS[7][:, b * 128:(b + 1) * 128], ckt[:, b, :], ident[:])
                return ins
            k.op("pe", f, ["sa_ck", "ident"], [psk(7)])
            c.cp("act", kcT[:], PS[7][:], [psk(7)], ["sa_kcT"])
            for dr in range(15):
                c.tt("dve", tab[:, dr, :], tab[:, dr, :], cm[:], ALU.add, ["sa_tab", "cm"], ["sa_tab"])
            tabf = tab[:].rearrange("p r c -> p (r c)")
            c.cp("dve", thi[:], tabf, ["sa_tab"], ["sa_thi"])
            c.tt("dve", tabf, tabf, thi[:], ALU.subtract, ["sa_tab", "sa_thi"], ["sa_tab"])
            c.cp("dve", tlo[:], tabf, ["sa_tab"], ["sa_tlo"])
            obk = 6
            items = [(r, hh) for r in range(NR) for hh in range(2)]
            ctxs = {}

            def p1(idx):
                r, hh = items[idx]
                rs = min(max(r - 4, 0), NR - 8)
                d0 = rs - r + 7
                hs = slice(64 * hh, 64 * hh + 64)
                i2 = idx % 2
                sA, sB = 2 * i2, 2 * i2 + 1
                qs = qt[hs, r * 64:(r + 1) * 64]
                c.mm(PS[sA][0:64, :], [(qs, kt[hs, rs * 64:(rs + 8) * 64]),
                                        (idm[hs, hs], thi[hs, d0 * 64:(d0 + 8) * 64]),
                                        (idm[hs, hs], tlo[hs, d0 * 64:(d0 + 8) * 64])],
                     ["sa_q", "sa_k", "sa_thi", "sa_tlo", "idm"], [psk(sA)])
                c.mm(PS[sB][0:64, :], [(qs, kcT[hs, :])], ["sa_q", "sa_kcT"], [psk(sB)])
                S = c.PSB[i2][0:64, :]
                s_ = st[i2]
                kst = ("sa_st", i2)
                k.op("dve", lambda: nc.vector.reduce_max(s_[:, 0:1], S, AX.X), [psk(sA), psk(sB)], [kst])
                c.ts("dve", s_[:, 1:2], s_[:, 0:1], -1.0, None, ALU.mult, None, [kst], [kst])
                p = pb[i2]
                kp = ("sa_p", i2)
                c.act(p[:], S, AF.Exp, [psk(sA), psk(sB), kst], [kp, kst],
                      bias=s_[:, 1:2], accum_out=s_[:, 2:3])
                k.op("dve", lambda: nc.vector.reciprocal(s_[:, 3:4], s_[:, 2:3]), [kst], [kst])

            def p2(idx):
                nonlocal n_og
                r, hh = items[idx]
                rs = min(max(r - 4, 0), NR - 8)
                hs = slice(64 * hh, 64 * hh + 64)
                i2 = idx % 2
                tbk = 4 + i2
                s_ = st[i2]
                kst = ("sa_st", i2)
                p = pb[i2]
                kp = ("sa_p", i2)

                def f():
                    ins = None
                    for kb in range(8):
                        ins = nc.tensor.transpose(PS[tbk][:, kb * 64:(kb + 1) * 64],
                                                  p[:, kb * 128:(kb + 1) * 128], ident[0:64, 0:64])
                    return ins
                k.op("pe", f, [kp, "ident"], [psk(tbk)])
                pt = ptb[i2]
                kpt = ("sa_pt", i2)
                c.cp("dve" if i2 else "act", pt[:], PS[tbk][:].rearrange("p (b q) -> p b q", b=8),
                     [psk(tbk)], [kpt])
                vloc = vt if rs % 2 == 0 else vt2
                pairs = [(pt[:, kb, :], vloc[:, rs // 2 + kb, hs]) for kb in range(4)]
                pairs += [(pt[:, 4 + kb, :], vc[:, kb, hs]) for kb in range(4)]
                c.mm(PS[7][0:64, hh * 64:(hh + 1) * 64], pairs, [kpt, "sa_v", "sa_vc"], [psk(7)])
                o = osb[r % 2]
                ko = ("sa_os", r % 2)
                c.act(o[:, hh * 64:(hh + 1) * 64], PS[7][0:64, hh * 64:(hh + 1) * 64], AF.Copy,
                      [psk(7), kst], [ko], scale=s_[:, 3:4])
                if hh == 1:
                    cc = (r % 8) * 64
                    k.op("pe", lambda: nc.tensor.transpose(PS[obk][:, cc:cc + 64], o[:], ident[0:64, 0:64]),
                         [ko, "ident"], [psk(obk)])
                    if r % 8 == 7:
                        r0 = r - 7
                        ogt = og[n_og % 2]
                        kog = ("sa_o", n_og % 2)
                        n_og += 1
                        c.cp("act", ogt[:], PS[obk][:], [psk(obk)], [kog])
                        k.dma("sp", MIX[j * 128:(j + 1) * 128, TP + r0 * 64:TP + (r0 + 8) * 64], ogt[:],
                              reads=[kog], writes=[("MIX", "a")])

            p1(0)
            for idx in range(1, len(items)):
                p1(idx)
                p2(idx - 1)
            p2(len(items) - 1)


def phase_b_ssm(c, l, UT, MIX, ssm_a, ssm_B, ssm_C, st0_d, vecs, wglu_t, nst_out):
    nc, k = c.nc, c.k
    PS, psk, ident = c.PS, c.psk, c.ident
    PI = math.pi
    C1 = 6.28125
    C2 = TWO_PI - C1
    with ExitStack() as ph:
        sb = lambda n, s, d=F32: c.sb(n, s, d, ph)
        utc = sb("ss_ut", [128, T])
        yTc = sb("ss_yT", [128, T])
        Bl = sb("ss_Bl", [128, 32, 2, 128])
        Cl = sb("ss_Cl", [128, 2, 32, 128])
        pa = sb("ss_pa", [128, 3, 32])
        Bst = sb("ss_Bst", [128, 2, 32, 16])
        Bb = sb("ss_Bb", [128, 2, 32, 16])
        h0 = sb("ss_h0", [128, 2, 16, 2])
        sm = sb("ss_sm", [128, 24, 32])
        zp = [sb("ss_zp%d" % i, [128, 128]) for i in range(8)]
        iot = sb("ss_iota", [128, LC])
        iot_i = sb("ss_iotai", [128, LC], I32)
        onesr = sb("ss_ones", [128, LC])
        ki = sb("ss_ki", [128, LC], I32)
        wk = [sb("ss_wk%d" % i, [128, LC]) for i in range(4)]
        cosT = sb("ss_cos", [128, LC])
        sinT = sb("ss_sin", [128, LC])
        magT = sb("ss_mag", [128, LC])
        fin = sb("ss_fin", [128, 4, 2, 16, 2])
        vec = sb("ss_vec", [128, 16])
        tmp = {}
        for nm in ("t1", "t2", "t3", "t4", "gr", "gi", "sr", "si", "hr", "hi"):
            tmp[nm] = [sb("ss_%s%d" % (nm, i), [128, LC]) for i in range(2)]
        ini = [sb("ss_ini%d" % i, [128, 4]) for i in range(2)]

        DT_, AR, TH, MAG, COS, SIN, ABR, ABI, NR_, NI_, DEN, CR, CI, RR, RI, I0R, I0I, W0, W1, W2, W3 = range(21)

        k.dma("sp", pa[:], ssm_a[l], writes=["pa"])
        k.dma("sp", Bst[:], ssm_B[l], writes=["Bst"])
        for ri in range(2):
            k.dma("sp", Cl[:, ri], ssm_C[l, ri].rearrange("p (b c) -> p b c", b=32), writes=["Cl"])
        k.dma("sp", h0[:], st0_d[l], writes=["h0"])
        k.dma("sp", vec[:], vecs[l], writes=["vec"])
        for z in zp:
            k.op("pool", lambda: nc.gpsimd.memset(z[:], 0.0), writes=["zp"])
        k.op("pool", lambda: nc.gpsimd.memset(onesr[:], 1.0), writes=["onesr"])
        k.op("pool", lambda: nc.gpsimd.iota(iot_i[:], pattern=[[1, LC]], base=0, channel_multiplier=0),
             writes=["iot_i"])
        c.cp("dve", iot[:], iot_i[:], ["iot_i"], ["iot"])
        c.ts("dve", Cl[:, 1], Cl[:, 1], -1.0, None, ALU.mult, None, ["Cl"], ["Cl"])

        def sincos(phi, kphi, n, s_out, c_out, kout):
            for which, out in ((0, s_out), (1, c_out)):
                a, b = wk[0], wk[1]
                if which == 0:
                    c.cp("dve", a[:, 0:n], phi, [kphi], ["wk0"])
                else:
                    c.ts("dve", a[:, 0:n], phi, PI / 2, None, ALU.add, None, [kphi], ["wk0"])
                c.ts("dve", b[:, 0:n], a[:, 0:n], 1.0 / TWO_PI, None, ALU.mult, None, ["wk0"], ["wk1"])
                c.cp("dve", ki[:, 0:n], b[:, 0:n], ["wk1"], ["ki"])
                c.cp("dve", b[:, 0:n], ki[:, 0:n], ["ki"], ["wk1"])
                c.stt(a[:, 0:n], b[:, 0:n], -C1, a[:, 0:n], ALU.mult, ALU.add, ["wk0", "wk1"], ["wk0"])
                c.stt(a[:, 0:n], b[:, 0:n], -C2, a[:, 0:n], ALU.mult, ALU.add, ["wk0", "wk1"], ["wk0"])
                c.ts("dve", b[:, 0:n], a[:, 0:n], PI, -TWO_PI, ALU.is_gt, ALU.mult, ["wk0"], ["wk1"])
                c.tt("dve", a[:, 0:n], a[:, 0:n], b[:, 0:n], ALU.add, ["wk0", "wk1"], ["wk0"])
                c.ts("dve", b[:, 0:n], a[:, 0:n], -PI, TWO_PI, ALU.is_lt, ALU.mult, ["wk0"], ["wk1"])
                c.tt("dve", a[:, 0:n], a[:, 0:n], b[:, 0:n], ALU.add, ["wk0", "wk1"], ["wk0"])
                c.ts("dve", a[:, 0:n], a[:, 0:n], PI, -PI, ALU.min, ALU.max, ["wk0"], ["wk0"])
                c.act(out, a[:, 0:n], AF.Sin, ["wk0"], [kout])

        S = lambda i: sm[:, i, :]
        c.act(S(DT_), pa[:, 2, :], AF.Exp, ["pa"], ["sm"])
        c.tt("dve", S(AR), pa[:, 0, :], S(DT_), ALU.mult, ["pa", "sm"], ["sm"])
        c.tt("dve", S(TH), pa[:, 1, :], S(DT_), ALU.mult, ["pa", "sm"], ["sm"])
        c.act(S(MAG), S(AR), AF.Exp, ["sm"], ["sm"])
        sincos(S(TH), "sm", 32, S(SIN), S(COS), "sm")
        c.tt("dve", S(ABR), S(MAG), S(COS), ALU.mult, ["sm"], ["sm"])
        c.tt("dve", S(ABI), S(MAG), S(SIN), ALU.mult, ["sm"], ["sm"])
        c.ts("dve", S(W0), S(ABR), -1.0, None, ALU.add, None, ["sm"], ["sm"])
        c.tt("dve", S(NR_), S(W0), pa[:, 0, :], ALU.mult, ["sm", "pa"], ["sm"])
        c.tt("dve", S(W1), S(ABI), pa[:, 1, :], ALU.mult, ["sm", "pa"], ["sm"])
        c.tt("dve", S(NR_), S(NR_), S(W1), ALU.add, ["sm"], ["sm"])
        c.tt("dve", S(NI_), S(ABI), pa[:, 0, :], ALU.mult, ["sm", "pa"], ["sm"])
        c.tt("dve", S(W1), S(W0), pa[:, 1, :], ALU.mult, ["sm", "pa"], ["sm"])
        c.tt("dve", S(NI_), S(NI_), S(W1), ALU.subtract, ["sm"], ["sm"])
        c.tt("dve", S(DEN), pa[:, 0, :], pa[:, 0, :], ALU.mult, ["pa"], ["sm"])
        c.tt("dve", S(W1), pa[:, 1, :], pa[:, 1, :], ALU.mult, ["pa"], ["sm"])
        c.tt("dve", S(DEN), S(DEN), S(W1), ALU.add, ["sm"], ["sm"])
        k.op("dve", lambda: nc.vector.reciprocal(S(DEN), S(DEN)), ["sm"], ["sm"])
        c.tt("dve", S(CR), S(NR_), S(DEN), ALU.mult, ["sm"], ["sm"])
        c.tt("dve", S(CI), S(NI_), S(DEN), ALU.mult, ["sm"], ["sm"])
        c.ts("dve", S(W2), S(TH), float(LC), None, ALU.mult, None, ["sm"], ["sm"])
        sincos(S(W2), "sm", 32, S(RI), S(RR), "sm")
        h0r = h0[:, :, :, 0]
        h0i = h0[:, :, :, 1]
        v3 = lambda i: sm[:, i, :].rearrange("p (d b) -> p d b", d=2)
        c.tt("dve", v3(I0R), v3(COS), h0r, ALU.mult, ["sm", "h0"], ["sm"])
        c.tt("dve", v3(W3), v3(SIN), h0i, ALU.mult, ["sm", "h0"], ["sm"])
        c.tt("dve", S(I0R), S(I0R), S(W3), ALU.subtract, ["sm"], ["sm"])
        c.tt("dve", v3(I0I), v3(COS), h0i, ALU.mult, ["sm", "h0"], ["sm"])
        c.tt("dve", v3(W3), v3(SIN), h0r, ALU.mult, ["sm", "h0"], ["sm"])
        c.tt("dve", S(I0I), S(I0I), S(W3), ALU.add, ["sm"], ["sm"])
        for db in range(32):
            cr, ci = sm[:, CR, db:db + 1], sm[:, CI, db:db + 1]
            c.ts("dve", Bb[:, 0, db, :], Bst[:, 1, db, :], ci, None, ALU.mult, None, ["Bst", "sm"], ["Bb"])
            c.stt(Bb[:, 0, db, :], Bst[:, 0, db, :], cr, Bb[:, 0, db, :], ALU.mult, ALU.subtract,
                  ["Bst", "sm", "Bb"], ["Bb"])
            c.ts("dve", Bb[:, 1, db, :], Bst[:, 1, db, :], cr, None, ALU.mult, None, ["Bst", "sm"], ["Bb"])
            c.stt(Bb[:, 1, db, :], Bst[:, 0, db, :], ci, Bb[:, 1, db, :], ALU.mult, ALU.add,
                  ["Bst", "sm", "Bb"], ["Bb"])
        n_t = 0
        for db in range(32):
            blk = db % 16
            for ri in range(2):
                z = zp[(blk % 4) * 2 + ri]
                kz = ("zp", (blk % 4) * 2 + ri)
                for gl in range(2):
                    c0 = (blk % 4) * 32 + gl * 16
                    c.cp("dve", z[64 * gl:64 * gl + 64, c0:c0 + 16], Bb[64 * gl:64 * gl + 64, ri, db, :],
                         ["Bb"], [kz])
                bank = n_t % 2
                n_t += 1
                k.op("pe", lambda: nc.tensor.transpose(PS[bank][:, 0:128], z[:], ident[:]), [kz, "ident"], [psk(bank)])
                c.cp("act", Bl[:, db, ri, :], PS[bank][:, 0:128], [psk(bank)], ["Bl"])

        it = 0
        GC = 2.0 * math.sqrt(2.0 / math.pi)
        for db_i in range(32):
            cch, d, b4 = db_i // 8, (db_i // 4) % 2, db_i % 4
            blk = 4 * cch + b4
            db = d * 16 + blk
            rev = (d == 1)
            if db_i % 8 == 0:
                k.dma("sp", utc[:], UT[cch * 128:(cch + 1) * 128, :], reads=[("UTd", cch)], writes=["ut"])
                k.op("pool", lambda: nc.gpsimd.memset(yTc[:], 0.0), writes=["yT"])
            c.ts("dve", wk[2][:], iot[:], sm[:, TH, db:db + 1], None, ALU.mult, None, ["iot", "sm"], ["wk2"])
            sincos(wk[2][:], "wk2", LC, sinT[:], cosT[:], "tab")
            c.ts("dve", magT[:], onesr[:], sm[:, MAG, db:db + 1], None, ALU.mult, None, ["onesr", "sm"], ["magT"])
            chunks = [(s * 256, 256, s, True) for s in range(4)]
            sc = [(TP + cix * LC, LC, None, cix == 0) for cix in range(TS // LC)]
            if rev:
                sc = [(TP + cix * LC, LC, None, cix == TS // LC - 1) for cix in reversed(range(TS // LC))]
            chunks += sc
            state = {"prev": None, "prev_k": None}
            base = it
            it += len(chunks)

            def common(ci):
                off, n, pidx, first = chunks[ci]
                i2 = (base + ci) % 2
                T_ = {nm: tmp[nm][i2] for nm in tmp}
                K_ = {nm: ("ss_" + nm, i2) for nm in tmp}
                rv = (lambda ap: ap[:, ::-1]) if rev else (lambda ap: ap)
                return off, n, pidx, first, i2, T_, K_, rv, 2 * i2, 2 * i2 + 1, 4 + i2

            def s1(ci):
                off, n, pidx, first, i2, T_, K_, rv, bR, bI, bY = common(ci)
                ucols = utc[:, off:off + n]
                c.mm(PS[bR][:, 0:n], [(Bl[:, db, 0, :], ucols)], ["Bl", "ut"], [psk(bR)])
                c.mm(PS[bI][:, 0:n], [(Bl[:, db, 1, :], ucols)], ["Bl", "ut"], [psk(bI)])
                bre, bim = rv(PS[bR][:, 0:n]), rv(PS[bI][:, 0:n])
                cs_, sn_ = cosT[:, 0:n], sinT[:, 0:n]
                c.tt("dve", T_["t1"][:, 0:n], bre, cs_, ALU.mult, [psk(bR), "tab"], [K_["t1"]])
                c.tt("dve", T_["t2"][:, 0:n], bim, sn_, ALU.mult, [psk(bI), "tab"], [K_["t2"]])
                c.tt("dve", T_["t3"][:, 0:n], bim, cs_, ALU.mult, [psk(bI), "tab"], [K_["t3"]])
                c.tt("dve", T_["t4"][:, 0:n], bre, sn_, ALU.mult, [psk(bR), "tab"], [K_["t4"]])
                c.tt("pool", T_["gr"][:, 0:n], T_["t1"][:, 0:n], T_["t2"][:, 0:n], ALU.add,
                     [K_["t1"], K_["t2"]], [K_["gr"]])
                c.tt("pool", T_["gi"][:, 0:n], T_["t3"][:, 0:n], T_["t4"][:, 0:n], ALU.subtract,
                     [K_["t3"], K_["t4"]], [K_["gi"]])

            def s2(ci):
                off, n, pidx, first, i2, T_, K_, rv, bR, bI, bY = common(ci)
                cs_, sn_ = cosT[:, 0:n], sinT[:, 0:n]
                if pidx is not None:
                    ir, ii_ = 0.0, 0.0
                    kin = []
                elif first:
                    ir, ii_ = sm[:, I0R, db:db + 1], sm[:, I0I, db:db + 1]
                    kin = ["sm"]
                else:
                    pr, pi_, pn = state["prev"]
                    prev_k = state["prev_k"]
                    iv = ini[i2]
                    kin = [("ss_ini", i2)]
                    rr, ri_ = sm[:, RR, db:db + 1], sm[:, RI, db:db + 1]
                    c.ts("dve", iv[:, 0:1], pi_[:, pn - 1:pn], ri_, None, ALU.mult, None, [prev_k[1], "sm"], kin)
                    c.stt(iv[:, 1:2], pr[:, pn - 1:pn], rr, iv[:, 0:1], ALU.mult, ALU.subtract,
                          [prev_k[0], "sm"] + kin, kin)
                    c.ts("dve", iv[:, 2:3], pi_[:, pn - 1:pn], rr, None, ALU.mult, None, [prev_k[1], "sm"], kin)
                    c.stt(iv[:, 3:4], pr[:, pn - 1:pn], ri_, iv[:, 2:3], ALU.mult, ALU.add,
                          [prev_k[0], "sm"] + kin, kin)
                    ir, ii_ = iv[:, 1:2], iv[:, 3:4]
                sr, si = T_["sr"], T_["si"]
                k.op("dve", lambda: nc.vector.tensor_tensor_scan(sr[:, 0:n], magT[:, 0:n], T_["gr"][:, 0:n], ir,
                                                                 ALU.mult, ALU.add),
                     ["magT", K_["gr"]] + kin, [K_["sr"]])
                k.op("dve", lambda: nc.vector.tensor_tensor_scan(si[:, 0:n], magT[:, 0:n], T_["gi"][:, 0:n], ii_,
                                                                 ALU.mult, ALU.add),
                     ["magT", K_["gi"]] + kin, [K_["si"]])
                state["prev"] = (sr, si, n)
                state["prev_k"] = (K_["sr"], K_["si"])
                c.tt("pool", T_["t1"][:, 0:n], sr[:, 0:n], cs_, ALU.mult, [K_["sr"], "tab"], [K_["t1"]])
                c.tt("pool", T_["t2"][:, 0:n], si[:, 0:n], sn_, ALU.mult, [K_["si"], "tab"], [K_["t2"]])
                c.tt("pool", T_["t3"][:, 0:n], sr[:, 0:n], sn_, ALU.mult, [K_["sr"], "tab"], [K_["t3"]])
                c.tt("pool", T_["t4"][:, 0:n], si[:, 0:n], cs_, ALU.mult, [K_["si"], "tab"], [K_["t4"]])
                c.tt("dve", T_["hr"][:, 0:n], T_["t1"][:, 0:n], T_["t2"][:, 0:n], ALU.subtract,
                     [K_["t1"], K_["t2"]], [K_["hr"]])
                c.tt("dve", T_["hi"][:, 0:n], T_["t3"][:, 0:n], T_["t4"][:, 0:n], ALU.add,
                     [K_["t3"], K_["t4"]], [K_["hi"]])
                c.mm(PS[bY][:, 0:n], [(Cl[:, 0, db, :], T_["hr"][:, 0:n]), (Cl[:, 1, db, :], T_["hi"][:, 0:n])],
                     ["Cl", K_["hr"], K_["hi"]], [psk(bY)])
                c.tt("dve", yTc[:, off:off + n], yTc[:, off:off + n], rv(PS[bY][:, 0:n]), ALU.add,
                     [psk(bY), "yT"], ["yT"])
                if pidx is not None:
                    c.cp("act", fin[:, pidx, d, blk, 0:1], T_["hr"][:, n - 1:n], [K_["hr"]], ["fin"])
                    c.cp("act", fin[:, pidx, d, blk, 1:2], T_["hi"][:, n - 1:n], [K_["hi"]], ["fin"])

            s1(0)
            for ci in range(1, len(chunks)):
                s1(ci)
                s2(ci - 1)
            s2(len(chunks) - 1)
            if db_i % 8 == 7:
                for t0 in range(0, T, 512):
                    cols = slice(t0, t0 + 512)
                    yv = yTc[:, cols]
                    c.stt(yv, utc[:, cols], vec[:, cch:cch + 1], yv, ALU.mult, ALU.add, ["ut", "vec", "yT"], ["yT"])
                    a, b = wk[0], wk[1]
                    c.tt("dve", a[:, 0:512], yv, yv, ALU.mult, ["yT"], ["wk0"])
                    c.ts("dve", a[:, 0:512], a[:, 0:512], 0.044715, 1.0, ALU.mult, ALU.add, ["wk0"], ["wk0"])
                    c.tt("dve", a[:, 0:512], a[:, 0:512], yv, ALU.mult, ["wk0", "yT"], ["wk0"])
                    c.act(b[:, 0:512], a[:, 0:512], AF.Sigmoid, ["wk0"], ["wk1"], scale=GC)
                    c.tt("dve", yv, yv, b[:, 0:512], ALU.mult, ["yT", "wk1"], ["yT"])
                k.dma("sp", UT[cch * 128:(cch + 1) * 128, :], yTc[:], reads=["yT", "ut"], writes=[("UTd", cch)])
        for s in range(4):
            k.dma("sp", nst_out[s, l].rearrange("d (b g) p r -> (g p) d b r", g=2), fin[:, s],
                  reads=["fin"], writes=["nst"])
    k.barrier()
    with ExitStack() as ph:
        sb = lambda n, s, d=F32: c.sb(n, s, d, ph)
        vec = sb("sg_vec", [128, 16])
        wg = sb("sg_wg", [128, 4, 1024])
        gy = [sb("sg_gy%d" % i, [128, 4, 512]) for i in range(2)]
        wk = [sb("sg_wk%d" % i, [128, 512]) for i in range(2)]
        stg = [sb("sg_st%d" % i, [128, 512], MDT) for i in range(2)]
        k.dma("sp", vec[:], vecs[l], writes=["vec"])
        k.dma("sp", wg[:], wglu_t[l].rearrange("p (k c) -> p k c", k=4), writes=["wg"])
        n_s = 0
        for ti, t0 in enumerate(range(0, T, 512)):
            cols = slice(t0, t0 + 512)
            g = gy[ti % 2]
            kg = ("sg_gy", ti % 2)
            k.dma("sp", g[:], UT[:, cols].rearrange("(k p) t -> p k t", p=128), writes=[kg])
            for cc in range(4):
                bv, bg = cc % 2, 2 + cc % 2
                c.mm(PS[bv][:], [(wg[:, kc, cc * 128:(cc + 1) * 128], g[:, kc, :]) for kc in range(4)],
                     ["wg", kg], [psk(bv)])
                c.mm(PS[bg][:], [(wg[:, kc, 512 + cc * 128:512 + (cc + 1) * 128], g[:, kc, :]) for kc in range(4)],
                     ["wg", kg], [psk(bg)])
                a, b = wk[0], wk[1]
                c.act(a[:], PS[bv][:], AF.Identity, [psk(bv), "vec"], ["wk0"], bias=vec[:, 4 + cc:5 + cc])
                c.act(b[:], PS[bg][:], AF.Sigmoid, [psk(bg), "vec"], ["wk1"], bias=vec[:, 8 + cc:9 + cc])
                s = stg[n_s % 2]
                ks = ("sg_st", n_s % 2)
                n_s += 1
                c.tt("dve", s[:], a[:], b[:], ALU.mult, ["wk0", "wk1"], [ks])
                k.dma("sp", MIX[1024 + cc * 128:1024 + (cc + 1) * 128, cols], s[:], reads=[ks], writes=[("MIX", "s")])


def _prep_shared(inp):
    f = lambda a: np.ascontiguousarray(np.asarray(a), dtype=np.float32)
    L = DEPTH
    sh = {}
    w_mod = f(inp["w_mod"])
    sh["wmod_t"] = np.ascontiguousarray(
        w_mod.reshape(L, 16, 128, 24, 512).transpose(0, 3, 2, 1, 4)).reshape(L, 24, 128, 8192)
    sh["bmodT"] = np.ascontiguousarray(f(inp["b_mod"]).reshape(L, 96, 128).transpose(0, 2, 1))
    gs = [inp["norm1_g"][0], inp["norm1_g"][1], inp["norm2_g"][0], inp["norm2_g"][1], inp["final_norm_g"]]
    sh["gT"] = np.ascontiguousarray(np.stack([f(g).reshape(16, 128).T for g in gs], axis=1))
    w_in = f(inp["w_in"])
    sh["win_t"] = np.ascontiguousarray(
        w_in.reshape(L, 16, 128, 32, 128).transpose(0, 3, 2, 1, 4)).reshape(L, 32, 128, 2048)
    w_out = f(inp["w_out"])
    sh["wout_t"] = np.ascontiguousarray(
        w_out.reshape(L, 16, 128, 16, 128).transpose(0, 3, 2, 1, 4)).reshape(L, 16, 128, 2048)
    fi = f(inp["ffn_w_in"])
    sh["ffin_t"] = np.ascontiguousarray(
        fi.reshape(L, 16, 128, 88, 128).transpose(0, 3, 2, 1, 4)).reshape(L, 88, 128, 2048)
    fo = f(inp["ffn_w_out"])
    sh["ffout_t"] = np.ascontiguousarray(
        fo.reshape(L, NSPLIT, SCH, 128, 16, 128).transpose(0, 1, 4, 3, 2, 5)).reshape(L, NSPLIT, 16, 128, SCH * 128)
    cq = np.arange(64)
    idx = np.clip(cq[None, :] - cq[:, None], -15, 15) + 15
    rpb = f(inp["attn_rpb"])
    tab = rpb[:, :, :, idx]
    sh["rpbtab"] = np.ascontiguousarray(tab.transpose(0, 1, 3, 2, 4)).reshape(L, NH, 64, 15 * 64)
    cs = np.clip(cq - 8, 0, 48)
    valid = (cq[None, :] >= cs[:, None]) & (cq[None, :] < cs[:, None] + 16)
    sh["colmask"] = np.where(valid, 0.0, NEG).astype(np.float32)
    def st_lay(x):
        tail = x.shape[4:]
        y = x.reshape((L, 2, 16, 2, 64) + tail)
        perm = (0, 3, 4, 1, 2) + tuple(range(5, 5 + len(tail)))
        return np.ascontiguousarray(y.transpose(perm)).reshape((L, 128, 32) + tail)
    a_re, a_im = f(inp["ssm_a_re"]), f(inp["ssm_a_im"])
    ldt = np.broadcast_to(f(inp["ssm_log_dt"])[:, :, :, None], (L, 2, 32, 64))
    sh["ssm_a"] = np.ascontiguousarray(np.stack([st_lay(a_re), st_lay(a_im), st_lay(np.ascontiguousarray(ldt))], axis=2))
    sh["ssm_B"] = np.ascontiguousarray(np.stack([st_lay(f(inp["ssm_b_re"])), st_lay(f(inp["ssm_b_im"]))], axis=2))
    C = np.zeros((L, 2, 128, 32, 128), np.float32)
    for ri, key in enumerate(("ssm_c_re", "ssm_c_im")):
        cc = f(inp[key])
        for d in range(2):
            for bk in range(16):
                for gl in range(2):
                    c0 = (bk % 4) * 32 + gl * 16
                    C[:, ri, gl * 64:(gl + 1) * 64, d * 16 + bk, c0:c0 + 16] = \
                        cc[:, d, 2 * bk + gl].transpose(0, 2, 1)
    sh["ssm_C"] = C.reshape(L, 2, 128, 32 * 128)
    vec = np.zeros((L, 128, 16), np.float32)
    vec[:, :, 0:4] = f(inp["ssm_d"]).reshape(L, 4, 128).transpose(0, 2, 1)
    vec[:, :, 4:12] = f(inp["ssm_b_glu"]).reshape(L, 8, 128).transpose(0, 2, 1)
    vec[:, :, 12:16] = f(inp["pool_scale"]).reshape(L, 4, 128).transpose(0, 2, 1)
    sh["vecs"] = vec
    sh["wglu_t"] = np.ascontiguousarray(f(inp["ssm_w_glu"]).reshape(L, 4, 128, 1024).transpose(0, 2, 1, 3)).reshape(L, 128, 4096)
    sh["poolw_t"] = np.ascontiguousarray(f(inp["pool_w"]).transpose(0, 2, 1, 3)).reshape(L, 128, 512)
    ic = np.zeros((4, 256 + TS), np.float32)
    for n, o in ((256, 0), (TS, 256)):
        t = np.arange(n)
        for g in range(4):
            w = 2 << g
            lo = np.clip(t - w // 2, 0, n)
            hi = np.clip(t - w // 2 + w, 0, n)
            ic[g, o:o + n] = 1.0 / (hi - lo)
    sh["invcnt"] = np.ascontiguousarray(np.broadcast_to(ic[None], (128, 4, 256 + TS)))
    sh["ident"] = np.eye(128, dtype=np.float32)
    return sh


def _prep_core(inp, c, sh):
    f = lambda a: np.ascontiguousarray(np.asarray(a), dtype=np.float32)
    b = c // 4
    m = dict(sh)
    m["x_in"] = np.concatenate([f(inp["x_prompt"][4 * c:4 * c + 4]).reshape(TP, D), f(inp["x_sample"][b])], axis=0)
    cond = np.stack([f(inp["c_ctx"]), f(inp["c"][b])], axis=0)
    m["condT"] = np.ascontiguousarray(cond.reshape(2, 16, 128).transpose(2, 1, 0))
    m["ck"] = f(inp["cache_k"][b])
    m["cv"] = f(inp["cache_v"][b])
    st = f(inp["state_ssm"][b])
    m["st0"] = np.ascontiguousarray(
        st.reshape(DEPTH, 2, 16, 2, 64, 2).transpose(0, 3, 4, 1, 2, 5)).reshape(DEPTH, 128, 2, 16, 2)
    return m


_NC_CACHE = {}


def kernel(**inp):
    n = 8
    key = (str(MDT), DEBUG, STOP_AFTER)
    if key not in _NC_CACHE:
        _NC_CACHE[key] = build_program()
    nc = _NC_CACHE[key]
    sh = _prep_shared(inp)
    in_maps = [_prep_core(inp, c, sh) for c in range(n)]
    res = run_bass_kernel_spmd(nc, in_maps, core_ids=list(range(n)))
    R = res.results
    y_prompt = np.concatenate([R[c]["y"][:TP].reshape(4, 256, D) for c in range(n)], axis=0)
    y_sample = np.stack([R[0]["y"][TP:], R[4]["y"][TP:]], axis=0)
    nk = np.concatenate([R[c]["nk"] for c in range(n)], axis=0)
    nv = np.concatenate([R[c]["nv"] for c in range(n)], axis=0)
    nst = np.concatenate([R[c]["nst"] for c in range(n)], axis=0)
    return (np.ascontiguousarray(y_prompt, dtype=np.float32), np.ascontiguousarray(y_sample, dtype=np.float32),
            np.ascontiguousarray(nk, dtype=np.float32), np.ascontiguousarray(nv, dtype=np.float32),
            np.ascontiguousarray(nst, dtype=np.float32))
```

```python
import math
from contextlib import ExitStack

import numpy as np
import concourse.bass as bass
import concourse.mybir as mybir
from concourse.bass_utils import run_bass_kernel_spmd

F32 = mybir.dt.float32
BF16 = mybir.dt.bfloat16
I32 = mybir.dt.int32
AF = mybir.ActivationFunctionType
ALU = mybir.AluOpType
AX = mybir.AxisListType

D = 2048
NCH = 16
DEPTH = 2
NH = 16
HD = 64
TP = 1024
TS = 2048
T = TP + TS
TT = 1024
NT = T // TT
FF = 5632
FCH = 44
NSPLIT = 4
SCH = FCH // NSPLIT
PAST = 512
GW = 64
EPS = 1e-6
NEG = -1e30
LC = 512
TWO_PI = 2.0 * math.pi

MDT = BF16
DEBUG = False
STOP_AFTER = None
BISECT = None


class KB:
    NDMA = 24

    def __init__(self, nc, stack):
        self.nc = nc
        self.engs = {"pe": nc.tensor, "act": nc.scalar, "dve": nc.vector,
                     "pool": nc.gpsimd, "sp": nc.sync}
        self.sem = {}
        self.cnt = {}
        for n in self.engs:
            self.sem[n] = stack.enter_context(nc.semaphore("s_" + n))
            self.cnt[n] = 0
        self.dsem = [stack.enter_context(nc.semaphore("d%d" % i)) for i in range(self.NDMA)]
        self.dcnt = [0] * self.NDMA
        self.dnext = 0
        self.seen = {n: {} for n in self.engs}
        self.bufs = {}
        self.n_ops = 0

    def _semh(self, sk):
        return self.sem[sk] if isinstance(sk, str) else self.dsem[sk[1]]

    def _wait(self, eng, tok):
        sk, v = tok
        if sk == eng and eng in ("pe", "sp"):
            return
        if self.seen[eng].get(sk, 0) >= v:
            return
        self.engs[eng].wait_ge(self._semh(sk), v)
        self.seen[eng][sk] = v

    def _deps(self, eng, reads, writes):
        toks = {}

        def add(t):
            if t is not None and toks.get(t[0], 0) < t[1]:
                toks[t[0]] = t[1]
        for r in reads:
            st = self.bufs.get(r)
            if st:
                add(st[0])
        for w in writes:
            st = self.bufs.get(w)
            if st:
                add(st[0])
                for t in st[1]:
                    add(t)
        for sk, v in toks.items():
            self._wait(eng, (sk, v))

    def _record(self, tok, reads, writes):
        for r in reads:
            st = self.bufs.setdefault(r, [None, []])
            st[1] = [t for t in st[1] if t[0] != tok[0]] + [tok]
        for w in writes:
            self.bufs[w] = [tok, []]

    def op(self, eng, fn, reads=(), writes=()):
        reads, writes = list(reads), list(writes)
        for r in reads:
            if isinstance(r, tuple) and r[0] == "ps" and r not in writes:
                writes.append(r)
        self._deps(eng, reads, writes)
        ins = fn()
        self.cnt[eng] += 1
        ins.then_inc(self.sem[eng], 1)
        tok = (eng, self.cnt[eng])
        self._record(tok, reads, writes)
        self.n_ops += 1
        return tok

    def dma(self, q, out, in_, reads=(), writes=(), **kw):
        i = self.dnext
        self.dnext = (self.dnext + 1) % self.NDMA
        if self.dcnt[i]:
            self._wait(q, (("d", i), 16 * self.dcnt[i]))
        self._deps(q, reads, writes)
        ins = self.engs[q].dma_start(out=out, in_=in_, **kw)
        self.dcnt[i] += 1
        ins.then_inc(self.dsem[i], 16)
        tok = (("d", i), 16 * self.dcnt[i])
        self._record(tok, reads, writes)
        self.n_ops += 1
        return tok

    def barrier(self):
        for e in self.engs:
            for o in self.engs:
                if self.cnt[o]:
                    self._wait(e, (o, self.cnt[o]))
            for i in range(self.NDMA):
                if self.dcnt[i]:
                    self._wait(e, (("d", i), 16 * self.dcnt[i]))
        self.bufs = {}


class WStream:
    def __init__(self, k, bufs, srcs):
        self.k, self.bufs, self.srcs = k, bufs, srcs
        self.nb = len(bufs)
        self.issued = 0

    def get(self, i):
        while self.issued < len(self.srcs) and self.issued <= i + self.nb - 1:
            j = self.issued
            src = self.srcs[j]
            n = src.shape[1]
            dst = self.bufs[j % self.nb][:, 0:n]
            if n > 1024:
                dst = dst.rearrange("p (a b) -> p a b", a=2)
                src = src.rearrange("p (a b) -> p a b", a=2)
            self.k.dma("pool", dst, src, writes=[("wb", j % self.nb)])
            self.issued += 1
        return self.bufs[i % self.nb], ("wb", i % self.nb)


def build_program():
    nc = bass.Bass("TRN2", target_bir_lowering=False)
    dt_in = lambda name, shape, dt=F32: nc.dram_tensor(name, list(shape), dt, kind="ExternalInput").ap()
    dt_out = lambda name, shape, dt=F32: nc.dram_tensor(name, list(shape), dt, kind="ExternalOutput").ap()
    dt_scr = lambda name, shape, dt=F32: (nc.dram_tensor(name, list(shape), dt, kind="ExternalOutput").ap()
                                          if DEBUG else nc.dram_tensor(name, list(shape), dt).ap())

    x_in = dt_in("x_in", [T, D])
    condT = dt_in("condT", [128, NCH, 2])
    ck_d = dt_in("ck", [DEPTH, NH, PAST, HD])
    cv_d = dt_in("cv", [DEPTH, NH, PAST, HD])
    st0_d = dt_in("st0", [DEPTH, 128, 2, 16, 2])
    wmod_t = dt_in("wmod_t", [DEPTH, 24, 128, NCH * 512])
    bmodT = dt_in("bmodT", [DEPTH, 128, 96])
    gT = dt_in("gT", [128, 5, NCH])
    win_t = dt_in("win_t", [DEPTH, 32, 128, NCH * 128])
    wout_t = dt_in("wout_t", [DEPTH, 16, 128, NCH * 128])
    ffin_t = dt_in("ffin_t", [DEPTH, 88, 128, NCH * 128])
    ffout_t = dt_in("ffout_t", [DEPTH, NSPLIT, 16, 128, SCH * 128])
    rpbtab = dt_in("rpbtab", [DEPTH, NH, 64, 15 * 64])
    colmask = dt_in("colmask", [64, 64])
    ssm_a = dt_in("ssm_a", [DEPTH, 128, 3, 32])
    ssm_B = dt_in("ssm_B", [DEPTH, 128, 2, 32, 16])
    ssm_C = dt_in("ssm_C", [DEPTH, 2, 128, 32 * 128])
    vecs = dt_in("vecs", [DEPTH, 128, 16])
    wglu_t = dt_in("wglu_t", [DEPTH, 128, 4 * 1024])
    poolw_t = dt_in("poolw_t", [DEPTH, 128, 4 * 128])
    invcnt = dt_in("invcnt", [128, 4, 256 + TS])
    ident_d = dt_in("ident", [128, 128])

    y_out = dt_out("y", [T, D])
    nk_out = dt_out("nk", [4, DEPTH, NH, 256, HD])
    nv_out = dt_out("nv", [4, DEPTH, NH, 256, HD])
    nst_out = dt_out("nst", [4, DEPTH, 2, 32, 64, 2])

    XT = dt_scr("XT", [D, T])
    QT = dt_scr("QT", [1024, T], MDT)
    KT = dt_scr("KT", [1024, T], MDT)
    VTOK = dt_scr("VTOK", [T, 1024], MDT)
    UT = dt_scr("UT", [512, T])
    PT = dt_scr("PT", [512, T])
    MIX = dt_scr("MIX", [D, T], MDT)

    with ExitStack() as top:
        k = KB(nc, top)
        _uid = [0]

        def sb(name, shape, dt=F32, st=top):
            _uid[0] += 1
            return st.enter_context(nc.sbuf_tensor("sb%d_%s" % (_uid[0], name), list(shape), dt))
        PSB = [top.enter_context(nc.psum_tensor("psb%d" % i, [128, 1024], F32)) for i in range(4)]
        PS = [PSB[i // 2][:, (i % 2) * 512:(i % 2 + 1) * 512] for i in range(8)]
        psk = lambda i: ("ps", i)

        ident = sb("ident", [128, 128])
        ones = sb("ones", [128, 128])
        modv = sb("modv", [128, DEPTH, 6, NCH, 2])
        gsb = sb("gsb", [128, 5, NCH])
        k.dma("sp", ident[:], ident_d, writes=["ident"])
        k.dma("sp", gsb[:], gT, writes=["gsb"])
        k.op("dve", lambda: nc.vector.memset(ones[:], 1.0), writes=["ones"])

        def mm(out, pairs, reads, writes):
            def f():
                n = len(pairs)
                ins = None
                for i, (l, r) in enumerate(pairs):
                    ins = nc.tensor.matmul(out, l, r, start=(i == 0), stop=(i == n - 1))
                return ins
            return k.op("pe", f, reads, writes)

        def act(out, in_, func, reads, writes, **kw):
            return k.op("act", lambda: nc.scalar.activation(out, in_, func, **kw), reads, writes)

        def tt(eng, out, a, b, op, reads, writes):
            e = nc.vector if eng == "dve" else nc.gpsimd
            return k.op(eng, lambda: e.tensor_tensor(out, a, b, op), reads, writes)

        def ts(eng, out, a, s1, s2, op0, op1, reads, writes):
            e = nc.vector if eng == "dve" else nc.gpsimd
            if s2 is None:
                return k.op(eng, lambda: e.tensor_scalar(out, a, s1, None, op0), reads, writes)
            return k.op(eng, lambda: e.tensor_scalar(out, a, s1, s2, op0, op1), reads, writes)

        def stt(out, a, s, b, op0, op1, reads, writes):
            return k.op("dve", lambda: nc.vector.scalar_tensor_tensor(out, a, s, b, op0, op1), reads, writes)

        def cp(eng, out, in_, reads, writes):
            if eng == "act":
                return k.op("act", lambda: nc.scalar.copy(out, in_), reads, writes)
            e = nc.vector if eng == "dve" else nc.gpsimd
            return k.op(eng, lambda: e.tensor_copy(out, in_), reads, writes)

        def stop(name):
            return STOP_AFTER == name

        class Ctx:
            pass
        c = Ctx()
        c.nc, c.k, c.PS, c.PSB, c.ident, c.ones, c.sb = nc, k, PS, PSB, ident, ones, sb
        c.mm, c.act, c.tt, c.ts, c.stt, c.cp, c.psk = mm, act, tt, ts, stt, cp, psk

        with ExitStack() as ph:
            xt4 = [sb("p0_x%d" % i, [128, 4, D], F32, ph) for i in range(2)]
            stg = [sb("p0_s%d" % i, [128, 512], F32, ph) for i in range(4)]
            si = 0
            for g in range(T // 512):
                xb = xt4[g % 2]
                kx = ("p0x", g % 2)
                k.dma("sp", xb[:], x_in[g * 512:(g + 1) * 512, :].rearrange("(b p) f -> p b f", p=128),
                      writes=[kx])
                for fc in range(NCH):
                    bank = fc % 4
                    def f(bank=bank, fc=fc, xb=xb):
                        ins = None
                        for b in range(4):
                            ins = nc.tensor.transpose(PS[bank][:, b * 128:(b + 1) * 128],
                                                      xb[:, b, fc * 128:(fc + 1) * 128], ident[:])
                        return ins
                    k.op("pe", f, reads=[kx, "ident"], writes=[psk(bank)])
                    s = stg[si % 4]
                    ks = ("p0s", si % 4)
                    si += 1
                    cp("act" if fc % 2 else "dve", s[:], PS[bank][:], [psk(bank)], [ks])
                    k.dma("sp", XT[fc * 128:(fc + 1) * 128, g * 512:(g + 1) * 512], s[:],
                          reads=[ks], writes=[("XT", g // 2)])
        k.barrier()

        with ExitStack() as ph:
            cs = sb("m_cs", [128, NCH, 2], F32, ph)
            wb = [sb("m_w%d" % i, [128, NCH * 512], F32, ph) for i in range(2)]
            row = [sb("m_r%d" % i, [2, 512], F32, ph) for i in range(2)]
            bm = sb("m_b", [128, DEPTH, 96], F32, ph)
            modT = sb("m_modT", [128, DEPTH, 96, 2], F32, ph)
            k.dma("sp", cs[:], condT, writes=["cs"])
            k.dma("sp", bm[:], bmodT.rearrange("l p c -> p l c"), writes=["bm"])
            act(cs[:], cs[:], AF.Silu, ["cs"], ["cs"])
            it = 0
            for l in range(DEPTH):
                for n in range(24):
                    w = wb[it % 2]
                    kw = ("mw", it % 2)
                    k.dma("sp" if it % 2 else "act", w[:], wmod_t[l, n], writes=[kw])
                    wv = w[:].rearrange("p (k c) -> p k c", k=NCH)
                    bank = it % 2
                    mm(PS[bank][0:2, :], [(cs[:, kc, :], wv[:, kc, :]) for kc in range(NCH)],
                       [kw, "cs"], [psk(bank)])
                    r = row[it % 2]
                    kr = ("mr", it % 2)
                    cp("act", r[:], PS[bank][0:2, :], [psk(bank)], [kr])
                    tb = 2 + (it % 2)
                    def f(r=r, tb=tb):
                        ins = None
                        for j in range(4):
                            ins = nc.tensor.transpose(PS[tb][:, j * 2:(j + 1) * 2],
                                                      r[0:2, j * 128:(j + 1) * 128], ident[0:2, 0:2])
                        return ins
                    k.op("pe", f, reads=[kr, "ident"], writes=[psk(tb)])
                    pv = PS[tb][:, 0:8].rearrange("p (j c) -> p j c", c=2)
                    for cc in range(2):
                        tt("dve", modT[:, l, n * 4:(n + 1) * 4, cc], pv[:, :, cc],
                           bm[:, l, n * 4:(n + 1) * 4], ALU.add, [psk(tb), "bm"], ["modT"])
                    it += 1
            for l in range(DEPTH):
                for half in range(2):
                    o = half * 48
                    gi = l if half == 0 else 2 + l
                    for cc in range(2):
                        stt(modv[:, l, 3 * half + 0, :, cc], modT[:, l, o + 16:o + 32, cc], 1.0,
                            gsb[:, gi, :], ALU.add, ALU.mult, ["modT", "gsb"], ["modv"])
                        cp("dve", modv[:, l, 3 * half + 1, :, cc], modT[:, l, o:o + 16, cc], ["modT"], ["modv"])
                        cp("dve", modv[:, l, 3 * half + 2, :, cc], modT[:, l, o + 32:o + 48, cc], ["modT"], ["modv"])
        if DEBUG:
            dbg_modv = dt_out("dbg_modv", [128, DEPTH * 6 * NCH * 2])
            k.dma("sp", dbg_modv, modv[:].rearrange("p l s f c -> p (l s f c)"), reads=["modv"], writes=["dbg"])
        k.barrier()
        if stop("M"):
            return finish(nc, k)

        def rms_rstd(xT, kx, rstd, sq, ssb):
            for fc in range(NCH):
                s = sq[fc % 2]
                ksq = ("sq", fc % 2)
                if fc % 2 == 0:
                    act(s[:], xT[:, fc, :], AF.Square, [kx], [ksq])
                else:
                    tt("dve", s[:], xT[:, fc, :], xT[:, fc, :], ALU.mult, [kx], [ksq])
                for hb in range(TT // 512):
                    def f(hb=hb, s=s, fc=fc):
                        return nc.tensor.matmul(PS[ssb + hb][:], ones[:], s[:, hb * 512:(hb + 1) * 512],
                                                start=(fc == 0), stop=(fc == NCH - 1))
                    k.op("pe", f, reads=[ksq, "ones"], writes=[psk(ssb + hb)])
            for hb in range(TT // 512):
                sl = slice(hb * 512, (hb + 1) * 512)
                ts("dve", rstd[:, sl], PS[ssb + hb][:], 1.0 / D, EPS, ALU.mult, ALU.add,
                   [psk(ssb + hb)], ["rstd"])
                k.op("act", lambda sl=sl: nc.scalar.sqrt(rstd[:, sl], rstd[:, sl]), ["rstd"], ["rstd"])
                k.op("dve", lambda sl=sl: nc.vector.reciprocal(rstd[:, sl], rstd[:, sl]), ["rstd"], ["rstd"])

        for l in range(DEPTH):
            last = (l == DEPTH - 1)
            with ExitStack() as ph:
                xT = sb("a_xT", [128, NCH, TT], F32, ph)
                hT = sb("a_hT", [128, NCH, TT], MDT, ph)
                rstd = sb("a_rstd", [128, TT], F32, ph)
                sq = [sb("a_sq%d" % i, [128, TT], F32, ph) for i in range(2)]
                wbufs = [sb("a_w%d" % i, [128, NCH * 128], MDT, ph) for i in range(8)]
                sgm = [sb("a_sm%d" % i, [128, TT], MDT, ph) for i in range(2)]
                sgf = [sb("a_sf%d" % i, [128, TT], F32, ph) for i in range(2)]
                tmf = [sb("a_tf%d" % i, [128, 4, 128], F32, ph) for i in range(2)]
                tmm = [sb("a_tm%d" % i, [128, 4, 128], MDT, ph) for i in range(2)]
                ws = WStream(k, wbufs, [win_t[l, oc] for oc in range(32)] * NT)
                wi = 0
                n_sm = n_sf = n_tm = 0
                for t in range(NT):
                    cond = 0 if t == 0 else 1
                    tok0 = t * TT
                    for fc in range(NCH):
                        k.dma("sp", xT[:, fc, :], XT[fc * 128:(fc + 1) * 128, tok0:tok0 + TT],
                              reads=[("XT", t)], writes=["a_xT"])
                    rms_rstd(xT, "a_xT", rstd, sq, 4)
                    for fc in range(NCH):
                        s = sq[fc % 2]
                        ksq = ("sq", fc % 2)
                        tt("dve", s[:], xT[:, fc, :], rstd[:], ALU.mult, ["a_xT", "rstd"], [ksq])
                        act(hT[:, fc, :], s[:], AF.Identity, [ksq, "modv"], ["a_hT"],
                            scale=modv[:, l, 0, fc, cond:cond + 1], bias=modv[:, l, 1, fc, cond:cond + 1])
                    if BISECT == -1:
                        return finish(nc, k)
                    for oc in range(32):
                        if BISECT is not None and oc == BISECT % 100:
                            return finish(nc, k)
                        w, kw = ws.get(wi)
                        wi += 1
                        wv = w[:].rearrange("p (k c) -> p k c", k=NCH)
                        acc = 2 * (oc % 2)
                        for hb in range(TT // 512):
                            mm(PS[acc + hb][:],
                               [(wv[:, kc, :], hT[:, kc, hb * 512:(hb + 1) * 512]) for kc in range(NCH)],
                               [kw, "a_hT"], [psk(acc + hb)])
                        kind = ("q" if oc < 8 else "k" if oc < 16 else "v" if oc < 24
                                else "u" if oc < 28 else "p")
                        need_f32 = kind in ("u", "p", "v") or (kind == "k" and t == 0)
                        need_m = kind in ("q", "k")
                        pr = [psk(acc), psk(acc + 1)]
                        if need_m:
                            s = sgm[n_sm % 2]
                            ks = ("a_sm", n_sm % 2)
                            n_sm += 1
                            for hb in range(TT // 512):
                                sl = slice(hb * 512, (hb + 1) * 512)
                                if kind == "q":
                                    k.op("act", lambda s=s, sl=sl, hb=hb: nc.scalar.mul(s[:, sl], PS[acc + hb][:], 0.125),
                                         [psk(acc + hb)], [ks])
                                else:
                                    cp("act", s[:, sl], PS[acc + hb][:], [psk(acc + hb)], [ks])
                            dst = QT if kind == "q" else KT
                            r0 = (oc % 8) * 128
                            k.dma("sp", dst[r0:r0 + 128, tok0:tok0 + TT], s[:], reads=[ks],
                                  writes=[("QK", kind, t)])
                        if need_f32:
                            s = sgf[n_sf % 2]
                            ks = ("a_sf", n_sf % 2)
                            n_sf += 1
                            for hb in range(TT // 512):
                                sl = slice(hb * 512, (hb + 1) * 512)
                                cp("dve", s[:, sl], PS[acc + hb][:], [psk(acc + hb)], [ks])
                            if kind in ("u", "p"):
                                dst = UT if kind == "u" else PT
                                r0 = (oc % 4) * 128
                                k.dma("sp", dst[r0:r0 + 128, tok0:tok0 + TT], s[:], reads=[ks],
                                      writes=[("UP", kind, t)])
                            else:
                                ocl = oc % 8
                                for q4 in range(TT // 512):
                                    if BISECT is not None and BISECT >= 200:
                                        continue
                                    tb = 6 + (n_tm % 2)
                                    def f(s=s, q4=q4, tb=tb):
                                        ins = None
                                        for b in range(4):
                                            c0 = q4 * 512 + b * 128
                                            ins = nc.tensor.transpose(PS[tb][:, b * 128:(b + 1) * 128],
                                                                      s[:, c0:c0 + 128], ident[:])
                                        return ins
                                    k.op("pe", f, reads=[ks, "ident"], writes=[psk(tb)])
                                    pv = PS[tb][:].rearrange("p (b f) -> p b f", b=4)
                                    if kind == "v":
                                        m = tmm[n_tm % 2]
                                        km = ("a_tmm", n_tm % 2)
                                        cp("act", m[:], pv, [psk(tb)], [km])
                                        r0 = tok0 + q4 * 512
                                        k.dma("sp", VTOK[r0:r0 + 512, ocl * 128:(ocl + 1) * 128]
                                              .rearrange("(b p) f -> p b f", p=128), m[:],
                                              reads=[km], writes=[("VTOK", t)])
                                    if t == 0:
                                        m = tmf[n_tm % 2]
                                        km = ("a_tmf", n_tm % 2)
                                        cp("dve", m[:], pv, [psk(tb)], [km])
                                        dst = nk_out if kind == "k" else nv_out
                                        for sq_ in range(2):
                                            seq = q4 * 2 + sq_
                                            for hh in range(2):
                                                if BISECT is not None and BISECT >= 100:
                                                    continue
                                                k.dma("sp",
                                                      dst[seq, l, 2 * ocl + hh, :, :].rearrange("(b p) d -> p b d", p=128),
                                                      m[:, 2 * sq_:2 * sq_ + 2, hh * 64:(hh + 1) * 64],
                                                      reads=[km], writes=[("nkv", kind)])
                                    n_tm += 1
            k.barrier()
            if stop("A%d" % l):
                return finish(nc, k)

            phase_b_pool(c, l, PT, MIX, vecs, poolw_t, invcnt)
            k.barrier()
            if stop("Bp%d" % l):
                return finish(nc, k)
            phase_b_attn_prompt(c, l, QT, KT, VTOK, MIX)
            k.barrier()
            if stop("Ba%d" % l):
                return finish(nc, k)
            phase_b_attn_sample(c, l, QT, KT, VTOK, MIX, ck_d, cv_d, rpbtab, colmask)
            k.barrier()
            if stop("Bs%d" % l):
                return finish(nc, k)
            phase_b_ssm(c, l, UT, MIX, ssm_a, ssm_B, ssm_C, st0_d, vecs, wglu_t, nst_out)
            k.barrier()
            if stop("B%d" % l):
                return finish(nc, k)

            with ExitStack() as ph:
                xT = sb("c_xT", [128, NCH, TT], F32, ph)
                mT = sb("c_mT", [128, NCH, TT], MDT, ph)
                hid = sb("c_hid", [128, SCH, TT], MDT, ph)
                rstd = sb("c_rstd", [128, TT], F32, ph)
                sq = [sb("c_sq%d" % i, [128, TT], F32, ph) for i in range(2)]
                wbufs = [sb("c_w%d" % i, [128, NCH * 128], MDT, ph) for i in range(8)]
                srcs = []
                for t in range(NT):
                    srcs += [wout_t[l, oc] for oc in range(16)]
                    for sp_ in range(NSPLIT):
                        for j in range(SCH):
                            srcs += [ffin_t[l, sp_ * SCH + j], ffin_t[l, 44 + sp_ * SCH + j]]
                        srcs += [ffout_t[l, sp_, oc] for oc in range(16)]
                ws = WStream(k, wbufs, srcs)
                wi = 0
                for t in range(NT):
                    cond = 0 if t == 0 else 1
                    tok0 = t * TT
                    for fc in range(NCH):
                        k.dma("sp", xT[:, fc, :], XT[fc * 128:(fc + 1) * 128, tok0:tok0 + TT],
                              reads=[("XT", t)], writes=[("c_xT", fc)])
                        k.dma("sp", mT[:, fc, :], MIX[fc * 128:(fc + 1) * 128, tok0:tok0 + TT],
                              reads=[("MIX", t)], writes=["c_mT"])
                    for oc in range(16):
                        w, kw = ws.get(wi)
                        wi += 1
                        wv = w[:].rearrange("p (k c) -> p k c", k=NCH)
                        acc = 2 * (oc % 4)
                        for hb in range(TT // 512):
                            sl = slice(hb * 512, (hb + 1) * 512)
                            mm(PS[acc + hb][:], [(wv[:, kc, :], mT[:, kc, sl]) for kc in range(NCH)],
                               [kw, "c_mT"], [psk(acc + hb)])
                            stt(xT[:, oc, sl], PS[acc + hb][:], modv[:, l, 2, oc, cond:cond + 1], xT[:, oc, sl],
                                ALU.mult, ALU.add, [psk(acc + hb), ("c_xT", oc), "modv"], [("c_xT", oc)])
                    allx = [("c_xT", fc) for fc in range(NCH)]
                    for fc in range(NCH):
                        s = sq[fc % 2]
                        ksq = ("sq", fc % 2)
                        if fc % 2 == 0:
                            act(s[:], xT[:, fc, :], AF.Square, [("c_xT", fc)], [ksq])
                        else:
                            tt("dve", s[:], xT[:, fc, :], xT[:, fc, :], ALU.mult, [("c_xT", fc)], [ksq])
                        for hb in range(TT // 512):
                            def f(hb=hb, s=s, fc=fc):
                                return nc.tensor.matmul(PS[6 + hb][:], ones[:], s[:, hb * 512:(hb + 1) * 512],
                                                        start=(fc == 0), stop=(fc == NCH - 1))
                            k.op("pe", f, reads=[ksq, "ones"], writes=[psk(6 + hb)])
                    for hb in range(TT // 512):
                        sl = slice(hb * 512, (hb + 1) * 512)
                        ts("dve", rstd[:, sl], PS[6 + hb][:], 1.0 / D, EPS, ALU.mult, ALU.add,
                           [psk(6 + hb)], ["rstd"])
                        k.op("act", lambda sl=sl: nc.scalar.sqrt(rstd[:, sl], rstd[:, sl]), ["rstd"], ["rstd"])
                        k.op("dve", lambda sl=sl: nc.vector.reciprocal(rstd[:, sl], rstd[:, sl]), ["rstd"], ["rstd"])
                    for fc in range(NCH):
                        s = sq[fc % 2]
                        ksq = ("sq", fc % 2)
                        tt("dve", s[:], xT[:, fc, :], rstd[:], ALU.mult, [("c_xT", fc), "rstd"], [ksq])
                        act(mT[:, fc, :], s[:], AF.Identity, [ksq, "modv"], ["c_mT"],
                            scale=modv[:, l, 3, fc, cond:cond + 1], bias=modv[:, l, 4, fc, cond:cond + 1])
                    for sp_ in range(NSPLIT):
                        for j in range(SCH):
                            wg, kwg = ws.get(wi)
                            wgv = wg[:].rearrange("p (k c) -> p k c", k=NCH)
                            pa = 4 * (j % 2)
                            for hb in range(TT // 512):
                                sl = slice(hb * 512, (hb + 1) * 512)
                                mm(PS[pa + hb][:], [(wgv[:, kc, :], mT[:, kc, sl]) for kc in range(NCH)],
                                   [kwg, "c_mT"], [psk(pa + hb)])
                            wu, kwu = ws.get(wi + 1)
                            wi += 2
                            wuv = wu[:].rearrange("p (k c) -> p k c", k=NCH)
                            for hb in range(TT // 512):
                                sl = slice(hb * 512, (hb + 1) * 512)
                                mm(PS[pa + 2 + hb][:], [(wuv[:, kc, :], mT[:, kc, sl]) for kc in range(NCH)],
                                   [kwu, "c_mT"], [psk(pa + 2 + hb)])
                            for hb in range(TT // 512):
                                sl = slice(hb * 512, (hb + 1) * 512)
                                s = sq[hb]
                                ksq = ("sq", hb)
                                act(s[:, 0:512], PS[pa + hb][:], AF.Silu, [psk(pa + hb)], [ksq])
                                tt("dve", hid[:, j, sl], s[:, 0:512], PS[pa + 2 + hb][:], ALU.mult,
                                   [ksq, psk(pa + 2 + hb)], [("hid", j)])
                        hk = [("hid", j) for j in range(SCH)]
                        for oc in range(16):
                            w, kw = ws.get(wi)
                            wi += 1
                            wv = w[:, 0:SCH * 128].rearrange("p (k c) -> p k c", k=SCH)
                            acc = 2 * (oc % 4)
                            for hb in range(TT // 512):
                                sl = slice(hb * 512, (hb + 1) * 512)
                                mm(PS[acc + hb][:], [(wv[:, kc, :], hid[:, kc, sl]) for kc in range(SCH)],
                                   [kw] + hk, [psk(acc + hb)])
                                stt(xT[:, oc, sl], PS[acc + hb][:], modv[:, l, 5, oc, cond:cond + 1], xT[:, oc, sl],
                                    ALU.mult, ALU.add, [psk(acc + hb), ("c_xT", oc), "modv"], [("c_xT", oc)])
                    if not last:
                        for fc in range(NCH):
                            k.dma("sp", XT[fc * 128:(fc + 1) * 128, tok0:tok0 + TT], xT[:, fc, :],
                                  reads=[("c_xT", fc)], writes=[("XT", t)])
                    else:
                        for fc in range(NCH):
                            s = sq[fc % 2]
                            ksq = ("sq", fc % 2)
                            if fc % 2 == 0:
                                act(s[:], xT[:, fc, :], AF.Square, [("c_xT", fc)], [ksq])
                            else:
                                tt("dve", s[:], xT[:, fc, :], xT[:, fc, :], ALU.mult, [("c_xT", fc)], [ksq])
                            for hb in range(TT // 512):
                                def f(hb=hb, s=s, fc=fc):
                                    return nc.tensor.matmul(PS[6 + hb][:], ones[:], s[:, hb * 512:(hb + 1) * 512],
                                                            start=(fc == 0), stop=(fc == NCH - 1))
                                k.op("pe", f, reads=[ksq, "ones"], writes=[psk(6 + hb)])
                        for hb in range(TT // 512):
                            sl = slice(hb * 512, (hb + 1) * 512)
                            ts("dve", rstd[:, sl], PS[6 + hb][:], 1.0 / D, EPS, ALU.mult, ALU.add,
                               [psk(6 + hb)], ["rstd"])
                            k.op("act", lambda sl=sl: nc.scalar.sqrt(rstd[:, sl], rstd[:, sl]), ["rstd"], ["rstd"])
                            k.op("dve", lambda sl=sl: nc.vector.reciprocal(rstd[:, sl], rstd[:, sl]), ["rstd"], ["rstd"])
                        for fc in range(NCH):
                            stt(xT[:, fc, :], xT[:, fc, :], gsb[:, 4, fc:fc + 1], rstd[:], ALU.mult, ALU.mult,
                                [("c_xT", fc), "rstd", "gsb"], [("c_xT", fc)])
                        ytm = [sq[0], sq[1]]
                        it = 0
                        for b in range(TT // 128):
                            for q4 in range(4):
                                tb = (it % 2)
                                def f(b=b, q4=q4, tb=tb):
                                    ins = None
                                    for j in range(4):
                                        fc = q4 * 4 + j
                                        ins = nc.tensor.transpose(PS[tb][:, j * 128:(j + 1) * 128],
                                                                  xT[:, fc, b * 128:(b + 1) * 128], ident[:])
                                    return ins
                                k.op("pe", f, reads=[("c_xT", q4 * 4 + j) for j in range(4)] + ["ident"],
                                     writes=[psk(tb)])
                                s = ytm[it % 2]
                                ksq = ("sq", it % 2)
                                cp("act" if it % 2 else "dve", s[:, 0:512], PS[tb][:], [psk(tb)], [ksq])
                                r0 = tok0 + b * 128
                                k.dma("sp", y_out[r0:r0 + 128, q4 * 512:(q4 + 1) * 512], s[:, 0:512],
                                      reads=[ksq], writes=["y"])
                                it += 1
            k.barrier()
            if stop("C%d" % l):
                return finish(nc, k)
        return finish(nc, k)


def finish(nc, k, **kw):
    k.barrier()
    return nc


ITEMS = [(s * 256, 256, s) for s in range(4)] + [(TP, TS, None)]


def phase_b_pool(c, l, PT, MIX, vecs, poolw_t, invcnt):
    nc, k = c.nc, c.k
    PS, psk = c.PS, c.psk
    with ExitStack() as ph:
        sb = lambda n, s, d=F32: c.sb(n, s, d, ph)
        L = TS + 16
        pp = sb("bp_pp", [128, L])
        sa = sb("bp_sa", [128, L])
        sbb = sb("bp_sb", [128, L])
        mix = sb("bp_mix", [128, TS])
        icn = sb("bp_icn", [128, 4, 256 + TS])
        pw = sb("bp_pw", [128, 4, 128])
        vec = sb("bp_vec", [128, 16])
        stg = [sb("bp_st%d" % i, [128, 512], MDT) for i in range(2)]
        k.dma("sp", icn[:], invcnt, writes=["icn"])
        k.dma("sp", pw[:], poolw_t[l].rearrange("p (g d) -> p g d", g=4), writes=["pw"])
        k.dma("sp", vec[:], vecs[l], writes=["vec"])
        k.op("dve", lambda: nc.vector.memset(pp[:, 0:8], 0.0), writes=["pp"])
        n_st = 0
        for g in range(4):
            w = 2 << g
            for (off, n, pidx) in ITEMS:
                Ln = n + 16
                ioff = 0 if pidx is not None else 256
                k.op("dve", lambda: nc.vector.memset(pp[:, 8 + n:16 + n], 0.0), writes=["pp"])
                k.dma("sp", pp[:, 8:8 + n], PT[g * 128:(g + 1) * 128, off:off + n], writes=["pp"])
                c.tt("dve", sa[:, 0:Ln - 1], pp[:, 0:Ln - 1], pp[:, 1:Ln], ALU.add, ["pp"], ["sa"])
                src, ks, start = sa, "sa", 7
                if w >= 4:
                    c.tt("dve", sbb[:, 0:Ln - 3], sa[:, 0:Ln - 3], sa[:, 2:Ln - 1], ALU.add, ["sa"], ["sb"])
                    src, ks, start = sbb, "sb", 6
                if w >= 8:
                    c.tt("dve", sa[:, 0:Ln - 7], sbb[:, 0:Ln - 7], sbb[:, 4:Ln - 3], ALU.add, ["sb"], ["sa"])
                    src, ks, start = sa, "sa", 4
                if w >= 16:
                    c.tt("dve", sbb[:, 0:Ln - 15], sa[:, 0:Ln - 15], sa[:, 8:Ln - 7], ALU.add, ["sa"], ["sb"])
                    src, ks, start = sbb, "sb", 0
                c.tt("dve", mix[:, 0:n], src[:, start:start + n], icn[:, g, ioff:ioff + n], ALU.mult,
                     [ks, "icn"], ["mix"])
                c.tt("dve", mix[:, 0:n], mix[:, 0:n], pp[:, 8:8 + n], ALU.subtract, ["mix", "pp"], ["mix"])
                for c0 in range(0, n, 512):
                    cw = min(512, n - c0)
                    bank = n_st % 2
                    c.mm(PS[bank][:, 0:cw], [(pw[:, g, :], mix[:, c0:c0 + cw])], ["pw", "mix"], [psk(bank)])
                    s = stg[n_st % 2]
                    kst = ("bp_st", n_st % 2)
                    n_st += 1
                    c.act(s[:, 0:cw], PS[bank][:, 0:cw], AF.Copy, [psk(bank), "vec"], [kst],
                          scale=vec[:, 12 + g:13 + g])
                    k.dma("sp", MIX[1536 + g * 128:1536 + (g + 1) * 128, off + c0:off + c0 + cw], s[:, 0:cw],
                          reads=[kst], writes=[("MIX", "p")])


def phase_b_attn_prompt(c, l, QT, KT, VTOK, MIX):
    nc, k = c.nc, c.k
    PS, psk, ident = c.PS, c.psk, c.ident
    with ExitStack() as ph:
        sb = lambda n, s, d=F32: c.sb(n, s, d, ph)
        qt = [sb("pa_q%d" % i, [128, 256], MDT) for i in range(2)]
        kt = [sb("pa_k%d" % i, [128, 256], MDT) for i in range(2)]
        vt = [sb("pa_v%d" % i, [128, 2, 128], MDT) for i in range(2)]
        pb = [sb("pa_p%d" % i, [128, 256]) for i in range(2)]
        ptb = [sb("pa_pt%d" % i, [128, 2, 128], MDT) for i in range(2)]
        st = [sb("pa_st%d" % i, [128, 4]) for i in range(2)]
        og = [sb("pa_o%d" % i, [128, 256], MDT) for i in range(2)]
        it = 0
        n_in = 0
        for s in range(4):
            for j in range(8):
                b = n_in % 2
                n_in += 1
                q, kk, v = qt[b], kt[b], vt[b]
                kq, kkk, kv = ("pa_q", b), ("pa_k", b), ("pa_v", b)
                k.dma("sp", q[:], QT[j * 128:(j + 1) * 128, s * 256:(s + 1) * 256], writes=[kq])
                k.dma("sp", kk[:], KT[j * 128:(j + 1) * 128, s * 256:(s + 1) * 256], writes=[kkk])
                k.dma("sp", v[:], VTOK[s * 256:(s + 1) * 256, j * 128:(j + 1) * 128]
                      .rearrange("(b p) f -> p b f", p=128), writes=[kv])
                ob = 4 + b
                for hh in range(2):
                    hs = slice(64 * hh, 64 * hh + 64)
                    for qb in range(2):
                        i2 = it % 2
                        it += 1
                        sbk = i2
                        tbk = 2 + i2
                        c.mm(PS[sbk][:, 0:256], [(q[hs, qb * 128:(qb + 1) * 128], kk[hs, :])], [kq, kkk], [psk(sbk)])
                        stt_ = st[i2]
                        kst = ("pa_st", i2)
                        k.op("dve", lambda: nc.vector.reduce_max(stt_[:, 0:1], PS[sbk][:, 0:256], AX.X),
                             [psk(sbk)], [kst])
                        c.ts("dve", stt_[:, 1:2], stt_[:, 0:1], -1.0, None, ALU.mult, None, [kst], [kst])
                        p = pb[i2]
                        kp = ("pa_p", i2)
                        c.act(p[:], PS[sbk][:, 0:256], AF.Exp, [psk(sbk), kst], [kp, kst],
                              bias=stt_[:, 1:2], accum_out=stt_[:, 2:3])
                        k.op("dve", lambda: nc.vector.reciprocal(stt_[:, 3:4], stt_[:, 2:3]), [kst], [kst])
                        c.ts("dve", p[:], p[:], stt_[:, 3:4], None, ALU.mult, None, [kp, kst], [kp])

                        def f():
                            ins = None
                            for kb in range(2):
                                ins = nc.tensor.transpose(PS[tbk][:, kb * 128:(kb + 1) * 128],
                                                          p[:, kb * 128:(kb + 1) * 128], ident[:])
                            return ins
                        k.op("pe", f, [kp, "ident"], [psk(tbk)])
                        pt = ptb[i2]
                        kpt = ("pa_pt", i2)
                        c.cp("act", pt[:], PS[tbk][:, 0:256].rearrange("p (b q) -> p b q", b=2), [psk(tbk)], [kpt])
                        c.mm(PS[ob][hs, qb * 128:(qb + 1) * 128],
                             [(v[:, kb, hs], pt[:, kb, :]) for kb in range(2)], [kv, kpt], [psk(ob)])
                o = og[b]
                ko = ("pa_o", b)
                c.cp("dve", o[:], PS[ob][:, 0:256], [psk(ob)], [ko])
                k.dma("sp", MIX[j * 128:(j + 1) * 128, s * 256:(s + 1) * 256], o[:], reads=[ko],
                      writes=[("MIX", "a")])


def phase_b_attn_sample(c, l, QT, KT, VTOK, MIX, ck_d, cv_d, rpbtab, colmask):
    nc, k = c.nc, c.k
    PS, psk, ident = c.PS, c.psk, c.ident
    NR = TS // GW
    with ExitStack() as ph:
        sb = lambda n, s, d=F32: c.sb(n, s, d, ph)
        qt = sb("sa_q", [128, TS], MDT)
        kt = sb("sa_k", [128, TS], MDT)
        vt = sb("sa_v", [128, 16, 128], MDT)
        vt2 = sb("sa_v2", [128, 15, 128], MDT)
        ckt = sb("sa_ck", [128, 4, 128])
        kcT = sb("sa_kcT", [128, PAST], MDT)
        vc = sb("sa_vc", [128, 4, 128], MDT)
        tab = sb("sa_tab", [128, 15, 64])
        thi = sb("sa_thi", [128, 15 * 64], MDT)
        tlo = sb("sa_tlo", [128, 15 * 64], MDT)
        cm = sb("sa_cm", [128, 64])
        idm = sb("sa_idm", [128, 128], MDT)
        pb = [sb("sa_p%d" % i, [64, 1024]) for i in range(2)]
        ptb = [sb("sa_pt%d" % i, [128, 8, 64], MDT) for i in range(2)]
        st = [sb("sa_st%d" % i, [64, 4]) for i in range(2)]
        osb = [sb("sa_os%d" % i, [64, 128]) for i in range(2)]
        og = [sb("sa_o%d" % i, [128, 512], MDT) for i in range(2)]
        k.dma("sp", cm[0:64, :], colmask, writes=["cm"])
        k.dma("sp", cm[64:128, :], colmask, writes=["cm"])
        c.cp("dve", idm[:], ident[:], ["ident"], ["idm"])
        it = 0
        n_og = 0
        for j in range(8):
            k.dma("sp", qt[:], QT[j * 128:(j + 1) * 128, TP:T], writes=["sa_q"])
            k.dma("sp", kt[:], KT[j * 128:(j + 1) * 128, TP:T], writes=["sa_k"])
            k.dma("sp", vt[:], VTOK[TP:T, j * 128:(j + 1) * 128].rearrange("(b p) f -> p b f", p=128),
                  writes=["sa_v"])
            k.dma("sp", vt2[:], VTOK[TP + 64:T - 64, j * 128:(j + 1) * 128].rearrange("(b p) f -> p b f", p=128),
                  writes=["sa_v"])
            for hh in range(2):
                k.dma("sp", ckt[:, :, 64 * hh:64 * hh + 64],
                      ck_d[l, 2 * j + hh].rearrange("(b p) d -> p b d", p=128), writes=["sa_ck"])
                k.dma("pool", vc[:, :, 64 * hh:64 * hh + 64],
                      cv_d[l, 2 * j + hh].rearrange("(b p) d -> p b d", p=128), writes=["sa_vc"])
                k.dma("sp", tab[64 * hh:64 * hh + 64, :, :],
                      rpbtab[l, 2 * j + hh].rearrange("q (r c) -> q r c", r=15), writes=["sa_tab"])

            def f():
                ins = None
                for b in range(4):
                    ins = nc.tensor.transpose(PS[7][:, b * 128:(b + 1) * 128], ckt[:, b, :], ident[:])
                return ins
            k.op("pe", f, ["sa_ck", "ident"], [psk(7)])
            c.cp("act", kcT[:], PS[7][:], [psk(7)], ["sa_kcT"])
            for dr in range(15):
                c.tt("dve", tab[:, dr, :], tab[:, dr, :], cm[:], ALU.add, ["sa_tab", "cm"], ["sa_tab"])
            tabf = tab[:].rearrange("p r c -> p (r c)")
            c.cp("dve", thi[:], tabf, ["sa_tab"], ["sa_thi"])
            c.tt("dve", tabf, tabf, thi[:], ALU.subtract, ["sa_tab", "sa_thi"], ["sa_tab"])
            c.cp("dve", tlo[:], tabf, ["sa_tab"], ["sa_tlo"])
            obk = 6
            items = [(r, hh) for r in range(NR) for hh in range(2)]
            ctxs = {}

            def p1(idx):
                r, hh = items[idx]
                rs = min(max(r - 4, 0), NR - 8)
                d0 = rs - r + 7
                hs = slice(64 * hh, 64 * hh + 64)
                i2 = idx % 2
                sA, sB = 2 * i2, 2 * i2 + 1
                qs = qt[hs, r * 64:(r + 1) * 64]
                c.mm(PS[sA][0:64, :], [(qs, kt[hs, rs * 64:(rs + 8) * 64]),
                                        (idm[hs, hs], thi[hs, d0 * 64:(d0 + 8) * 64]),
                                        (idm[hs, hs], tlo[hs, d0 * 64:(d0 + 8) * 64])],
                     ["sa_q", "sa_k", "sa_thi", "sa_tlo", "idm"], [psk(sA)])
                c.mm(PS[sB][0:64, :], [(qs, kcT[hs, :])], ["sa_q", "sa_kcT"], [psk(sB)])
                S = c.PSB[i2][0:64, :]
                s_ = st[i2]
                kst = ("sa_st", i2)
                k.op("dve", lambda: nc.vector.reduce_max(s_[:, 0:1], S, AX.X), [psk(sA), psk(sB)], [kst])
                c.ts("dve", s_[:, 1:2], s_[:, 0:1], -1.0, None, ALU.mult, None, [kst], [kst])
                p = pb[i2]
                kp = ("sa_p", i2)
                c.act(p[:], S, AF.Exp, [psk(sA), psk(sB), kst], [kp, kst],
                      bias=s_[:, 1:2], accum_out=s_[:, 2:3])
                k.op("dve", lambda: nc.vector.reciprocal(s_[:, 3:4], s_[:, 2:3]), [kst], [kst])

            def p2(idx):
                nonlocal n_og
                r, hh = items[idx]
                rs = min(max(r - 4, 0), NR - 8)
                hs = slice(64 * hh, 64 * hh + 64)
                i2 = idx % 2
                tbk = 4 + i2
                s_ = st[i2]
                kst = ("sa_st", i2)
                p = pb[i2]
                kp = ("sa_p", i2)

                def f():
                    ins = None
                    for kb in range(8):
                        ins = nc.tensor.transpose(PS[tbk][:, kb * 64:(kb + 1) * 64],
                                                  p[:, kb * 128:(kb + 1) * 128], ident[0:64, 0:64])
                    return ins
                k.op("pe", f, [kp, "ident"], [psk(tbk)])
                pt = ptb[i2]
                kpt = ("sa_pt", i2)
                c.cp("dve" if i2 else "act", pt[:], PS[tbk][:].rearrange("p (b q) -> p b q", b=8),
                     [psk(tbk)], [kpt])
                vloc = vt if rs % 2 == 0 else vt2
                pairs = [(pt[:, kb, :], vloc[:, rs // 2 + kb, hs]) for kb in range(4)]
                pairs += [(pt[:, 4 + kb, :], vc[:, kb, hs]) for kb in range(4)]
                c.mm(PS[7][0:64, hh * 64:(hh + 1) * 64], pairs, [kpt, "sa_v", "sa_vc"], [psk(7)])
                o = osb[r % 2]
                ko = ("sa_os", r % 2)
                c.act(o[:, hh * 64:(hh + 1) * 64], PS[7][0:64, hh * 64:(hh + 1) * 64], AF.Copy,
                      [psk(7), kst], [ko], scale=s_[:, 3:4])
                if hh == 1:
                    cc = (r % 8) * 64
                    k.op("pe", lambda: nc.tensor.transpose(PS[obk][:, cc:cc + 64], o[:], ident[0:64, 0:64]),
                         [ko, "ident"], [psk(obk)])
                    if r % 8 == 7:
                        r0 = r - 7
                        ogt = og[n_og % 2]
                        kog = ("sa_o", n_og % 2)
                        n_og += 1
                        c.cp("act", ogt[:], PS[obk][:], [psk(obk)], [kog])
                        k.dma("sp", MIX[j * 128:(j + 1) * 128, TP + r0 * 64:TP + (r0 + 8) * 64], ogt[:],
                              reads=[kog], writes=[("MIX", "a")])

            p1(0)
            for idx in range(1, len(items)):
                p1(idx)
                p2(idx - 1)
            p2(len(items) - 1)


def phase_b_ssm(c, l, UT, MIX, ssm_a, ssm_B, ssm_C, st0_d, vecs, wglu_t, nst_out):
    nc, k = c.nc, c.k
    PS, psk, ident = c.PS, c.psk, c.ident
    PI = math.pi
    C1 = 6.28125
    C2 = TWO_PI - C1
    with ExitStack() as ph:
        sb = lambda n, s, d=F32: c.sb(n, s, d, ph)
        utc = sb("ss_ut", [128, T])
        yTc = sb("ss_yT", [128, T])
        Bl = sb("ss_Bl", [128, 32, 2, 128])
        Cl = sb("ss_Cl", [128, 2, 32, 128])
        pa = sb("ss_pa", [128, 3, 32])
        Bst = sb("ss_Bst", [128, 2, 32, 16])
        Bb = sb("ss_Bb", [128, 2, 32, 16])
        h0 = sb("ss_h0", [128, 2, 16, 2])
        sm = sb("ss_sm", [128, 24, 32])
        zp = [sb("ss_zp%d" % i, [128, 128]) for i in range(8)]
        iot = sb("ss_iota", [128, LC])
        iot_i = sb("ss_iotai", [128, LC], I32)
        onesr = sb("ss_ones", [128, LC])
        ki = sb("ss_ki", [128, LC], I32)
        wk = [sb("ss_wk%d" % i, [128, LC]) for i in range(4)]
        cosT = sb("ss_cos", [128, LC])
        sinT = sb("ss_sin", [128, LC])
        magT = sb("ss_mag", [128, LC])
        fin = sb("ss_fin", [128, 4, 2, 16, 2])
        vec = sb("ss_vec", [128, 16])
        tmp = {}
        for nm in ("t1", "t2", "t3", "t4", "gr", "gi", "sr", "si", "hr", "hi"):
            tmp[nm] = [sb("ss_%s%d" % (nm, i), [128, LC]) for i in range(2)]
        ini = [sb("ss_ini%d" % i, [128, 4]) for i in range(2)]

        DT_, AR, TH, MAG, COS, SIN, ABR, ABI, NR_, NI_, DEN, CR, CI, RR, RI, I0R, I0I, W0, W1, W2, W3 = range(21)

        k.dma("sp", pa[:], ssm_a[l], writes=["pa"])
        k.dma("sp", Bst[:], ssm_B[l], writes=["Bst"])
        for ri in range(2):
            k.dma("sp", Cl[:, ri], ssm_C[l, ri].rearrange("p (b c) -> p b c", b=32), writes=["Cl"])
        k.dma("sp", h0[:], st0_d[l], writes=["h0"])
        k.dma("sp", vec[:], vecs[l], writes=["vec"])
        for z in zp:
            k.op("pool", lambda: nc.gpsimd.memset(z[:], 0.0), writes=["zp"])
        k.op("pool", lambda: nc.gpsimd.memset(onesr[:], 1.0), writes=["onesr"])
        k.op("pool", lambda: nc.gpsimd.iota(iot_i[:], pattern=[[1, LC]], base=0, channel_multiplier=0),
             writes=["iot_i"])
        c.cp("dve", iot[:], iot_i[:], ["iot_i"], ["iot"])
        c.ts("dve", Cl[:, 1], Cl[:, 1], -1.0, None, ALU.mult, None, ["Cl"], ["Cl"])

        def sincos(phi, kphi, n, s_out, c_out, kout):
            for which, out in ((0, s_out), (1, c_out)):
                a, b = wk[0], wk[1]
                if which == 0:
                    c.cp("dve", a[:, 0:n], phi, [kphi], ["wk0"])
                else:
                    c.ts("dve", a[:, 0:n], phi, PI / 2, None, ALU.add, None, [kphi], ["wk0"])
                c.ts("dve", b[:, 0:n], a[:, 0:n], 1.0 / TWO_PI, None, ALU.mult, None, ["wk0"], ["wk1"])
                c.cp("dve", ki[:, 0:n], b[:, 0:n], ["wk1"], ["ki"])
                c.cp("dve", b[:, 0:n], ki[:, 0:n], ["ki"], ["wk1"])
                c.stt(a[:, 0:n], b[:, 0:n], -C1, a[:, 0:n], ALU.mult, ALU.add, ["wk0", "wk1"], ["wk0"])
                c.stt(a[:, 0:n], b[:, 0:n], -C2, a[:, 0:n], ALU.mult, ALU.add, ["wk0", "wk1"], ["wk0"])
                c.ts("dve", b[:, 0:n], a[:, 0:n], PI, -TWO_PI, ALU.is_gt, ALU.mult, ["wk0"], ["wk1"])
                c.tt("dve", a[:, 0:n], a[:, 0:n], b[:, 0:n], ALU.add, ["wk0", "wk1"], ["wk0"])
                c.ts("dve", b[:, 0:n], a[:, 0:n], -PI, TWO_PI, ALU.is_lt, ALU.mult, ["wk0"], ["wk1"])
                c.tt("dve", a[:, 0:n], a[:, 0:n], b[:, 0:n], ALU.add, ["wk0", "wk1"], ["wk0"])
                c.ts("dve", a[:, 0:n], a[:, 0:n], PI, -PI, ALU.min, ALU.max, ["wk0"], ["wk0"])
                c.act(out, a[:, 0:n], AF.Sin, ["wk0"], [kout])

        S = lambda i: sm[:, i, :]
        c.act(S(DT_), pa[:, 2, :], AF.Exp, ["pa"], ["sm"])
        c.tt("dve", S(AR), pa[:, 0, :], S(DT_), ALU.mult, ["pa", "sm"], ["sm"])
        c.tt("dve", S(TH), pa[:, 1, :], S(DT_), ALU.mult, ["pa", "sm"], ["sm"])
        c.act(S(MAG), S(AR), AF.Exp, ["sm"], ["sm"])
        sincos(S(TH), "sm", 32, S(SIN), S(COS), "sm")
        c.tt("dve", S(ABR), S(MAG), S(COS), ALU.mult, ["sm"], ["sm"])
        c.tt("dve", S(ABI), S(MAG), S(SIN), ALU.mult, ["sm"], ["sm"])
        c.ts("dve", S(W0), S(ABR), -1.0, None, ALU.add, None, ["sm"], ["sm"])
        c.tt("dve", S(NR_), S(W0), pa[:, 0, :], ALU.mult, ["sm", "pa"], ["sm"])
        c.tt("dve", S(W1), S(ABI), pa[:, 1, :], ALU.mult, ["sm", "pa"], ["sm"])
        c.tt("dve", S(NR_), S(NR_), S(W1), ALU.add, ["sm"], ["sm"])
        c.tt("dve", S(NI_), S(ABI), pa[:, 0, :], ALU.mult, ["sm", "pa"], ["sm"])
        c.tt("dve", S(W1), S(W0), pa[:, 1, :], ALU.mult, ["sm", "pa"], ["sm"])
        c.tt("dve", S(NI_), S(NI_), S(W1), ALU.subtract, ["sm"], ["sm"])
        c.tt("dve", S(DEN), pa[:, 0, :], pa[:, 0, :], ALU.mult, ["pa"], ["sm"])
        c.tt("dve", S(W1), pa[:, 1, :], pa[:, 1, :], ALU.mult, ["pa"], ["sm"])
        c.tt("dve", S(DEN), S(DEN), S(W1), ALU.add, ["sm"], ["sm"])
        k.op("dve", lambda: nc.vector.reciprocal(S(DEN), S(DEN)), ["sm"], ["sm"])
        c.tt("dve", S(CR), S(NR_), S(DEN), ALU.mult, ["sm"], ["sm"])
        c.tt("dve", S(CI), S(NI_), S(DEN), ALU.mult, ["sm"], ["sm"])
        c.ts("dve", S(W2), S(TH), float(LC), None, ALU.mult, None, ["sm"], ["sm"])
        sincos(S(W2), "sm", 32, S(RI), S(RR), "sm")
        h0r = h0[:, :, :, 0]
        h0i = h0[:, :, :, 1]
        v3 = lambda i: sm[:, i, :].rearrange("p (d b) -> p d b", d=2)
        c.tt("dve", v3(I0R), v3(COS), h0r, ALU.mult, ["sm", "h0"], ["sm"])
        c.tt("dve", v3(W3), v3(SIN), h0i, ALU.mult, ["sm", "h0"], ["sm"])
        c.tt("dve", S(I0R), S(I0R), S(W3), ALU.subtract, ["sm"], ["sm"])
        c.tt("dve", v3(I0I), v3(COS), h0i, ALU.mult, ["sm", "h0"], ["sm"])
        c.tt("dve", v3(W3), v3(SIN), h0r, ALU.mult, ["sm", "h0"], ["sm"])
        c.tt("dve", S(I0I), S(I0I), S(W3), ALU.add, ["sm"], ["sm"])
        for db in range(32):
            cr, ci = sm[:, CR, db:db + 1], sm[:, CI, db:db + 1]
            c.ts("dve", Bb[:, 0, db, :], Bst[:, 1, db, :], ci, None, ALU.mult, None, ["Bst", "sm"], ["Bb"])
            c.stt(Bb[:, 0, db, :], Bst[:, 0, db, :], cr, Bb[:, 0, db, :], ALU.mult, ALU.subtract,
                  ["Bst", "sm", "Bb"], ["Bb"])
            c.ts("dve", Bb[:, 1, db, :], Bst[:, 1, db, :], cr, None, ALU.mult, None, ["Bst", "sm"], ["Bb"])
            c.stt(Bb[:, 1, db, :], Bst[:, 0, db, :], ci, Bb[:, 1, db, :], ALU.mult, ALU.add,
                  ["Bst", "sm", "Bb"], ["Bb"])
        n_t = 0
        for db in range(32):
            blk = db % 16
            for ri in range(2):
                z = zp[(blk % 4) * 2 + ri]
                kz = ("zp", (blk % 4) * 2 + ri)
                for gl in range(2):
                    c0 = (blk % 4) * 32 + gl * 16
                    c.cp("dve", z[64 * gl:64 * gl + 64, c0:c0 + 16], Bb[64 * gl:64 * gl + 64, ri, db, :],
                         ["Bb"], [kz])
                bank = n_t % 2
                n_t += 1
                k.op("pe", lambda: nc.tensor.transpose(PS[bank][:, 0:128], z[:], ident[:]), [kz, "ident"], [psk(bank)])
                c.cp("act", Bl[:, db, ri, :], PS[bank][:, 0:128], [psk(bank)], ["Bl"])

        it = 0
        GC = 2.0 * math.sqrt(2.0 / math.pi)
        for db_i in range(32):
            cch, d, b4 = db_i // 8, (db_i // 4) % 2, db_i % 4
            blk = 4 * cch + b4
            db = d * 16 + blk
            rev = (d == 1)
            if db_i % 8 == 0:
                k.dma("sp", utc[:], UT[cch * 128:(cch + 1) * 128, :], reads=[("UTd", cch)], writes=["ut"])
                k.op("pool", lambda: nc.gpsimd.memset(yTc[:], 0.0), writes=["yT"])
            c.ts("dve", wk[2][:], iot[:], sm[:, TH, db:db + 1], None, ALU.mult, None, ["iot", "sm"], ["wk2"])
            sincos(wk[2][:], "wk2", LC, sinT[:], cosT[:], "tab")
            c.ts("dve", magT[:], onesr[:], sm[:, MAG, db:db + 1], None, ALU.mult, None, ["onesr", "sm"], ["magT"])
            chunks = [(s * 256, 256, s, True) for s in range(4)]
            sc = [(TP + cix * LC, LC, None, cix == 0) for cix in range(TS // LC)]
            if rev:
                sc = [(TP + cix * LC, LC, None, cix == TS // LC - 1) for cix in reversed(range(TS // LC))]
            chunks += sc
            state = {"prev": None, "prev_k": None}
            base = it
            it += len(chunks)

            def common(ci):
                off, n, pidx, first = chunks[ci]
                i2 = (base + ci) % 2
                T_ = {nm: tmp[nm][i2] for nm in tmp}
                K_ = {nm: ("ss_" + nm, i2) for nm in tmp}
                rv = (lambda ap: ap[:, ::-1]) if rev else (lambda ap: ap)
                return off, n, pidx, first, i2, T_, K_, rv, 2 * i2, 2 * i2 + 1, 4 + i2

            def s1(ci):
                off, n, pidx, first, i2, T_, K_, rv, bR, bI, bY = common(ci)
                ucols = utc[:, off:off + n]
                c.mm(PS[bR][:, 0:n], [(Bl[:, db, 0, :], ucols)], ["Bl", "ut"], [psk(bR)])
                c.mm(PS[bI][:, 0:n], [(Bl[:, db, 1, :], ucols)], ["Bl", "ut"], [psk(bI)])
                bre, bim = rv(PS[bR][:, 0:n]), rv(PS[bI][:, 0:n])
                cs_, sn_ = cosT[:, 0:n], sinT[:, 0:n]
                c.tt("dve", T_["t1"][:, 0:n], bre, cs_, ALU.mult, [psk(bR), "tab"], [K_["t1"]])
                c.tt("dve", T_["t2"][:, 0:n], bim, sn_, ALU.mult, [psk(bI), "tab"], [K_["t2"]])
                c.tt("dve", T_["t3"][:, 0:n], bim, cs_, ALU.mult, [psk(bI), "tab"], [K_["t3"]])
                c.tt("dve", T_["t4"][:, 0:n], bre, sn_, ALU.mult, [psk(bR), "tab"], [K_["t4"]])
                c.tt("dve", T_["gr"][:, 0:n], T_["t1"][:, 0:n], T_["t2"][:, 0:n], ALU.add,
                     [K_["t1"], K_["t2"]], [K_["gr"]])
                c.tt("dve", T_["gi"][:, 0:n], T_["t3"][:, 0:n], T_["t4"][:, 0:n], ALU.subtract,
                     [K_["t3"], K_["t4"]], [K_["gi"]])

            def s2(ci):
                off, n, pidx, first, i2, T_, K_, rv, bR, bI, bY = common(ci)
                cs_, sn_ = cosT[:, 0:n], sinT[:, 0:n]
                if pidx is not None:
                    ir, ii_ = 0.0, 0.0
                    kin = []
                elif first:
                    ir, ii_ = sm[:, I0R, db:db + 1], sm[:, I0I, db:db + 1]
                    kin = ["sm"]
                else:
                    pr, pi_, pn = state["prev"]
                    prev_k = state["prev_k"]
                    iv = ini[i2]
                    kin = [("ss_ini", i2)]
                    rr, ri_ = sm[:, RR, db:db + 1], sm[:, RI, db:db + 1]
                    c.ts("dve", iv[:, 0:1], pi_[:, pn - 1:pn], ri_, None, ALU.mult, None, [prev_k[1], "sm"], kin)
                    c.stt(iv[:, 1:2], pr[:, pn - 1:pn], rr, iv[:, 0:1], ALU.mult, ALU.subtract,
                          [prev_k[0], "sm"] + kin, kin)
                    c.ts("dve", iv[:, 2:3], pi_[:, pn - 1:pn], rr, None, ALU.mult, None, [prev_k[1], "sm"], kin)
                    c.stt(iv[:, 3:4], pr[:, pn - 1:pn], ri_, iv[:, 2:3], ALU.mult, ALU.add,
                          [prev_k[0], "sm"] + kin, kin)
                    ir, ii_ = iv[:, 1:2], iv[:, 3:4]
                sr, si = T_["sr"], T_["si"]
                k.op("dve", lambda: nc.vector.tensor_tensor_scan(sr[:, 0:n], magT[:, 0:n], T_["gr"][:, 0:n], ir,
                                                                 ALU.mult, ALU.add),
                     ["magT", K_["gr"]] + kin, [K_["sr"]])
                k.op("dve", lambda: nc.vector.tensor_tensor_scan(si[:, 0:n], magT[:, 0:n], T_["gi"][:, 0:n], ii_,
                                                                 ALU.mult, ALU.add),
                     ["magT", K_["gi"]] + kin, [K_["si"]])
                state["prev"] = (sr, si, n)
                state["prev_k"] = (K_["sr"], K_["si"])
                c.tt("dve", T_["t1"][:, 0:n], sr[:, 0:n], cs_, ALU.mult, [K_["sr"], "tab"], [K_["t1"]])
                c.tt("dve", T_["t2"][:, 0:n], si[:, 0:n], sn_, ALU.mult, [K_["si"], "tab"], [K_["t2"]])
                c.tt("dve", T_["t3"][:, 0:n], sr[:, 0:n], sn_, ALU.mult, [K_["sr"], "tab"], [K_["t3"]])
                c.tt("dve", T_["t4"][:, 0:n], si[:, 0:n], cs_, ALU.mult, [K_["si"], "tab"], [K_["t4"]])
                c.tt("dve", T_["hr"][:, 0:n], T_["t1"][:, 0:n], T_["t2"][:, 0:n], ALU.subtract,
                     [K_["t1"], K_["t2"]], [K_["hr"]])
                c.tt("dve", T_["hi"][:, 0:n], T_["t3"][:, 0:n], T_["t4"][:, 0:n], ALU.add,
                     [K_["t3"], K_["t4"]], [K_["hi"]])
                c.mm(PS[bY][:, 0:n], [(Cl[:, 0, db, :], T_["hr"][:, 0:n]), (Cl[:, 1, db, :], T_["hi"][:, 0:n])],
                     ["Cl", K_["hr"], K_["hi"]], [psk(bY)])
                c.tt("dve", yTc[:, off:off + n], yTc[:, off:off + n], rv(PS[bY][:, 0:n]), ALU.add,
                     [psk(bY), "yT"], ["yT"])
                if pidx is not None:
                    c.cp("act", fin[:, pidx, d, blk, 0:1], T_["hr"][:, n - 1:n], [K_["hr"]], ["fin"])
                    c.cp("act", fin[:, pidx, d, blk, 1:2], T_["hi"][:, n - 1:n], [K_["hi"]], ["fin"])

            s1(0)
            for ci in range(1, len(chunks)):
                s1(ci)
                s2(ci - 1)
            s2(len(chunks) - 1)
            if db_i % 8 == 7:
                for t0 in range(0, T, 512):
                    cols = slice(t0, t0 + 512)
                    yv = yTc[:, cols]
                    c.stt(yv, utc[:, cols], vec[:, cch:cch + 1], yv, ALU.mult, ALU.add, ["ut", "vec", "yT"], ["yT"])
                    a, b = wk[0], wk[1]
                    c.tt("dve", a[:, 0:512], yv, yv, ALU.mult, ["yT"], ["wk0"])
                    c.ts("dve", a[:, 0:512], a[:, 0:512], 0.044715, 1.0, ALU.mult, ALU.add, ["wk0"], ["wk0"])
                    c.tt("dve", a[:, 0:512], a[:, 0:512], yv, ALU.mult, ["wk0", "yT"], ["wk0"])
                    c.act(b[:, 0:512], a[:, 0:512], AF.Sigmoid, ["wk0"], ["wk1"], scale=GC)
                    c.tt("dve", yv, yv, b[:, 0:512], ALU.mult, ["yT", "wk1"], ["yT"])
                k.dma("sp", UT[cch * 128:(cch + 1) * 128, :], yTc[:], reads=["yT", "ut"], writes=[("UTd", cch)])
        for s in range(4):
            k.dma("sp", nst_out[s, l].rearrange("d (b g) p r -> (g p) d b r", g=2), fin[:, s],
                  reads=["fin"], writes=["nst"])
    k.barrier()
    with ExitStack() as ph:
        sb = lambda n, s, d=F32: c.sb(n, s, d, ph)
        vec = sb("sg_vec", [128, 16])
        wg = sb("sg_wg", [128, 4, 1024])
        gy = [sb("sg_gy%d" % i, [128, 4, 512]) for i in range(2)]
        wk = [sb("sg_wk%d" % i, [128, 512]) for i in range(2)]
        stg = [sb("sg_st%d" % i, [128, 512], MDT) for i in range(2)]
        k.dma("sp", vec[:], vecs[l], writes=["vec"])
        k.dma("sp", wg[:], wglu_t[l].rearrange("p (k c) -> p k c", k=4), writes=["wg"])
        n_s = 0
        for ti, t0 in enumerate(range(0, T, 512)):
            cols = slice(t0, t0 + 512)
            g = gy[ti % 2]
            kg = ("sg_gy", ti % 2)
            k.dma("sp", g[:], UT[:, cols].rearrange("(k p) t -> p k t", p=128), writes=[kg])
            for cc in range(4):
                bv, bg = cc % 2, 2 + cc % 2
                c.mm(PS[bv][:], [(wg[:, kc, cc * 128:(cc + 1) * 128], g[:, kc, :]) for kc in range(4)],
                     ["wg", kg], [psk(bv)])
                c.mm(PS[bg][:], [(wg[:, kc, 512 + cc * 128:512 + (cc + 1) * 128], g[:, kc, :]) for kc in range(4)],
                     ["wg", kg], [psk(bg)])
                a, b = wk[0], wk[1]
                c.act(a[:], PS[bv][:], AF.Identity, [psk(bv), "vec"], ["wk0"], bias=vec[:, 4 + cc:5 + cc])
                c.act(b[:], PS[bg][:], AF.Sigmoid, [psk(bg), "vec"], ["wk1"], bias=vec[:, 8 + cc:9 + cc])
                s = stg[n_s % 2]
                ks = ("sg_st", n_s % 2)
                n_s += 1
                c.tt("dve", s[:], a[:], b[:], ALU.mult, ["wk0", "wk1"], [ks])
                k.dma("sp", MIX[1024 + cc * 128:1024 + (cc + 1) * 128, cols], s[:], reads=[ks], writes=[("MIX", "s")])


def _prep_shared(inp):
    f = lambda a: np.ascontiguousarray(np.asarray(a), dtype=np.float32)
    L = DEPTH
    sh = {}
    w_mod = f(inp["w_mod"])
    sh["wmod_t"] = np.ascontiguousarray(
        w_mod.reshape(L, 16, 128, 24, 512).transpose(0, 3, 2, 1, 4)).reshape(L, 24, 128, 8192)
    sh["bmodT"] = np.ascontiguousarray(f(inp["b_mod"]).reshape(L, 96, 128).transpose(0, 2, 1))
    gs = [inp["norm1_g"][0], inp["norm1_g"][1], inp["norm2_g"][0], inp["norm2_g"][1], inp["final_norm_g"]]
    sh["gT"] = np.ascontiguousarray(np.stack([f(g).reshape(16, 128).T for g in gs], axis=1))
    w_in = f(inp["w_in"])
    sh["win_t"] = np.ascontiguousarray(
        w_in.reshape(L, 16, 128, 32, 128).transpose(0, 3, 2, 1, 4)).reshape(L, 32, 128, 2048)
    w_out = f(inp["w_out"])
    sh["wout_t"] = np.ascontiguousarray(
        w_out.reshape(L, 16, 128, 16, 128).transpose(0, 3, 2, 1, 4)).reshape(L, 16, 128, 2048)
    fi = f(inp["ffn_w_in"])
    sh["ffin_t"] = np.ascontiguousarray(
        fi.reshape(L, 16, 128, 88, 128).transpose(0, 3, 2, 1, 4)).reshape(L, 88, 128, 2048)
    fo = f(inp["ffn_w_out"])
    sh["ffout_t"] = np.ascontiguousarray(
        fo.reshape(L, NSPLIT, SCH, 128, 16, 128).transpose(0, 1, 4, 3, 2, 5)).reshape(L, NSPLIT, 16, 128, SCH * 128)
    cq = np.arange(64)
    idx = np.clip(cq[None, :] - cq[:, None], -15, 15) + 15
    rpb = f(inp["attn_rpb"])
    tab = rpb[:, :, :, idx]
    sh["rpbtab"] = np.ascontiguousarray(tab.transpose(0, 1, 3, 2, 4)).reshape(L, NH, 64, 15 * 64)
    cs = np.clip(cq - 8, 0, 48)
    valid = (cq[None, :] >= cs[:, None]) & (cq[None, :] < cs[:, None] + 16)
    sh["colmask"] = np.where(valid, 0.0, NEG).astype(np.float32)
    def st_lay(x):
        tail = x.shape[4:]
        y = x.reshape((L, 2, 16, 2, 64) + tail)
        perm = (0, 3, 4, 1, 2) + tuple(range(5, 5 + len(tail)))
        return np.ascontiguousarray(y.transpose(perm)).reshape((L, 128, 32) + tail)
    a_re, a_im = f(inp["ssm_a_re"]), f(inp["ssm_a_im"])
    ldt = np.broadcast_to(f(inp["ssm_log_dt"])[:, :, :, None], (L, 2, 32, 64))
    sh["ssm_a"] = np.ascontiguousarray(np.stack([st_lay(a_re), st_lay(a_im), st_lay(np.ascontiguousarray(ldt))], axis=2))
    sh["ssm_B"] = np.ascontiguousarray(np.stack([st_lay(f(inp["ssm_b_re"])), st_lay(f(inp["ssm_b_im"]))], axis=2))
    C = np.zeros((L, 2, 128, 32, 128), np.float32)
    for ri, key in enumerate(("ssm_c_re", "ssm_c_im")):
        cc = f(inp[key])
        for d in range(2):
            for bk in range(16):
                for gl in range(2):
                    c0 = (bk % 4) * 32 + gl * 16
                    C[:, ri, gl * 64:(gl + 1) * 64, d * 16 + bk, c0:c0 + 16] = \
                        cc[:, d, 2 * bk + gl].transpose(0, 2, 1)
    sh["ssm_C"] = C.reshape(L, 2, 128, 32 * 128)
    vec = np.zeros((L, 128, 16), np.float32)
    vec[:, :, 0:4] = f(inp["ssm_d"]).reshape(L, 4, 128).transpose(0, 2, 1)
    vec[:, :, 4:12] = f(inp["ssm_b_glu"]).reshape(L, 8, 128).transpose(0, 2, 1)
    vec[:, :, 12:16] = f(inp["pool_scale"]).reshape(L, 4, 128).transpose(0, 2, 1)
    sh["vecs"] = vec
    sh["wglu_t"] = np.ascontiguousarray(f(inp["ssm_w_glu"]).reshape(L, 4, 128, 1024).transpose(0, 2, 1, 3)).reshape(L, 128, 4096)
    sh["poolw_t"] = np.ascontiguousarray(f(inp["pool_w"]).transpose(0, 2, 1, 3)).reshape(L, 128, 512)
    ic = np.zeros((4, 256 + TS), np.float32)
    for n, o in ((256, 0), (TS, 256)):
        t = np.arange(n)
        for g in range(4):
            w = 2 << g
            lo = np.clip(t - w // 2, 0, n)
            hi = np.clip(t - w // 2 + w, 0, n)
            ic[g, o:o + n] = 1.0 / (hi - lo)
    sh["invcnt"] = np.ascontiguousarray(np.broadcast_to(ic[None], (128, 4, 256 + TS)))
    sh["ident"] = np.eye(128, dtype=np.float32)
    return sh


def _prep_core(inp, c, sh):
    f = lambda a: np.ascontiguousarray(np.asarray(a), dtype=np.float32)
    b = c // 4
    m = dict(sh)
    m["x_in"] = np.concatenate([f(inp["x_prompt"][4 * c:4 * c + 4]).reshape(TP, D), f(inp["x_sample"][b])], axis=0)
    cond = np.stack([f(inp["c_ctx"]), f(inp["c"][b])], axis=0)
    m["condT"] = np.ascontiguousarray(cond.reshape(2, 16, 128).transpose(2, 1, 0))
    m["ck"] = f(inp["cache_k"][b])
    m["cv"] = f(inp["cache_v"][b])
    st = f(inp["state_ssm"][b])
    m["st0"] = np.ascontiguousarray(
        st.reshape(DEPTH, 2, 16, 2, 64, 2).transpose(0, 3, 4, 1, 2, 5)).reshape(DEPTH, 128, 2, 16, 2)
    return m


_NC_CACHE = {}


def kernel(**inp):
    n = 8
    key = (str(MDT), DEBUG, STOP_AFTER)
    if key not in _NC_CACHE:
        _NC_CACHE[key] = build_program()
    nc = _NC_CACHE[key]
    sh = _prep_shared(inp)
    in_maps = [_prep_core(inp, c, sh) for c in range(n)]
    res = run_bass_kernel_spmd(nc, in_maps, core_ids=list(range(n)))
    R = res.results
    y_prompt = np.concatenate([R[c]["y"][:TP].reshape(4, 256, D) for c in range(n)], axis=0)
    y_sample = np.stack([R[0]["y"][TP:], R[4]["y"][TP:]], axis=0)
    nk = np.concatenate([R[c]["nk"] for c in range(n)], axis=0)
    nv = np.concatenate([R[c]["nv"] for c in range(n)], axis=0)
    nst = np.concatenate([R[c]["nst"] for c in range(n)], axis=0)
    return (np.ascontiguousarray(y_prompt, dtype=np.float32), np.ascontiguousarray(y_sample, dtype=np.float32),
            np.ascontiguousarray(nk, dtype=np.float32), np.ascontiguousarray(nv, dtype=np.float32),
            np.ascontiguousarray(nst, dtype=np.float32))
```

```python
import math
from contextlib import ExitStack

import numpy as np
import concourse.bass as bass
import concourse.mybir as mybir
from concourse.bass_utils import run_bass_kernel_spmd

F32 = mybir.dt.float32
BF16 = mybir.dt.bfloat16
I32 = mybir.dt.int32
AF = mybir.ActivationFunctionType
ALU = mybir.AluOpType
AX = mybir.AxisListType

D = 2048
NCH = 16
DEPTH = 2
NH = 16
HD = 64
TP = 1024
TS = 2048
T = TP + TS
TT = 1024
NT = T // TT
FF = 5632
FCH = 44
NSPLIT = 4
SCH = FCH // NSPLIT
PAST = 512
GW = 64
EPS = 1e-6
NEG = -1e30
LC = 512
TWO_PI = 2.0 * math.pi

MDT = BF16
DEBUG = False
STOP_AFTER = None
BISECT = None


class KB:
    NDMA = 24

    def __init__(self, nc, stack):
        self.nc = nc
        self.engs = {"pe": nc.tensor, "act": nc.scalar, "dve": nc.vector,
                     "pool": nc.gpsimd, "sp": nc.sync}
        self.sem = {}
        self.cnt = {}
        for n in self.engs:
            self.sem[n] = stack.enter_context(nc.semaphore("s_" + n))
            self.cnt[n] = 0
        self.dsem = [stack.enter_context(nc.semaphore("d%d" % i)) for i in range(self.NDMA)]
        self.dcnt = [0] * self.NDMA
        self.dnext = 0
        self.seen = {n: {} for n in self.engs}
        self.bufs = {}
        self.n_ops = 0

    def _semh(self, sk):
        return self.sem[sk] if isinstance(sk, str) else self.dsem[sk[1]]

    def _wait(self, eng, tok):
        sk, v = tok
        if sk == eng and eng in ("pe", "sp"):
            return
        if self.seen[eng].get(sk, 0) >= v:
            return
        self.engs[eng].wait_ge(self._semh(sk), v)
        self.seen[eng][sk] = v

    def _deps(self, eng, reads, writes):
        toks = {}

        def add(t):
            if t is not None and toks.get(t[0], 0) < t[1]:
                toks[t[0]] = t[1]
        for r in reads:
            st = self.bufs.get(r)
            if st:
                add(st[0])
        for w in writes:
            st = self.bufs.get(w)
            if st:
                add(st[0])
                for t in st[1]:
                    add(t)
        for sk, v in toks.items():
            self._wait(eng, (sk, v))

    def _record(self, tok, reads, writes):
        for r in reads:
            st = self.bufs.setdefault(r, [None, []])
            st[1] = [t for t in st[1] if t[0] != tok[0]] + [tok]
        for w in writes:
            self.bufs[w] = [tok, []]

    def op(self, eng, fn, reads=(), writes=()):
        reads, writes = list(reads), list(writes)
        for r in reads:
            if isinstance(r, tuple) and r[0] == "ps" and r not in writes:
                writes.append(r)
        self._deps(eng, reads, writes)
        ins = fn()
        self.cnt[eng] += 1
        ins.then_inc(self.sem[eng], 1)
        tok = (eng, self.cnt[eng])
        self._record(tok, reads, writes)
        self.n_ops += 1
        return tok

    def dma(self, q, out, in_, reads=(), writes=(), **kw):
        i = self.dnext
        self.dnext = (self.dnext + 1) % self.NDMA
        if self.dcnt[i]:
            self._wait(q, (("d", i), 16 * self.dcnt[i]))
        self._deps(q, reads, writes)
        ins = self.engs[q].dma_start(out=out, in_=in_, **kw)
        self.dcnt[i] += 1
        ins.then_inc(self.dsem[i], 16)
        tok = (("d", i), 16 * self.dcnt[i])
        self._record(tok, reads, writes)
        self.n_ops += 1
        return tok

    def barrier(self):
        for e in self.engs:
            for o in self.engs:
                if self.cnt[o]:
                    self._wait(e, (o, self.cnt[o]))
            for i in range(self.NDMA):
                if self.dcnt[i]:
                    self._wait(e, (("d", i), 16 * self.dcnt[i]))
        self.bufs = {}


class WStream:
    def __init__(self, k, bufs, srcs):
        self.k, self.bufs, self.srcs = k, bufs, srcs
        self.nb = len(bufs)
        self.issued = 0

    def get(self, i):
        while self.issued < len(self.srcs) and self.issued <= i + self.nb - 1:
            j = self.issued
            src = self.srcs[j]
            n = src.shape[1]
            dst = self.bufs[j % self.nb][:, 0:n]
            if n > 1024:
                dst = dst.rearrange("p (a b) -> p a b", a=2)
                src = src.rearrange("p (a b) -> p a b", a=2)
            self.k.dma("pool", dst, src, writes=[("wb", j % self.nb)])
            self.issued += 1
        return self.bufs[i % self.nb], ("wb", i % self.nb)


def build_program():
    nc = bass.Bass("TRN2", target_bir_lowering=False)
    dt_in = lambda name, shape, dt=F32: nc.dram_tensor(name, list(shape), dt, kind="ExternalInput").ap()
    dt_out = lambda name, shape, dt=F32: nc.dram_tensor(name, list(shape), dt, kind="ExternalOutput").ap()
    dt_scr = lambda name, shape, dt=F32: (nc.dram_tensor(name, list(shape), dt, kind="ExternalOutput").ap()
                                          if DEBUG else nc.dram_tensor(name, list(shape), dt).ap())

    x_in = dt_in("x_in", [T, D])
    condT = dt_in("condT", [128, NCH, 2])
    ck_d = dt_in("ck", [DEPTH, NH, PAST, HD])
    cv_d = dt_in("cv", [DEPTH, NH, PAST, HD])
    st0_d = dt_in("st0", [DEPTH, 128, 2, 16, 2])
    wmod_t = dt_in("wmod_t", [DEPTH, 24, 128, NCH * 512])
    bmodT = dt_in("bmodT", [DEPTH, 128, 96])
    gT = dt_in("gT", [128, 5, NCH])
    win_t = dt_in("win_t", [DEPTH, 32, 128, NCH * 128])
    wout_t = dt_in("wout_t", [DEPTH, 16, 128, NCH * 128])
    ffin_t = dt_in("ffin_t", [DEPTH, 88, 128, NCH * 128])
    ffout_t = dt_in("ffout_t", [DEPTH, NSPLIT, 16, 128, SCH * 128])
    rpbtab = dt_in("rpbtab", [DEPTH, NH, 64, 15 * 64])
    colmask = dt_in("colmask", [64, 64])
    ssm_a = dt_in("ssm_a", [DEPTH, 128, 3, 32])
    ssm_B = dt_in("ssm_B", [DEPTH, 128, 2, 32, 16])
    ssm_C = dt_in("ssm_C", [DEPTH, 2, 128, 32 * 128])
    vecs = dt_in("vecs", [DEPTH, 128, 16])
    wglu_t = dt_in("wglu_t", [DEPTH, 128, 4 * 1024])
    poolw_t = dt_in("poolw_t", [DEPTH, 128, 4 * 128])
    invcnt = dt_in("invcnt", [128, 4, 256 + TS])
    ident_d = dt_in("ident", [128, 128])

    y_out = dt_out("y", [T, D])
    nk_out = dt_out("nk", [4, DEPTH, NH, 256, HD])
    nv_out = dt_out("nv", [4, DEPTH, NH, 256, HD])
    nst_out = dt_out("nst", [4, DEPTH, 2, 32, 64, 2])

    XT = dt_scr("XT", [D, T])
    QT = dt_scr("QT", [1024, T], MDT)
    KT = dt_scr("KT", [1024, T], MDT)
    VTOK = dt_scr("VTOK", [T, 1024], MDT)
    UT = dt_scr("UT", [512, T])
    PT = dt_scr("PT", [512, T])
    MIX = dt_scr("MIX", [D, T], MDT)

    with ExitStack() as top:
        k = KB(nc, top)
        _uid = [0]

        def sb(name, shape, dt=F32, st=top):
            _uid[0] += 1
            return st.enter_context(nc.sbuf_tensor("sb%d_%s" % (_uid[0], name), list(shape), dt))
        PSB = [top.enter_context(nc.psum_tensor("psb%d" % i, [128, 1024], F32)) for i in range(4)]
        PS = [PSB[i // 2][:, (i % 2) * 512:(i % 2 + 1) * 512] for i in range(8)]
        psk = lambda i: ("ps", i)

        ident = sb("ident", [128, 128])
        ones = sb("ones", [128, 128])
        modv = sb("modv", [128, DEPTH, 6, NCH, 2])
        gsb = sb("gsb", [128, 5, NCH])
        k.dma("sp", ident[:], ident_d, writes=["ident"])
        k.dma("sp", gsb[:], gT, writes=["gsb"])
        k.op("dve", lambda: nc.vector.memset(ones[:], 1.0), writes=["ones"])

        def mm(out, pairs, reads, writes):
            def f():
                n = len(pairs)
                ins = None
                for i, (l, r) in enumerate(pairs):
                    ins = nc.tensor.matmul(out, l, r, start=(i == 0), stop=(i == n - 1))
                return ins
            return k.op("pe", f, reads, writes)

        def act(out, in_, func, reads, writes, **kw):
            return k.op("act", lambda: nc.scalar.activation(out, in_, func, **kw), reads, writes)

        def tt(eng, out, a, b, op, reads, writes):
            e = nc.vector if eng == "dve" else nc.gpsimd
            return k.op(eng, lambda: e.tensor_tensor(out, a, b, op), reads, writes)

        def ts(eng, out, a, s1, s2, op0, op1, reads, writes):
            e = nc.vector if eng == "dve" else nc.gpsimd
            if s2 is None:
                return k.op(eng, lambda: e.tensor_scalar(out, a, s1, None, op0), reads, writes)
            return k.op(eng, lambda: e.tensor_scalar(out, a, s1, s2, op0, op1), reads, writes)

        def stt(out, a, s, b, op0, op1, reads, writes):
            return k.op("dve", lambda: nc.vector.scalar_tensor_tensor(out, a, s, b, op0, op1), reads, writes)

        def cp(eng, out, in_, reads, writes):
            if eng == "act":
                return k.op("act", lambda: nc.scalar.copy(out, in_), reads, writes)
            e = nc.vector if eng == "dve" else nc.gpsimd
            return k.op(eng, lambda: e.tensor_copy(out, in_), reads, writes)

        def stop(name):
            return STOP_AFTER == name

        class Ctx:
            pass
        c = Ctx()
        c.nc, c.k, c.PS, c.PSB, c.ident, c.ones, c.sb = nc, k, PS, PSB, ident, ones, sb
        c.mm, c.act, c.tt, c.ts, c.stt, c.cp, c.psk = mm, act, tt, ts, stt, cp, psk

        with ExitStack() as ph:
            xt4 = [sb("p0_x%d" % i, [128, 4, D], F32, ph) for i in range(2)]
            stg = [sb("p0_s%d" % i, [128, 512], F32, ph) for i in range(4)]
            si = 0
            for g in range(T // 512):
                xb = xt4[g % 2]
                kx = ("p0x", g % 2)
                k.dma("sp", xb[:], x_in[g * 512:(g + 1) * 512, :].rearrange("(b p) f -> p b f", p=128),
                      writes=[kx])
                for fc in range(NCH):
                    bank = fc % 4
                    def f(bank=bank, fc=fc, xb=xb):
                        ins = None
                        for b in range(4):
                            ins = nc.tensor.transpose(PS[bank][:, b * 128:(b + 1) * 128],
                                                      xb[:, b, fc * 128:(fc + 1) * 128], ident[:])
                        return ins
                    k.op("pe", f, reads=[kx, "ident"], writes=[psk(bank)])
                    s = stg[si % 4]
                    ks = ("p0s", si % 4)
                    si += 1
                    cp("act" if fc % 2 else "dve", s[:], PS[bank][:], [psk(bank)], [ks])
                    k.dma("sp", XT[fc * 128:(fc + 1) * 128, g * 512:(g + 1) * 512], s[:],
                          reads=[ks], writes=[("XT", g // 2)])
        k.barrier()

        with ExitStack() as ph:
            cs = sb("m_cs", [128, NCH, 2], F32, ph)
            wb = [sb("m_w%d" % i, [128, NCH * 512], F32, ph) for i in range(2)]
            row = [sb("m_r%d" % i, [2, 512], F32, ph) for i in range(2)]
            bm = sb("m_b", [128, DEPTH, 96], F32, ph)
            modT = sb("m_modT", [128, DEPTH, 96, 2], F32, ph)
            k.dma("sp", cs[:], condT, writes=["cs"])
            k.dma("sp", bm[:], bmodT.rearrange("l p c -> p l c"), writes=["bm"])
            act(cs[:], cs[:], AF.Silu, ["cs"], ["cs"])
            it = 0
            for l in range(DEPTH):
                for n in range(24):
                    w = wb[it % 2]
                    kw = ("mw", it % 2)
                    k.dma("sp" if it % 2 else "act", w[:], wmod_t[l, n], writes=[kw])
                    wv = w[:].rearrange("p (k c) -> p k c", k=NCH)
                    bank = it % 2
                    mm(PS[bank][0:2, :], [(cs[:, kc, :], wv[:, kc, :]) for kc in range(NCH)],
                       [kw, "cs"], [psk(bank)])
                    r = row[it % 2]
                    kr = ("mr", it % 2)
                    cp("act", r[:], PS[bank][0:2, :], [psk(bank)], [kr])
                    tb = 2 + (it % 2)
                    def f(r=r, tb=tb):
                        ins = None
                        for j in range(4):
                            ins = nc.tensor.transpose(PS[tb][:, j * 2:(j + 1) * 2],
                                                      r[0:2, j * 128:(j + 1) * 128], ident[0:2, 0:2])
                        return ins
                    k.op("pe", f, reads=[kr, "ident"], writes=[psk(tb)])
                    pv = PS[tb][:, 0:8].rearrange("p (j c) -> p j c", c=2)
                    for cc in range(2):
                        tt("dve", modT[:, l, n * 4:(n + 1) * 4, cc], pv[:, :, cc],
                           bm[:, l, n * 4:(n + 1) * 4], ALU.add, [psk(tb), "bm"], ["modT"])
                    it += 1
            for l in range(DEPTH):
                for half in range(2):
                    o = half * 48
                    gi = l if half == 0 else 2 + l
                    for cc in range(2):
                        stt(modv[:, l, 3 * half + 0, :, cc], modT[:, l, o + 16:o + 32, cc], 1.0,
                            gsb[:, gi, :], ALU.add, ALU.mult, ["modT", "gsb"], ["modv"])
                        cp("dve", modv[:, l, 3 * half + 1, :, cc], modT[:, l, o:o + 16, cc], ["modT"], ["modv"])
                        cp("dve", modv[:, l, 3 * half + 2, :, cc], modT[:, l, o + 32:o + 48, cc], ["modT"], ["modv"])
        if DEBUG:
            dbg_modv = dt_out("dbg_modv", [128, DEPTH * 6 * NCH * 2])
            k.dma("sp", dbg_modv, modv[:].rearrange("p l s f c -> p (l s f c)"), reads=["modv"], writes=["dbg"])
        k.barrier()
        if stop("M"):
            return finish(nc, k)

        def rms_rstd(xT, kx, rstd, sq, ssb):
            for fc in range(NCH):
                s = sq[fc % 2]
                ksq = ("sq", fc % 2)
                if fc % 2 == 0:
                    act(s[:], xT[:, fc, :], AF.Square, [kx], [ksq])
                else:
                    tt("dve", s[:], xT[:, fc, :], xT[:, fc, :], ALU.mult, [kx], [ksq])
                for hb in range(TT // 512):
                    def f(hb=hb, s=s, fc=fc):
                        return nc.tensor.matmul(PS[ssb + hb][:], ones[:], s[:, hb * 512:(hb + 1) * 512],
                                                start=(fc == 0), stop=(fc == NCH - 1))
                    k.op("pe", f, reads=[ksq, "ones"], writes=[psk(ssb + hb)])
            for hb in range(TT // 512):
                sl = slice(hb * 512, (hb + 1) * 512)
                ts("dve", rstd[:, sl], PS[ssb + hb][:], 1.0 / D, EPS, ALU.mult, ALU.add,
                   [psk(ssb + hb)], ["rstd"])
                k.op("act", lambda sl=sl: nc.scalar.sqrt(rstd[:, sl], rstd[:, sl]), ["rstd"], ["rstd"])
                k.op("dve", lambda sl=sl: nc.vector.reciprocal(rstd[:, sl], rstd[:, sl]), ["rstd"], ["rstd"])

        for l in range(DEPTH):
            last = (l == DEPTH - 1)
            with ExitStack() as ph:
                xT = sb("a_xT", [128, NCH, TT], F32, ph)
                hT = sb("a_hT", [128, NCH, TT], MDT, ph)
                rstd = sb("a_rstd", [128, TT], F32, ph)
                sq = [sb("a_sq%d" % i, [128, TT], F32, ph) for i in range(2)]
                wbufs = [sb("a_w%d" % i, [128, NCH * 128], MDT, ph) for i in range(8)]
                sgm = [sb("a_sm%d" % i, [128, TT], MDT, ph) for i in range(2)]
                sgf = [sb("a_sf%d" % i, [128, TT], F32, ph) for i in range(2)]
                tmf = [sb("a_tf%d" % i, [128, 4, 128], F32, ph) for i in range(2)]
                tmm = [sb("a_tm%d" % i, [128, 4, 128], MDT, ph) for i in range(2)]
                ws = WStream(k, wbufs, [win_t[l, oc] for oc in range(32)] * NT)
                wi = 0
                n_sm = n_sf = n_tm = 0
                for t in range(NT):
                    cond = 0 if t == 0 else 1
                    tok0 = t * TT
                    for fc in range(NCH):
                        k.dma("sp", xT[:, fc, :], XT[fc * 128:(fc + 1) * 128, tok0:tok0 + TT],
                              reads=[("XT", t)], writes=["a_xT"])
                    rms_rstd(xT, "a_xT", rstd, sq, 4)
                    for fc in range(NCH):
                        s = sq[fc % 2]
                        ksq = ("sq", fc % 2)
                        tt("dve", s[:], xT[:, fc, :], rstd[:], ALU.mult, ["a_xT", "rstd"], [ksq])
                        act(hT[:, fc, :], s[:], AF.Identity, [ksq, "modv"], ["a_hT"],
                            scale=modv[:, l, 0, fc, cond:cond + 1], bias=modv[:, l, 1, fc, cond:cond + 1])
                    if BISECT == -1:
                        return finish(nc, k)
                    for oc in range(32):
                        if BISECT is not None and oc == BISECT % 100:
                            return finish(nc, k)
                        w, kw = ws.get(wi)
                        wi += 1
                        wv = w[:].rearrange("p (k c) -> p k c", k=NCH)
                        acc = 2 * (oc % 2)
                        for hb in range(TT // 512):
                            mm(PS[acc + hb][:],
                               [(wv[:, kc, :], hT[:, kc, hb * 512:(hb + 1) * 512]) for kc in range(NCH)],
                               [kw, "a_hT"], [psk(acc + hb)])
                        kind = ("q" if oc < 8 else "k" if oc < 16 else "v" if oc < 24
                                else "u" if oc < 28 else "p")
                        need_f32 = kind in ("u", "p", "v") or (kind == "k" and t == 0)
                        need_m = kind in ("q", "k")
                        pr = [psk(acc), psk(acc + 1)]
                        if need_m:
                            s = sgm[n_sm % 2]
                            ks = ("a_sm", n_sm % 2)
                            n_sm += 1
                            for hb in range(TT // 512):
                                sl = slice(hb * 512, (hb + 1) * 512)
                                if kind == "q":
                                    k.op("act", lambda s=s, sl=sl, hb=hb: nc.scalar.mul(s[:, sl], PS[acc + hb][:], 0.125),
                                         [psk(acc + hb)], [ks])
                                else:
                                    cp("act", s[:, sl], PS[acc + hb][:], [psk(acc + hb)], [ks])
                            dst = QT if kind == "q" else KT
                            r0 = (oc % 8) * 128
                            k.dma("sp", dst[r0:r0 + 128, tok0:tok0 + TT], s[:], reads=[ks],
                                  writes=[("QK", kind, t)])
                        if need_f32:
                            s = sgf[n_sf % 2]
                            ks = ("a_sf", n_sf % 2)
                            n_sf += 1
                            for hb in range(TT // 512):
                                sl = slice(hb * 512, (hb + 1) * 512)
                                cp("dve", s[:, sl], PS[acc + hb][:], [psk(acc + hb)], [ks])
                            if kind in ("u", "p"):
                                dst = UT if kind == "u" else PT
                                r0 = (oc % 4) * 128
                                k.dma("sp", dst[r0:r0 + 128, tok0:tok0 + TT], s[:], reads=[ks],
                                      writes=[("UP", kind, t)])
                            else:
                                ocl = oc % 8
                                for q4 in range(TT // 512):
                                    if BISECT is not None and BISECT >= 200:
                                        continue
                                    tb = 6 + (n_tm % 2)
                                    def f(s=s, q4=q4, tb=tb):
                                        ins = None
                                        for b in range(4):
                                            c0 = q4 * 512 + b * 128
                                            ins = nc.tensor.transpose(PS[tb][:, b * 128:(b + 1) * 128],
                                                                      s[:, c0:c0 + 128], ident[:])
                                        return ins
                                    k.op("pe", f, reads=[ks, "ident"], writes=[psk(tb)])
                                    pv = PS[tb][:].rearrange("p (b f) -> p b f", b=4)
                                    if kind == "v":
                                        m = tmm[n_tm % 2]
                                        km = ("a_tmm", n_tm % 2)
                                        cp("act", m[:], pv, [psk(tb)], [km])
                                        r0 = tok0 + q4 * 512
                                        k.dma("sp", VTOK[r0:r0 + 512, ocl * 128:(ocl + 1) * 128]
                                              .rearrange("(b p) f -> p b f", p=128), m[:],
                                              reads=[km], writes=[("VTOK", t)])
                                    if t == 0:
                                        m = tmf[n_tm % 2]
                                        km = ("a_tmf", n_tm % 2)
                                        cp("dve", m[:], pv, [psk(tb)], [km])
                                        dst = nk_out if kind == "k" else nv_out
                                        for sq_ in range(2):
                                            seq = q4 * 2 + sq_
                                            for hh in range(2):
                                                if BISECT is not None and BISECT >= 100:
                                                    continue
                                                k.dma("sp",
                                                      dst[seq, l, 2 * ocl + hh, :, :].rearrange("(b p) d -> p b d", p=128),
                                                      m[:, 2 * sq_:2 * sq_ + 2, hh * 64:(hh + 1) * 64],
                                                      reads=[km], writes=[("nkv", kind)])
                                    n_tm += 1
            k.barrier()
            if stop("A%d" % l):
                return finish(nc, k)

            phase_b_pool(c, l, PT, MIX, vecs, poolw_t, invcnt)
            k.barrier()
            if stop("Bp%d" % l):
                return finish(nc, k)
            phase_b_attn_prompt(c, l, QT, KT, VTOK, MIX)
            k.barrier()
            if stop("Ba%d" % l):
                return finish(nc, k)
            phase_b_attn_sample(c, l, QT, KT, VTOK, MIX, ck_d, cv_d, rpbtab, colmask)
            k.barrier()
            if stop("Bs%d" % l):
                return finish(nc, k)
            phase_b_ssm(c, l, UT, MIX, ssm_a, ssm_B, ssm_C, st0_d, vecs, wglu_t, nst_out)
            k.barrier()
            if stop("B%d" % l):
                return finish(nc, k)

            with ExitStack() as ph:
                xT = sb("c_xT", [128, NCH, TT], F32, ph)
                mT = sb("c_mT", [128, NCH, TT], MDT, ph)
                hid = sb("c_hid", [128, SCH, TT], MDT, ph)
                rstd = sb("c_rstd", [128, TT], F32, ph)
                sq = [sb("c_sq%d" % i, [128, TT], F32, ph) for i in range(2)]
                wbufs = [sb("c_w%d" % i, [128, NCH * 128], MDT, ph) for i in range(8)]
                srcs = []
                for t in range(NT):
                    srcs += [wout_t[l, oc] for oc in range(16)]
                    for sp_ in range(NSPLIT):
                        for j in range(SCH):
                            srcs += [ffin_t[l, sp_ * SCH + j], ffin_t[l, 44 + sp_ * SCH + j]]
                        srcs += [ffout_t[l, sp_, oc] for oc in range(16)]
                ws = WStream(k, wbufs, srcs)
                wi = 0
                for t in range(NT):
                    cond = 0 if t == 0 else 1
                    tok0 = t * TT
                    for fc in range(NCH):
                        k.dma("sp", xT[:, fc, :], XT[fc * 128:(fc + 1) * 128, tok0:tok0 + TT],
                              reads=[("XT", t)], writes=[("c_xT", fc)])
                        k.dma("sp", mT[:, fc, :], MIX[fc * 128:(fc + 1) * 128, tok0:tok0 + TT],
                              reads=[("MIX", t)], writes=["c_mT"])
                    for oc in range(16):
                        w, kw = ws.get(wi)
                        wi += 1
                        wv = w[:].rearrange("p (k c) -> p k c", k=NCH)
                        acc = 2 * (oc % 4)
                        for hb in range(TT // 512):
                            sl = slice(hb * 512, (hb + 1) * 512)
                            mm(PS[acc + hb][:], [(wv[:, kc, :], mT[:, kc, sl]) for kc in range(NCH)],
                               [kw, "c_mT"], [psk(acc + hb)])
                            stt(xT[:, oc, sl], PS[acc + hb][:], modv[:, l, 2, oc, cond:cond + 1], xT[:, oc, sl],
                                ALU.mult, ALU.add, [psk(acc + hb), ("c_xT", oc), "modv"], [("c_xT", oc)])
                    allx = [("c_xT", fc) for fc in range(NCH)]
                    for fc in range(NCH):
                        s = sq[fc % 2]
                        ksq = ("sq", fc % 2)
                        if fc % 2 == 0:
                            act(s[:], xT[:, fc, :], AF.Square, [("c_xT", fc)], [ksq])
                        else:
                            tt("dve", s[:], xT[:, fc, :], xT[:, fc, :], ALU.mult, [("c_xT", fc)], [ksq])
                        for hb in range(TT // 512):
                            def f(hb=hb, s=s, fc=fc):
                                return nc.tensor.matmul(PS[6 + hb][:], ones[:], s[:, hb * 512:(hb + 1) * 512],
                                                        start=(fc == 0), stop=(fc == NCH - 1))
                            k.op("pe", f, reads=[ksq, "ones"], writes=[psk(6 + hb)])
                    for hb in range(TT // 512):
                        sl = slice(hb * 512, (hb + 1) * 512)
                        ts("dve", rstd[:, sl], PS[6 + hb][:], 1.0 / D, EPS, ALU.mult, ALU.add,
                           [psk(6 + hb)], ["rstd"])
                        k.op("act", lambda sl=sl: nc.scalar.sqrt(rstd[:, sl], rstd[:, sl]), ["rstd"], ["rstd"])
                        k.op("dve", lambda sl=sl: nc.vector.reciprocal(rstd[:, sl], rstd[:, sl]), ["rstd"], ["rstd"])
                    for fc in range(NCH):
                        s = sq[fc % 2]
                        ksq = ("sq", fc % 2)
                        tt("dve", s[:], xT[:, fc, :], rstd[:], ALU.mult, [("c_xT", fc), "rstd"], [ksq])
                        act(mT[:, fc, :], s[:], AF.Identity, [ksq, "modv"], ["c_mT"],
                            scale=modv[:, l, 3, fc, cond:cond + 1], bias=modv[:, l, 4, fc, cond:cond + 1])
                    for sp_ in range(NSPLIT):
                        for j in range(SCH):
                            wg, kwg = ws.get(wi)
                            wgv = wg[:].rearrange("p (k c) -> p k c", k=NCH)
                            pa = 4 * (j % 2)
                            for hb in range(TT // 512):
                                sl = slice(hb * 512, (hb + 1) * 512)
                                mm(PS[pa + hb][:], [(wgv[:, kc, :], mT[:, kc, sl]) for kc in range(NCH)],
                                   [kwg, "c_mT"], [psk(pa + hb)])
                            wu, kwu = ws.get(wi + 1)
                            wi += 2
                            wuv = wu[:].rearrange("p (k c) -> p k c", k=NCH)
                            for hb in range(TT // 512):
                                sl = slice(hb * 512, (hb + 1) * 512)
                                mm(PS[pa + 2 + hb][:], [(wuv[:, kc, :], mT[:, kc, sl]) for kc in range(NCH)],
                                   [kwu, "c_mT"], [psk(pa + 2 + hb)])
                            for hb in range(TT // 512):
                                sl = slice(hb * 512, (hb + 1) * 512)
                                s = sq[hb]
                                ksq = ("sq", hb)
                                act(s[:, 0:512], PS[pa + hb][:], AF.Silu, [psk(pa + hb)], [ksq])
                                tt("dve", hid[:, j, sl], s[:, 0:512], PS[pa + 2 + hb][:], ALU.mult,
                                   [ksq, psk(pa + 2 + hb)], [("hid", j)])
                        hk = [("hid", j) for j in range(SCH)]
                        for oc in range(16):
                            w, kw = ws.get(wi)
                            wi += 1
                            wv = w[:, 0:SCH * 128].rearrange("p (k c) -> p k c", k=SCH)
                            acc = 2 * (oc % 4)
                            for hb in range(TT // 512):
                                sl = slice(hb * 512, (hb + 1) * 512)
                                mm(PS[acc + hb][:], [(wv[:, kc, :], hid[:, kc, sl]) for kc in range(SCH)],
                                   [kw] + hk, [psk(acc + hb)])
                                stt(xT[:, oc, sl], PS[acc + hb][:], modv[:, l, 5, oc, cond:cond + 1], xT[:, oc, sl],
                                    ALU.mult, ALU.add, [psk(acc + hb), ("c_xT", oc), "modv"], [("c_xT", oc)])
                    if not last:
                        for fc in range(NCH):
                            k.dma("sp", XT[fc * 128:(fc + 1) * 128, tok0:tok0 + TT], xT[:, fc, :],
                                  reads=[("c_xT", fc)], writes=[("XT", t)])
                    else:
                        for fc in range(NCH):
                            s = sq[fc % 2]
                            ksq = ("sq", fc % 2)
                            if fc % 2 == 0:
                                act(s[:], xT[:, fc, :], AF.Square, [("c_xT", fc)], [ksq])
                            else:
                                tt("dve", s[:], xT[:, fc, :], xT[:, fc, :], ALU.mult, [("c_xT", fc)], [ksq])
                            for hb in range(TT // 512):
                                def f(hb=hb, s=s, fc=fc):
                                    return nc.tensor.matmul(PS[6 + hb][:], ones[:], s[:, hb * 512:(hb + 1) * 512],
                                                            start=(fc == 0), stop=(fc == NCH - 1))
                                k.op("pe", f, reads=[ksq, "ones"], writes=[psk(6 + hb)])
                        for hb in range(TT // 512):
                            sl = slice(hb * 512, (hb + 1) * 512)
                            ts("dve", rstd[:, sl], PS[6 + hb][:], 1.0 / D, EPS, ALU.mult, ALU.add,
                               [psk(6 + hb)], ["rstd"])
                            k.op("act", lambda sl=sl: nc.scalar.sqrt(rstd[:, sl], rstd[:, sl]), ["rstd"], ["rstd"])
                            k.op("dve", lambda sl=sl: nc.vector.reciprocal(rstd[:, sl], rstd[:, sl]), ["rstd"], ["rstd"])
                        for fc in range(NCH):
                            stt(xT[:, fc, :], xT[:, fc, :], gsb[:, 4, fc:fc + 1], rstd[:], ALU.mult, ALU.mult,
                                [("c_xT", fc), "rstd", "gsb"], [("c_xT", fc)])
                        ytm = [sq[0], sq[1]]
                        it = 0
                        for b in range(TT // 128):
                            for q4 in range(4):
                                tb = (it % 2)
                                def f(b=b, q4=q4, tb=tb):
                                    ins = None
                                    for j in range(4):
                                        fc = q4 * 4 + j
                                        ins = nc.tensor.transpose(PS[tb][:, j * 128:(j + 1) * 128],
                                                                  xT[:, fc, b * 128:(b + 1) * 128], ident[:])
                                    return ins
                                k.op("pe", f, reads=[("c_xT", q4 * 4 + j) for j in range(4)] + ["ident"],
                                     writes=[psk(tb)])
                                s = ytm[it % 2]
                                ksq = ("sq", it % 2)
                                cp("act" if it % 2 else "dve", s[:, 0:512], PS[tb][:], [psk(tb)], [ksq])
                                r0 = tok0 + b * 128
                                k.dma("sp", y_out[r0:r0 + 128, q4 * 512:(q4 + 1) * 512], s[:, 0:512],
                                      reads=[ksq], writes=["y"])
                                it += 1
            k.barrier()
            if stop("C%d" % l):
                return finish(nc, k)
        return finish(nc, k)


def finish(nc, k, **kw):
    k.barrier()
    return nc


ITEMS = [(s * 256, 256, s) for s in range(4)] + [(TP, TS, None)]


def phase_b_pool(c, l, PT, MIX, vecs, poolw_t, invcnt):
    nc, k = c.nc, c.k
    PS, psk = c.PS, c.psk
    with ExitStack() as ph:
        sb = lambda n, s, d=F32: c.sb(n, s, d, ph)
        L = TS + 16
        pp = sb("bp_pp", [128, L])
        sa = sb("bp_sa", [128, L])
        sbb = sb("bp_sb", [128, L])
        mix = sb("bp_mix", [128, TS])
        icn = sb("bp_icn", [128, 4, 256 + TS])
        pw = sb("bp_pw", [128, 4, 128])
        vec = sb("bp_vec", [128, 16])
        stg = [sb("bp_st%d" % i, [128, 512], MDT) for i in range(2)]
        k.dma("sp", icn[:], invcnt, writes=["icn"])
        k.dma("sp", pw[:], poolw_t[l].rearrange("p (g d) -> p g d", g=4), writes=["pw"])
        k.dma("sp", vec[:], vecs[l], writes=["vec"])
        k.op("dve", lambda: nc.vector.memset(pp[:, 0:8], 0.0), writes=["pp"])
        n_st = 0
        for g in range(4):
            w = 2 << g
            for (off, n, pidx) in ITEMS:
                Ln = n + 16
                ioff = 0 if pidx is not None else 256
                k.op("dve", lambda: nc.vector.memset(pp[:, 8 + n:16 + n], 0.0), writes=["pp"])
                k.dma("sp", pp[:, 8:8 + n], PT[g * 128:(g + 1) * 128, off:off + n], writes=["pp"])
                c.tt("dve", sa[:, 0:Ln - 1], pp[:, 0:Ln - 1], pp[:, 1:Ln], ALU.add, ["pp"], ["sa"])
                src, ks, start = sa, "sa", 7
                if w >= 4:
                    c.tt("dve", sbb[:, 0:Ln - 3], sa[:, 0:Ln - 3], sa[:, 2:Ln - 1], ALU.add, ["sa"], ["sb"])
                    src, ks, start = sbb, "sb", 6
                if w >= 8:
                    c.tt("dve", sa[:, 0:Ln - 7], sbb[:, 0:Ln - 7], sbb[:, 4:Ln - 3], ALU.add, ["sb"], ["sa"])
                    src, ks, start = sa, "sa", 4
                if w >= 16:
                    c.tt("dve", sbb[:, 0:Ln - 15], sa[:, 0:Ln - 15], sa[:, 8:Ln - 7], ALU.add, ["sa"], ["sb"])
                    src, ks, start = sbb, "sb", 0
                c.tt("dve", mix[:, 0:n], src[:, start:start + n], icn[:, g, ioff:ioff + n], ALU.mult,
                     [ks, "icn"], ["mix"])
                c.tt("dve", mix[:, 0:n], mix[:, 0:n], pp[:, 8:8 + n], ALU.subtract, ["mix", "pp"], ["mix"])
                for c0 in range(0, n, 512):
                    cw = min(512, n - c0)
                    bank = n_st % 2
                    c.mm(PS[bank][:, 0:cw], [(pw[:, g, :], mix[:, c0:c0 + cw])], ["pw", "mix"], [psk(bank)])
                    s = stg[n_st % 2]
                    kst = ("bp_st", n_st % 2)
                    n_st += 1
                    c.act(s[:, 0:cw], PS[bank][:, 0:cw], AF.Copy, [psk(bank), "vec"], [kst],
                          scale=vec[:, 12 + g:13 + g])
                    k.dma("sp", MIX[1536 + g * 128:1536 + (g + 1) * 128, off + c0:off + c0 + cw], s[:, 0:cw],
                          reads=[kst], writes=[("MIX", "p")])


def phase_b_attn_prompt(c, l, QT, KT, VTOK, MIX):
    nc, k = c.nc, c.k
    PS, psk, ident = c.PS, c.psk, c.ident
    with ExitStack() as ph:
        sb = lambda n, s, d=F32: c.sb(n, s, d, ph)
        qt = [sb("pa_q%d" % i, [128, 256], MDT) for i in range(2)]
        kt = [sb("pa_k%d" % i, [128, 256], MDT) for i in range(2)]
        vt = [sb("pa_v%d" % i, [128, 2, 128], MDT) for i in range(2)]
        pb = [sb("pa_p%d" % i, [128, 256]) for i in range(2)]
        ptb = [sb("pa_pt%d" % i, [128, 2, 128], MDT) for i in range(2)]
        st = [sb("pa_st%d" % i, [128, 4]) for i in range(2)]
        og = [sb("pa_o%d" % i, [128, 256], MDT) for i in range(2)]
        items = []
        for s in range(4):
            for j in range(8):
                for hh in range(2):
                    for qb in range(2):
                        items.append((s, j, hh, qb))

        def bufs_of(idx):
            b = (idx // 4) % 2
            return b, qt[b], kt[b], vt[b], ("pa_q", b), ("pa_k", b), ("pa_v", b)

        def p1(idx):
            s, j, hh, qb = items[idx]
            b, q, kk, v, kq, kkk, kv = bufs_of(idx)
            if idx % 4 == 0:
                k.dma("sp", q[:], QT[j * 128:(j + 1) * 128, s * 256:(s + 1) * 256], writes=[kq])
                k.dma("sp", kk[:], KT[j * 128:(j + 1) * 128, s * 256:(s + 1) * 256], writes=[kkk])
                k.dma("sp", v[:], VTOK[s * 256:(s + 1) * 256, j * 128:(j + 1) * 128]
                      .rearrange("(b p) f -> p b f", p=128), writes=[kv])
            hs = slice(64 * hh, 64 * hh + 64)
            i2 = idx % 2
            sbk = i2
            c.mm(PS[sbk][:, 0:256], [(q[hs, qb * 128:(qb + 1) * 128], kk[hs, :])], [kq, kkk], [psk(sbk)])
            stt_ = st[i2]
            kst = ("pa_st", i2)
            k.op("dve", lambda: nc.vector.reduce_max(stt_[:, 0:1], PS[sbk][:, 0:256], AX.X), [psk(sbk)], [kst])
            c.ts("dve", stt_[:, 1:2], stt_[:, 0:1], -1.0, None, ALU.mult, None, [kst], [kst])
            p = pb[i2]
            kp = ("pa_p", i2)
            c.act(p[:], PS[sbk][:, 0:256], AF.Exp, [psk(sbk), kst], [kp, kst],
                  bias=stt_[:, 1:2], accum_out=stt_[:, 2:3])
            k.op("dve", lambda: nc.vector.reciprocal(stt_[:, 3:4], stt_[:, 2:3]), [kst], [kst])
            c.ts("dve", p[:], p[:], stt_[:, 3:4], None, ALU.mult, None, [kp, kst], [kp])

        def p2(idx):
            s, j, hh, qb = items[idx]
            b, q, kk, v, kq, kkk, kv = bufs_of(idx)
            hs = slice(64 * hh, 64 * hh + 64)
            i2 = idx % 2
            tbk = 2 + i2
            ob = 4 + b
            p = pb[i2]
            kp = ("pa_p", i2)

            def f():
                ins = None
                for kb in range(2):
                    ins = nc.tensor.transpose(PS[tbk][:, kb * 128:(kb + 1) * 128],
                                              p[:, kb * 128:(kb + 1) * 128], ident[:])
                return ins
            k.op("pe", f, [kp, "ident"], [psk(tbk)])
            pt = ptb[i2]
            kpt = ("pa_pt", i2)
            c.cp("act", pt[:], PS[tbk][:, 0:256].rearrange("p (b q) -> p b q", b=2), [psk(tbk)], [kpt])
            c.mm(PS[ob][hs, qb * 128:(qb + 1) * 128],
                 [(v[:, kb, hs], pt[:, kb, :]) for kb in range(2)], [kv, kpt], [psk(ob)])
            if idx % 4 == 3:
                o = og[b]
                ko = ("pa_o", b)
                c.cp("dve", o[:], PS[ob][:, 0:256], [psk(ob)], [ko])
                k.dma("sp", MIX[j * 128:(j + 1) * 128, s * 256:(s + 1) * 256], o[:], reads=[ko],
                      writes=[("MIX", "a")])

        p1(0)
        for idx in range(1, len(items)):
            p1(idx)
            p2(idx - 1)
        p2(len(items) - 1)


def phase_b_attn_sample(c, l, QT, KT, VTOK, MIX, ck_d, cv_d, rpbtab, colmask):
    nc, k = c.nc, c.k
    PS, psk, ident = c.PS, c.psk, c.ident
    NR = TS // GW
    with ExitStack() as ph:
        sb = lambda n, s, d=F32: c.sb(n, s, d, ph)
        qt = sb("sa_q", [128, TS], MDT)
        kt = sb("sa_k", [128, TS], MDT)
        vt = sb("sa_v", [128, 16, 128], MDT)
        vt2 = sb("sa_v2", [128, 15, 128], MDT)
        ckt = sb("sa_ck", [128, 4, 128])
        kcT = sb("sa_kcT", [128, PAST], MDT)
        vc = sb("sa_vc", [128, 4, 128], MDT)
        tab = sb("sa_tab", [128, 15, 64])
        thi = sb("sa_thi", [128, 15 * 64], MDT)
        tlo = sb("sa_tlo", [128, 15 * 64], MDT)
        cm = sb("sa_cm", [128, 64])
        idm = sb("sa_idm", [128, 128], MDT)
        pb = [sb("sa_p%d" % i, [64, 1024]) for i in range(2)]
        ptb = [sb("sa_pt%d" % i, [128, 8, 64], MDT) for i in range(2)]
        st = [sb("sa_st%d" % i, [64, 4]) for i in range(2)]
        osb = [sb("sa_os%d" % i, [64, 128]) for i in range(2)]
        og = [sb("sa_o%d" % i, [128, 512], MDT) for i in range(2)]
        k.dma("sp", cm[0:64, :], colmask, writes=["cm"])
        k.dma("sp", cm[64:128, :], colmask, writes=["cm"])
        c.cp("dve", idm[:], ident[:], ["ident"], ["idm"])
        it = 0
        n_og = 0
        for j in range(8):
            k.dma("sp", qt[:], QT[j * 128:(j + 1) * 128, TP:T], writes=["sa_q"])
            k.dma("sp", kt[:], KT[j * 128:(j + 1) * 128, TP:T], writes=["sa_k"])
            k.dma("sp", vt[:], VTOK[TP:T, j * 128:(j + 1) * 128].rearrange("(b p) f -> p b f", p=128),
                  writes=["sa_v"])
            k.dma("sp", vt2[:], VTOK[TP + 64:T - 64, j * 128:(j + 1) * 128].rearrange("(b p) f -> p b f", p=128),
                  writes=["sa_v"])
            for hh in range(2):
                k.dma("sp", ckt[:, :, 64 * hh:64 * hh + 64],
                      ck_d[l, 2 * j + hh].rearrange("(b p) d -> p b d", p=128), writes=["sa_ck"])
                k.dma("pool", vc[:, :, 64 * hh:64 * hh + 64],
                      cv_d[l, 2 * j + hh].rearrange("(b p) d -> p b d", p=128), writes=["sa_vc"])
                k.dma("sp", tab[64 * hh:64 * hh + 64, :, :],
                      rpbtab[l, 2 * j + hh].rearrange("q (r c) -> q r c", r=15), writes=["sa_tab"])

            def f():
                ins = None
                for b in range(4):
                    ins = nc.tensor.transpose(PS[7][:, b * 128:(b + 1) * 128], ckt[:, b, :], ident[:])
                return ins
            k.op("pe", f, ["sa_ck", "ident"], [psk(7)])
            c.cp("act", kcT[:], PS[7][:], [psk(7)], ["sa_kcT"])
            for dr in range(15):
                c.tt("dve", tab[:, dr, :], tab[:, dr, :], cm[:], ALU.add, ["sa_tab", "cm"], ["sa_tab"])
            tabf = tab[:].rearrange("p r c -> p (r c)")
            c.cp("dve", thi[:], tabf, ["sa_tab"], ["sa_thi"])
            c.tt("dve", tabf, tabf, thi[:], ALU.subtract, ["sa_tab", "sa_thi"], ["sa_tab"])
            c.cp("dve", tlo[:], tabf, ["sa_tab"], ["sa_tlo"])
            obk = 6
            items = [(r, hh) for r in range(NR) for hh in range(2)]
            ctxs = {}

            def p1(idx):
                r, hh = items[idx]
                rs = min(max(r - 4, 0), NR - 8)
                d0 = rs - r + 7
                hs = slice(64 * hh, 64 * hh + 64)
                i2 = idx % 2
                sA, sB = 2 * i2, 2 * i2 + 1
                qs = qt[hs, r * 64:(r + 1) * 64]
                c.mm(PS[sA][0:64, :], [(qs, kt[hs, rs * 64:(rs + 8) * 64]),
                                        (idm[hs, hs], thi[hs, d0 * 64:(d0 + 8) * 64]),
                                        (idm[hs, hs], tlo[hs, d0 * 64:(d0 + 8) * 64])],
                     ["sa_q", "sa_k", "sa_thi", "sa_tlo", "idm"], [psk(sA)])
                c.mm(PS[sB][0:64, :], [(qs, kcT[hs, :])], ["sa_q", "sa_kcT"], [psk(sB)])
                S = c.PSB[i2][0:64, :]
                s_ = st[i2]
                kst = ("sa_st", i2)
                k.op("dve", lambda: nc.vector.reduce_max(s_[:, 0:1], S, AX.X), [psk(sA), psk(sB)], [kst])
                c.ts("dve", s_[:, 1:2], s_[:, 0:1], -1.0, None, ALU.mult, None, [kst], [kst])
                p = pb[i2]
                kp = ("sa_p", i2)
                c.act(p[:], S, AF.Exp, [psk(sA), psk(sB), kst], [kp, kst],
                      bias=s_[:, 1:2], accum_out=s_[:, 2:3])
                k.op("dve", lambda: nc.vector.reciprocal(s_[:, 3:4], s_[:, 2:3]), [kst], [kst])

            def p2(idx):
                nonlocal n_og
                r, hh = items[idx]
                rs = min(max(r - 4, 0), NR - 8)
                hs = slice(64 * hh, 64 * hh + 64)
                i2 = idx % 2
                tbk = 4 + i2
                s_ = st[i2]
                kst = ("sa_st", i2)
                p = pb[i2]
                kp = ("sa_p", i2)

                def f():
                    ins = None
                    for kb in range(8):
                        ins = nc.tensor.transpose(PS[tbk][:, kb * 64:(kb + 1) * 64],
                                                  p[:, kb * 128:(kb + 1) * 128], ident[0:64, 0:64])
                    return ins
                k.op("pe", f, [kp, "ident"], [psk(tbk)])
                pt = ptb[i2]
                kpt = ("sa_pt", i2)
                c.cp("dve" if i2 else "act", pt[:], PS[tbk][:].rearrange("p (b q) -> p b q", b=8),
                     [psk(tbk)], [kpt])
                vloc = vt if rs % 2 == 0 else vt2
                pairs = [(pt[:, kb, :], vloc[:, rs // 2 + kb, hs]) for kb in range(4)]
                pairs += [(pt[:, 4 + kb, :], vc[:, kb, hs]) for kb in range(4)]
                c.mm(PS[7][0:64, hh * 64:(hh + 1) * 64], pairs, [kpt, "sa_v", "sa_vc"], [psk(7)])
                o = osb[r % 2]
                ko = ("sa_os", r % 2)
                c.act(o[:, hh * 64:(hh + 1) * 64], PS[7][0:64, hh * 64:(hh + 1) * 64], AF.Copy,
                      [psk(7), kst], [ko], scale=s_[:, 3:4])
                if hh == 1:
                    cc = (r % 8) * 64
                    k.op("pe", lambda: nc.tensor.transpose(PS[obk][:, cc:cc + 64], o[:], ident[0:64, 0:64]),
                         [ko, "ident"], [psk(obk)])
                    if r % 8 == 7:
                        r0 = r - 7
                        ogt = og[n_og % 2]
                        kog = ("sa_o", n_og % 2)
                        n_og += 1
                        c.cp("act", ogt[:], PS[obk][:], [psk(obk)], [kog])
                        k.dma("sp", MIX[j * 128:(j + 1) * 128, TP + r0 * 64:TP + (r0 + 8) * 64], ogt[:],
                              reads=[kog], writes=[("MIX", "a")])

            p1(0)
            for idx in range(1, len(items)):
                p1(idx)
                p2(idx - 1)
            p2(len(items) - 1)


def phase_b_ssm(c, l, UT, MIX, ssm_a, ssm_B, ssm_C, st0_d, vecs, wglu_t, nst_out):
    nc, k = c.nc, c.k
    PS, psk, ident = c.PS, c.psk, c.ident
    PI = math.pi
    C1 = 6.28125
    C2 = TWO_PI - C1
    with ExitStack() as ph:
        sb = lambda n, s, d=F32: c.sb(n, s, d, ph)
        utc = sb("ss_ut", [128, T])
        yTc = sb("ss_yT", [128, T])
        Bl = sb("ss_Bl", [128, 32, 2, 128])
        Cl = sb("ss_Cl", [128, 2, 32, 128])
        pa = sb("ss_pa", [128, 3, 32])
        Bst = sb("ss_Bst", [128, 2, 32, 16])
        Bb = sb("ss_Bb", [128, 2, 32, 16])
        h0 = sb("ss_h0", [128, 2, 16, 2])
        sm = sb("ss_sm", [128, 24, 32])
        zp = [sb("ss_zp%d" % i, [128, 128]) for i in range(8)]
        iot = sb("ss_iota", [128, LC])
        iot_i = sb("ss_iotai", [128, LC], I32)
        onesr = sb("ss_ones", [128, LC])
        ki = sb("ss_ki", [128, LC], I32)
        wk = [sb("ss_wk%d" % i, [128, LC]) for i in range(4)]
        cosT = sb("ss_cos", [128, LC])
        sinT = sb("ss_sin", [128, LC])
        magT = sb("ss_mag", [128, LC])
        fin = sb("ss_fin", [128, 4, 2, 16, 2])
        vec = sb("ss_vec", [128, 16])
        tmp = {}
        for nm in ("t1", "t2", "t3", "t4", "gr", "gi", "sr", "si", "hr", "hi"):
            tmp[nm] = [sb("ss_%s%d" % (nm, i), [128, LC]) for i in range(2)]
        ini = [sb("ss_ini%d" % i, [128, 4]) for i in range(2)]

        DT_, AR, TH, MAG, COS, SIN, ABR, ABI, NR_, NI_, DEN, CR, CI, RR, RI, I0R, I0I, W0, W1, W2, W3 = range(21)

        k.dma("sp", pa[:], ssm_a[l], writes=["pa"])
        k.dma("sp", Bst[:], ssm_B[l], writes=["Bst"])
        for ri in range(2):
            k.dma("sp", Cl[:, ri], ssm_C[l, ri].rearrange("p (b c) -> p b c", b=32), writes=["Cl"])
        k.dma("sp", h0[:], st0_d[l], writes=["h0"])
        k.dma("sp", vec[:], vecs[l], writes=["vec"])
        for z in zp:
            k.op("pool", lambda: nc.gpsimd.memset(z[:], 0.0), writes=["zp"])
        k.op("pool", lambda: nc.gpsimd.memset(onesr[:], 1.0), writes=["onesr"])
        k.op("pool", lambda: nc.gpsimd.iota(iot_i[:], pattern=[[1, LC]], base=0, channel_multiplier=0),
             writes=["iot_i"])
        c.cp("dve", iot[:], iot_i[:], ["iot_i"], ["iot"])
        c.ts("dve", Cl[:, 1], Cl[:, 1], -1.0, None, ALU.mult, None, ["Cl"], ["Cl"])

        def sincos(phi, kphi, n, s_out, c_out, kout):
            for which, out in ((0, s_out), (1, c_out)):
                a, b = wk[0], wk[1]
                if which == 0:
                    c.cp("dve", a[:, 0:n], phi, [kphi], ["wk0"])
                else:
                    c.ts("dve", a[:, 0:n], phi, PI / 2, None, ALU.add, None, [kphi], ["wk0"])
                c.ts("dve", b[:, 0:n], a[:, 0:n], 1.0 / TWO_PI, None, ALU.mult, None, ["wk0"], ["wk1"])
                c.cp("dve", ki[:, 0:n], b[:, 0:n], ["wk1"], ["ki"])
                c.cp("dve", b[:, 0:n], ki[:, 0:n], ["ki"], ["wk1"])
                c.stt(a[:, 0:n], b[:, 0:n], -C1, a[:, 0:n], ALU.mult, ALU.add, ["wk0", "wk1"], ["wk0"])
                c.stt(a[:, 0:n], b[:, 0:n], -C2, a[:, 0:n], ALU.mult, ALU.add, ["wk0", "wk1"], ["wk0"])
                c.ts("dve", b[:, 0:n], a[:, 0:n], PI, -TWO_PI, ALU.is_gt, ALU.mult, ["wk0"], ["wk1"])
                c.tt("dve", a[:, 0:n], a[:, 0:n], b[:, 0:n], ALU.add, ["wk0", "wk1"], ["wk0"])
                c.ts("dve", b[:, 0:n], a[:, 0:n], -PI, TWO_PI, ALU.is_lt, ALU.mult, ["wk0"], ["wk1"])
                c.tt("dve", a[:, 0:n], a[:, 0:n], b[:, 0:n], ALU.add, ["wk0", "wk1"], ["wk0"])
                c.ts("dve", a[:, 0:n], a[:, 0:n], PI, -PI, ALU.min, ALU.max, ["wk0"], ["wk0"])
                c.act(out, a[:, 0:n], AF.Sin, ["wk0"], [kout])

        S = lambda i: sm[:, i, :]
        c.act(S(DT_), pa[:, 2, :], AF.Exp, ["pa"], ["sm"])
        c.tt("dve", S(AR), pa[:, 0, :], S(DT_), ALU.mult, ["pa", "sm"], ["sm"])
        c.tt("dve", S(TH), pa[:, 1, :], S(DT_), ALU.mult, ["pa", "sm"], ["sm"])
        c.act(S(MAG), S(AR), AF.Exp, ["sm"], ["sm"])
        sincos(S(TH), "sm", 32, S(SIN), S(COS), "sm")
        c.tt("dve", S(ABR), S(MAG), S(COS), ALU.mult, ["sm"], ["sm"])
        c.tt("dve", S(ABI), S(MAG), S(SIN), ALU.mult, ["sm"], ["sm"])
        c.ts("dve", S(W0), S(ABR), -1.0, None, ALU.add, None, ["sm"], ["sm"])
        c.tt("dve", S(NR_), S(W0), pa[:, 0, :], ALU.mult, ["sm", "pa"], ["sm"])
        c.tt("dve", S(W1), S(ABI), pa[:, 1, :], ALU.mult, ["sm", "pa"], ["sm"])
        c.tt("dve", S(NR_), S(NR_), S(W1), ALU.add, ["sm"], ["sm"])
        c.tt("dve", S(NI_), S(ABI), pa[:, 0, :], ALU.mult, ["sm", "pa"], ["sm"])
        c.tt("dve", S(W1), S(W0), pa[:, 1, :], ALU.mult, ["sm", "pa"], ["sm"])
        c.tt("dve", S(NI_), S(NI_), S(W1), ALU.subtract, ["sm"], ["sm"])
        c.tt("dve", S(DEN), pa[:, 0, :], pa[:, 0, :], ALU.mult, ["pa"], ["sm"])
        c.tt("dve", S(W1), pa[:, 1, :], pa[:, 1, :], ALU.mult, ["pa"], ["sm"])
        c.tt("dve", S(DEN), S(DEN), S(W1), ALU.add, ["sm"], ["sm"])
        k.op("dve", lambda: nc.vector.reciprocal(S(DEN), S(DEN)), ["sm"], ["sm"])
        c.tt("dve", S(CR), S(NR_), S(DEN), ALU.mult, ["sm"], ["sm"])
        c.tt("dve", S(CI), S(NI_), S(DEN), ALU.mult, ["sm"], ["sm"])
        c.ts("dve", S(W2), S(TH), float(LC), None, ALU.mult, None, ["sm"], ["sm"])
        sincos(S(W2), "sm", 32, S(RI), S(RR), "sm")
        h0r = h0[:, :, :, 0]
        h0i = h0[:, :, :, 1]
        v3 = lambda i: sm[:, i, :].rearrange("p (d b) -> p d b", d=2)
        c.tt("dve", v3(I0R), v3(COS), h0r, ALU.mult, ["sm", "h0"], ["sm"])
        c.tt("dve", v3(W3), v3(SIN), h0i, ALU.mult, ["sm", "h0"], ["sm"])
        c.tt("dve", S(I0R), S(I0R), S(W3), ALU.subtract, ["sm"], ["sm"])
        c.tt("dve", v3(I0I), v3(COS), h0i, ALU.mult, ["sm", "h0"], ["sm"])
        c.tt("dve", v3(W3), v3(SIN), h0r, ALU.mult, ["sm", "h0"], ["sm"])
        c.tt("dve", S(I0I), S(I0I), S(W3), ALU.add, ["sm"], ["sm"])
        for db in range(32):
            cr, ci = sm[:, CR, db:db + 1], sm[:, CI, db:db + 1]
            c.ts("dve", Bb[:, 0, db, :], Bst[:, 1, db, :], ci, None, ALU.mult, None, ["Bst", "sm"], ["Bb"])
            c.stt(Bb[:, 0, db, :], Bst[:, 0, db, :], cr, Bb[:, 0, db, :], ALU.mult, ALU.subtract,
                  ["Bst", "sm", "Bb"], ["Bb"])
            c.ts("dve", Bb[:, 1, db, :], Bst[:, 1, db, :], cr, None, ALU.mult, None, ["Bst", "sm"], ["Bb"])
            c.stt(Bb[:, 1, db, :], Bst[:, 0, db, :], ci, Bb[:, 1, db, :], ALU.mult, ALU.add,
                  ["Bst", "sm", "Bb"], ["Bb"])
        n_t = 0
        for db in range(32):
            blk = db % 16
            for ri in range(2):
                z = zp[(blk % 4) * 2 + ri]
                kz = ("zp", (blk % 4) * 2 + ri)
                for gl in range(2):
                    c0 = (blk % 4) * 32 + gl * 16
                    c.cp("dve", z[64 * gl:64 * gl + 64, c0:c0 + 16], Bb[64 * gl:64 * gl + 64, ri, db, :],
                         ["Bb"], [kz])
                bank = n_t % 2
                n_t += 1
                k.op("pe", lambda: nc.tensor.transpose(PS[bank][:, 0:128], z[:], ident[:]), [kz, "ident"], [psk(bank)])
                c.cp("act", Bl[:, db, ri, :], PS[bank][:, 0:128], [psk(bank)], ["Bl"])

        it = 0
        GC = 2.0 * math.sqrt(2.0 / math.pi)
        for db_i in range(32):
            cch, d, b4 = db_i // 8, (db_i // 4) % 2, db_i % 4
            blk = 4 * cch + b4
            db = d * 16 + blk
            rev = (d == 1)
            if db_i % 8 == 0:
                k.dma("sp", utc[:], UT[cch * 128:(cch + 1) * 128, :], reads=[("UTd", cch)], writes=["ut"])
                k.op("pool", lambda: nc.gpsimd.memset(yTc[:], 0.0), writes=["yT"])
            c.ts("dve", wk[2][:], iot[:], sm[:, TH, db:db + 1], None, ALU.mult, None, ["iot", "sm"], ["wk2"])
            sincos(wk[2][:], "wk2", LC, sinT[:], cosT[:], "tab")
            c.ts("dve", magT[:], onesr[:], sm[:, MAG, db:db + 1], None, ALU.mult, None, ["onesr", "sm"], ["magT"])
            chunks = [(s * 256, 256, s, True) for s in range(4)]
            sc = [(TP + cix * LC, LC, None, cix == 0) for cix in range(TS // LC)]
            if rev:
                sc = [(TP + cix * LC, LC, None, cix == TS // LC - 1) for cix in reversed(range(TS // LC))]
            chunks += sc
            state = {"prev": None, "prev_k": None}
            base = it
            it += len(chunks)

            def common(ci):
                off, n, pidx, first = chunks[ci]
                i2 = (base + ci) % 2
                T_ = {nm: tmp[nm][i2] for nm in tmp}
                K_ = {nm: ("ss_" + nm, i2) for nm in tmp}
                rv = (lambda ap: ap[:, ::-1]) if rev else (lambda ap: ap)
                return off, n, pidx, first, i2, T_, K_, rv, 2 * i2, 2 * i2 + 1, 4 + i2

            def s1(ci):
                off, n, pidx, first, i2, T_, K_, rv, bR, bI, bY = common(ci)
                ucols = utc[:, off:off + n]
                c.mm(PS[bR][:, 0:n], [(Bl[:, db, 0, :], ucols)], ["Bl", "ut"], [psk(bR)])
                c.mm(PS[bI][:, 0:n], [(Bl[:, db, 1, :], ucols)], ["Bl", "ut"], [psk(bI)])
                bre, bim = rv(PS[bR][:, 0:n]), rv(PS[bI][:, 0:n])
                cs_, sn_ = cosT[:, 0:n], sinT[:, 0:n]
                c.tt("dve", T_["t1"][:, 0:n], bre, cs_, ALU.mult, [psk(bR), "tab"], [K_["t1"]])
                c.tt("dve", T_["t2"][:, 0:n], bim, sn_, ALU.mult, [psk(bI), "tab"], [K_["t2"]])
                c.tt("dve", T_["t3"][:, 0:n], bim, cs_, ALU.mult, [psk(bI), "tab"], [K_["t3"]])
                c.tt("dve", T_["t4"][:, 0:n], bre, sn_, ALU.mult, [psk(bR), "tab"], [K_["t4"]])
                c.tt("dve", T_["gr"][:, 0:n], T_["t1"][:, 0:n], T_["t2"][:, 0:n], ALU.add,
                     [K_["t1"], K_["t2"]], [K_["gr"]])
                c.tt("dve", T_["gi"][:, 0:n], T_["t3"][:, 0:n], T_["t4"][:, 0:n], ALU.subtract,
                     [K_["t3"], K_["t4"]], [K_["gi"]])

            def s2(ci):
                off, n, pidx, first, i2, T_, K_, rv, bR, bI, bY = common(ci)
                cs_, sn_ = cosT[:, 0:n], sinT[:, 0:n]
                if pidx is not None:
                    ir, ii_ = 0.0, 0.0
                    kin = []
                elif first:
                    ir, ii_ = sm[:, I0R, db:db + 1], sm[:, I0I, db:db + 1]
                    kin = ["sm"]
                else:
                    pr, pi_, pn = state["prev"]
                    prev_k = state["prev_k"]
                    iv = ini[i2]
                    kin = [("ss_ini", i2)]
                    rr, ri_ = sm[:, RR, db:db + 1], sm[:, RI, db:db + 1]
                    c.ts("dve", iv[:, 0:1], pi_[:, pn - 1:pn], ri_, None, ALU.mult, None, [prev_k[1], "sm"], kin)
                    c.stt(iv[:, 1:2], pr[:, pn - 1:pn], rr, iv[:, 0:1], ALU.mult, ALU.subtract,
                          [prev_k[0], "sm"] + kin, kin)
                    c.ts("dve", iv[:, 2:3], pi_[:, pn - 1:pn], rr, None, ALU.mult, None, [prev_k[1], "sm"], kin)
                    c.stt(iv[:, 3:4], pr[:, pn - 1:pn], ri_, iv[:, 2:3], ALU.mult, ALU.add,
                          [prev_k[0], "sm"] + kin, kin)
                    ir, ii_ = iv[:, 1:2], iv[:, 3:4]
                sr, si = T_["sr"], T_["si"]
                k.op("dve", lambda: nc.vector.tensor_tensor_scan(sr[:, 0:n], magT[:, 0:n], T_["gr"][:, 0:n], ir,
                                                                 ALU.mult, ALU.add),
                     ["magT", K_["gr"]] + kin, [K_["sr"]])
                k.op("dve", lambda: nc.vector.tensor_tensor_scan(si[:, 0:n], magT[:, 0:n], T_["gi"][:, 0:n], ii_,
                                                                 ALU.mult, ALU.add),
                     ["magT", K_["gi"]] + kin, [K_["si"]])
                state["prev"] = (sr, si, n)
                state["prev_k"] = (K_["sr"], K_["si"])
                c.tt("dve", T_["t1"][:, 0:n], sr[:, 0:n], cs_, ALU.mult, [K_["sr"], "tab"], [K_["t1"]])
                c.tt("dve", T_["t2"][:, 0:n], si[:, 0:n], sn_, ALU.mult, [K_["si"], "tab"], [K_["t2"]])
                c.tt("dve", T_["t3"][:, 0:n], sr[:, 0:n], sn_, ALU.mult, [K_["sr"], "tab"], [K_["t3"]])
                c.tt("dve", T_["t4"][:, 0:n], si[:, 0:n], cs_, ALU.mult, [K_["si"], "tab"], [K_["t4"]])
                c.tt("dve", T_["hr"][:, 0:n], T_["t1"][:, 0:n], T_["t2"][:, 0:n], ALU.subtract,
                     [K_["t1"], K_["t2"]], [K_["hr"]])
                c.tt("dve", T_["hi"][:, 0:n], T_["t3"][:, 0:n], T_["t4"][:, 0:n], ALU.add,
                     [K_["t3"], K_["t4"]], [K_["hi"]])
                c.mm(PS[bY][:, 0:n], [(Cl[:, 0, db, :], T_["hr"][:, 0:n]), (Cl[:, 1, db, :], T_["hi"][:, 0:n])],
                     ["Cl", K_["hr"], K_["hi"]], [psk(bY)])
                c.tt("dve", yTc[:, off:off + n], yTc[:, off:off + n], rv(PS[bY][:, 0:n]), ALU.add,
                     [psk(bY), "yT"], ["yT"])
                if pidx is not None:
                    c.cp("act", fin[:, pidx, d, blk, 0:1], T_["hr"][:, n - 1:n], [K_["hr"]], ["fin"])
                    c.cp("act", fin[:, pidx, d, blk, 1:2], T_["hi"][:, n - 1:n], [K_["hi"]], ["fin"])

            s1(0)
            for ci in range(1, len(chunks)):
                s1(ci)
                s2(ci - 1)
            s2(len(chunks) - 1)
            if db_i % 8 == 7:
                for t0 in range(0, T, 512):
                    cols = slice(t0, t0 + 512)
                    yv = yTc[:, cols]
                    c.stt(yv, utc[:, cols], vec[:, cch:cch + 1], yv, ALU.mult, ALU.add, ["ut", "vec", "yT"], ["yT"])
                    a, b = wk[0], wk[1]
                    c.tt("dve", a[:, 0:512], yv, yv, ALU.mult, ["yT"], ["wk0"])
                    c.ts("dve", a[:, 0:512], a[:, 0:512], 0.044715, 1.0, ALU.mult, ALU.add, ["wk0"], ["wk0"])
                    c.tt("dve", a[:, 0:512], a[:, 0:512], yv, ALU.mult, ["wk0", "yT"], ["wk0"])
                    c.act(b[:, 0:512], a[:, 0:512], AF.Sigmoid, ["wk0"], ["wk1"], scale=GC)
                    c.tt("dve", yv, yv, b[:, 0:512], ALU.mult, ["yT", "wk1"], ["yT"])
                k.dma("sp", UT[cch * 128:(cch + 1) * 128, :], yTc[:], reads=["yT", "ut"], writes=[("UTd", cch)])
        for s in range(4):
            k.dma("sp", nst_out[s, l].rearrange("d (b g) p r -> (g p) d b r", g=2), fin[:, s],
                  reads=["fin"], writes=["nst"])
    k.barrier()
    with ExitStack() as ph:
        sb = lambda n, s, d=F32: c.sb(n, s, d, ph)
        vec = sb("sg_vec", [128, 16])
        wg = sb("sg_wg", [128, 4, 1024])
        gy = [sb("sg_gy%d" % i, [128, 4, 512]) for i in range(2)]
        wk = [sb("sg_wk%d" % i, [128, 512]) for i in range(2)]
        stg = [sb("sg_st%d" % i, [128, 512], MDT) for i in range(2)]
        k.dma("sp", vec[:], vecs[l], writes=["vec"])
        k.dma("sp", wg[:], wglu_t[l].rearrange("p (k c) -> p k c", k=4), writes=["wg"])
        n_s = 0
        for ti, t0 in enumerate(range(0, T, 512)):
            cols = slice(t0, t0 + 512)
            g = gy[ti % 2]
            kg = ("sg_gy", ti % 2)
            k.dma("sp", g[:], UT[:, cols].rearrange("(k p) t -> p k t", p=128), writes=[kg])
            for cc in range(4):
                bv, bg = cc % 2, 2 + cc % 2
                c.mm(PS[bv][:], [(wg[:, kc, cc * 128:(cc + 1) * 128], g[:, kc, :]) for kc in range(4)],
                     ["wg", kg], [psk(bv)])
                c.mm(PS[bg][:], [(wg[:, kc, 512 + cc * 128:512 + (cc + 1) * 128], g[:, kc, :]) for kc in range(4)],
                     ["wg", kg], [psk(bg)])
                a, b = wk[0], wk[1]
                c.act(a[:], PS[bv][:], AF.Identity, [psk(bv), "vec"], ["wk0"], bias=vec[:, 4 + cc:5 + cc])
                c.act(b[:], PS[bg][:], AF.Sigmoid, [psk(bg), "vec"], ["wk1"], bias=vec[:, 8 + cc:9 + cc])
                s = stg[n_s % 2]
                ks = ("sg_st", n_s % 2)
                n_s += 1
                c.tt("dve", s[:], a[:], b[:], ALU.mult, ["wk0", "wk1"], [ks])
                k.dma("sp", MIX[1024 + cc * 128:1024 + (cc + 1) * 128, cols], s[:], reads=[ks], writes=[("MIX", "s")])


def _prep_shared(inp):
    f = lambda a: np.ascontiguousarray(np.asarray(a), dtype=np.float32)
    L = DEPTH
    sh = {}
    w_mod = f(inp["w_mod"])
    sh["wmod_t"] = np.ascontiguousarray(
        w_mod.reshape(L, 16, 128, 24, 512).transpose(0, 3, 2, 1, 4)).reshape(L, 24, 128, 8192)
    sh["bmodT"] = np.ascontiguousarray(f(inp["b_mod"]).reshape(L, 96, 128).transpose(0, 2, 1))
    gs = [inp["norm1_g"][0], inp["norm1_g"][1], inp["norm2_g"][0], inp["norm2_g"][1], inp["final_norm_g"]]
    sh["gT"] = np.ascontiguousarray(np.stack([f(g).reshape(16, 128).T for g in gs], axis=1))
    w_in = f(inp["w_in"])
    sh["win_t"] = np.ascontiguousarray(
        w_in.reshape(L, 16, 128, 32, 128).transpose(0, 3, 2, 1, 4)).reshape(L, 32, 128, 2048)
    w_out = f(inp["w_out"])
    sh["wout_t"] = np.ascontiguousarray(
        w_out.reshape(L, 16, 128, 16, 128).transpose(0, 3, 2, 1, 4)).reshape(L, 16, 128, 2048)
    fi = f(inp["ffn_w_in"])
    sh["ffin_t"] = np.ascontiguousarray(
        fi.reshape(L, 16, 128, 88, 128).transpose(0, 3, 2, 1, 4)).reshape(L, 88, 128, 2048)
    fo = f(inp["ffn_w_out"])
    sh["ffout_t"] = np.ascontiguousarray(
        fo.reshape(L, NSPLIT, SCH, 128, 16, 128).transpose(0, 1, 4, 3, 2, 5)).reshape(L, NSPLIT, 16, 128, SCH * 128)
    cq = np.arange(64)
    idx = np.clip(cq[None, :] - cq[:, None], -15, 15) + 15
    rpb = f(inp["attn_rpb"])
    tab = rpb[:, :, :, idx]
    sh["rpbtab"] = np.ascontiguousarray(tab.transpose(0, 1, 3, 2, 4)).reshape(L, NH, 64, 15 * 64)
    cs = np.clip(cq - 8, 0, 48)
    valid = (cq[None, :] >= cs[:, None]) & (cq[None, :] < cs[:, None] + 16)
    sh["colmask"] = np.where(valid, 0.0, NEG).astype(np.float32)
    def st_lay(x):
        tail = x.shape[4:]
        y = x.reshape((L, 2, 16, 2, 64) + tail)
        perm = (0, 3, 4, 1, 2) + tuple(range(5, 5 + len(tail)))
        return np.ascontiguousarray(y.transpose(perm)).reshape((L, 128, 32) + tail)
    a_re, a_im = f(inp["ssm_a_re"]), f(inp["ssm_a_im"])
    ldt = np.broadcast_to(f(inp["ssm_log_dt"])[:, :, :, None], (L, 2, 32, 64))
    sh["ssm_a"] = np.ascontiguousarray(np.stack([st_lay(a_re), st_lay(a_im), st_lay(np.ascontiguousarray(ldt))], axis=2))
    sh["ssm_B"] = np.ascontiguousarray(np.stack([st_lay(f(inp["ssm_b_re"])), st_lay(f(inp["ssm_b_im"]))], axis=2))
    C = np.zeros((L, 2, 128, 32, 128), np.float32)
    for ri, key in enumerate(("ssm_c_re", "ssm_c_im")):
        cc = f(inp[key])
        for d in range(2):
            for bk in range(16):
                for gl in range(2):
                    c0 = (bk % 4) * 32 + gl * 16
                    C[:, ri, gl * 64:(gl + 1) * 64, d * 16 + bk, c0:c0 + 16] = \
                        cc[:, d, 2 * bk + gl].transpose(0, 2, 1)
    sh["ssm_C"] = C.reshape(L, 2, 128, 32 * 128)
    vec = np.zeros((L, 128, 16), np.float32)
    vec[:, :, 0:4] = f(inp["ssm_d"]).reshape(L, 4, 128).transpose(0, 2, 1)
    vec[:, :, 4:12] = f(inp["ssm_b_glu"]).reshape(L, 8, 128).transpose(0, 2, 1)
    vec[:, :, 12:16] = f(inp["pool_scale"]).reshape(L, 4, 128).transpose(0, 2, 1)
    sh["vecs"] = vec
    sh["wglu_t"] = np.ascontiguousarray(f(inp["ssm_w_glu"]).reshape(L, 4, 128, 1024).transpose(0, 2, 1, 3)).reshape(L, 128, 4096)
    sh["poolw_t"] = np.ascontiguousarray(f(inp["pool_w"]).transpose(0, 2, 1, 3)).reshape(L, 128, 512)
    ic = np.zeros((4, 256 + TS), np.float32)
    for n, o in ((256, 0), (TS, 256)):
        t = np.arange(n)
        for g in range(4):
            w = 2 << g
            lo = np.clip(t - w // 2, 0, n)
            hi = np.clip(t - w // 2 + w, 0, n)
            ic[g, o:o + n] = 1.0 / (hi - lo)
    sh["invcnt"] = np.ascontiguousarray(np.broadcast_to(ic[None], (128, 4, 256 + TS)))
    sh["ident"] = np.eye(128, dtype=np.float32)
    return sh


def _prep_core(inp, c, sh):
    f = lambda a: np.ascontiguousarray(np.asarray(a), dtype=np.float32)
    b = c // 4
    m = dict(sh)
    m["x_in"] = np.concatenate([f(inp["x_prompt"][4 * c:4 * c + 4]).reshape(TP, D), f(inp["x_sample"][b])], axis=0)
    cond = np.stack([f(inp["c_ctx"]), f(inp["c"][b])], axis=0)
    m["condT"] = np.ascontiguousarray(cond.reshape(2, 16, 128).transpose(2, 1, 0))
    m["ck"] = f(inp["cache_k"][b])
    m["cv"] = f(inp["cache_v"][b])
    st = f(inp["state_ssm"][b])
    m["st0"] = np.ascontiguousarray(
        st.reshape(DEPTH, 2, 16, 2, 64, 2).transpose(0, 3, 4, 1, 2, 5)).reshape(DEPTH, 128, 2, 16, 2)
    return m


_NC_CACHE = {}


def kernel(**inp):
    n = 8
    key = (str(MDT), DEBUG, STOP_AFTER)
    if key not in _NC_CACHE:
        _NC_CACHE[key] = build_program()
    nc = _NC_CACHE[key]
    sh = _prep_shared(inp)
    in_maps = [_prep_core(inp, c, sh) for c in range(n)]
    res = run_bass_kernel_spmd(nc, in_maps, core_ids=list(range(n)))
    R = res.results
    y_prompt = np.concatenate([R[c]["y"][:TP].reshape(4, 256, D) for c in range(n)], axis=0)
    y_sample = np.stack([R[0]["y"][TP:], R[4]["y"][TP:]], axis=0)
    nk = np.concatenate([R[c]["nk"] for c in range(n)], axis=0)
    nv = np.concatenate([R[c]["nv"] for c in range(n)], axis=0)
    nst = np.concatenate([R[c]["nst"] for c in range(n)], axis=0)
    return (np.ascontiguousarray(y_prompt, dtype=np.float32), np.ascontiguousarray(y_sample, dtype=np.float32),
            np.ascontiguousarray(nk, dtype=np.float32), np.ascontiguousarray(nv, dtype=np.float32),
            np.ascontiguousarray(nst, dtype=np.float32))
```
